# Optimizing a Trainium2 kernel written in Bass

```python
import math
import jax, jax.numpy as jnp
from jax import lax
import numpy as np

D_MODEL = 2048
BATCH = 2
SEQ = 16384
DEPTH = 1

CHUNK = 64
D_MIX = D_MODEL
D_SGU = D_MIX // 2
D_ATT = D_MIX - D_SGU
SGU_GROUPS = 8
SGU_GROUP_DIM = D_SGU // SGU_GROUPS
SGU_BLOCK = 128
ATT_HEADS = 8
ATT_VDIM = D_ATT // ATT_HEADS
ATT_QKDIM = ATT_VDIM // 2
N_BUCKETS = 32
MAX_DISTANCE = 128
Q_BLOCK = 128
EPS = 1e-6
SPLITS = (D_SGU, 2 * D_SGU, 3 * D_SGU, 3 * D_SGU + D_ATT, 3 * D_SGU + 2 * D_ATT, 3 * D_SGU + 3 * D_ATT)
D_IN = 3 * D_SGU + 4 * D_ATT

kernel_name = "hymba_sgu_diffattn_block"


def rms_norm(x, g):
    xf = x.astype(jnp.float32)
    y = xf * lax.rsqrt(jnp.mean(xf * xf, axis=-1, keepdims=True) + EPS)
    return (y * g.astype(jnp.float32)).astype(x.dtype)


def layer_norm(x, g, b):
    xf = x.astype(jnp.float32)
    mu = jnp.mean(xf, axis=-1, keepdims=True)
    xc = xf - mu
    y = xc * lax.rsqrt(jnp.mean(xc * xc, axis=-1, keepdims=True) + EPS)
    return (y * g.astype(jnp.float32) + b.astype(jnp.float32)).astype(x.dtype)


def t5_bucket(rel):
    nb = N_BUCKETS // 2
    max_exact = nb // 2
    side = jnp.where(rel > 0, nb, 0)
    n = jnp.abs(rel)
    nf = jnp.maximum(n, 1).astype(jnp.float32)
    large = max_exact + (jnp.log(nf / max_exact) / math.log(MAX_DISTANCE / max_exact)
                         * (nb - max_exact)).astype(jnp.int32)
    large = jnp.minimum(large, nb - 1)
    return side + jnp.where(n < max_exact, n, large)


def spatial_gating(u, v, ln_g, ln_b, w_s, b_s):
    B, S, _ = v.shape
    v = layer_norm(v, ln_g, ln_b)
    nblk = S // SGU_BLOCK
    vb = v.reshape(B, nblk, SGU_BLOCK, SGU_GROUPS, SGU_GROUP_DIM)
    t = jnp.arange(SGU_BLOCK)
    mask = (t[None, :] // CHUNK) <= (t[:, None] // CHUNK)
    w = jnp.where(mask[None], w_s, jnp.zeros((), w_s.dtype))
    mixed = jnp.einsum('gts,bnsgc->bntgc', w, vb) + b_s.T[None, None, :, :, None]
    return u * mixed.reshape(B, S, D_SGU)


def diff_attention(q, k, v, lam, rel_bias):
    B, S = q.shape[0], q.shape[1]
    nblk = S // Q_BLOCK
    scale = ATT_QKDIM ** -0.5
    kpos = jnp.arange(S, dtype=jnp.int32)
    kchunk = kpos // CHUNK
    qs = jnp.moveaxis(q.reshape(B, nblk, Q_BLOCK, ATT_HEADS, 2, ATT_QKDIM), 1, 0)
    neg = jnp.finfo(jnp.float32).min

    def one_block(args):
        i, qb = args
        qpos = i * Q_BLOCK + jnp.arange(Q_BLOCK, dtype=jnp.int32)
        rel = kpos[None, :] - qpos[:, None]
        bias = jnp.moveaxis(rel_bias[t5_bucket(rel)], -1, 0).astype(jnp.float32)
        allowed = kchunk[None, :] <= (qpos // CHUNK)[:, None]
        logits = jnp.einsum('bqhmd,bkhmd->bhmqk', qb, k).astype(jnp.float32) * scale
        logits = jnp.where(allowed, logits + bias[None, :, None], neg)
        p = jax.nn.softmax(logits, axis=-1)
        w = p[:, :, 0] - lam * p[:, :, 1]
        return jnp.einsum('bhqk,bkhd->bqhd', w.astype(v.dtype), v)

    out = lax.map(one_block, (jnp.arange(nblk, dtype=jnp.int32), qs))
    return jnp.moveaxis(out, 0, 1).reshape(B, S, ATT_HEADS, ATT_VDIM)


def setup_inputs(seed: int = 0) -> dict:
    key = jax.random.key(seed)
    ks = jax.random.split(key, 16)
    f = jnp.float32
    x = jax.random.normal(ks[0], (BATCH, SEQ, D_MODEL), f)
    norm_g = 1.0 + 0.02 * jax.random.normal(ks[1], (DEPTH, D_MODEL), f)
    w_in = jax.random.normal(ks[2], (DEPTH, D_MODEL, D_IN), f) * D_MODEL ** -0.5
    sgu_ln_g = 1.0 + 0.02 * jax.random.normal(ks[3], (DEPTH, D_SGU), f)
    sgu_ln_b = 0.02 * jax.random.normal(ks[4], (DEPTH, D_SGU), f)
    sgu_w = jax.random.normal(ks[5], (DEPTH, SGU_GROUPS, SGU_BLOCK, SGU_BLOCK), f) * SGU_BLOCK ** -0.5
    sgu_b = 1.0 + 0.1 * jax.random.normal(ks[6], (DEPTH, SGU_GROUPS, SGU_BLOCK), f)
    lambda_q1 = 0.1 * jax.random.normal(ks[7], (DEPTH, ATT_QKDIM), f)
    lambda_k1 = 0.1 * jax.random.normal(ks[8], (DEPTH, ATT_QKDIM), f)
    lambda_q2 = 0.1 * jax.random.normal(ks[9], (DEPTH, ATT_QKDIM), f)
    lambda_k2 = 0.1 * jax.random.normal(ks[10], (DEPTH, ATT_QKDIM), f)
    subln_g = 1.0 + 0.02 * jax.random.normal(ks[11], (DEPTH, ATT_VDIM), f)
    rel_bias = 0.5 * jax.random.normal(ks[12], (N_BUCKETS, ATT_HEADS), f)
    w_out = jax.random.normal(ks[13], (DEPTH, D_MIX, D_MODEL), f) * D_MIX ** -0.5
    final_g = 1.0 + 0.02 * jax.random.normal(ks[14], (D_MODEL,), f)
    return {"x": x, "norm_g": norm_g, "w_in": w_in, "sgu_ln_g": sgu_ln_g, "sgu_ln_b": sgu_ln_b,
            "sgu_w": sgu_w, "sgu_b": sgu_b, "lambda_q1": lambda_q1, "lambda_k1": lambda_k1,
            "lambda_q2": lambda_q2, "lambda_k2": lambda_k2, "subln_g": subln_g,
            "rel_bias": rel_bias, "w_out": w_out, "final_g": final_g}


def reference(x, norm_g, w_in, sgu_ln_g, sgu_ln_b, sgu_w, sgu_b, lambda_q1, lambda_k1,
              lambda_q2, lambda_k2, subln_g, rel_bias, w_out, final_g):
    B, S, _ = x.shape
    for l in range(DEPTH):
        lambda_init = 0.8 - 0.6 * math.exp(-0.3 * l)
        h = rms_norm(x, norm_g[l])
        z = jnp.einsum('bsd,de->bse', h, w_in[l])
        u, v, g_a, q, k, v_att, g_b = jnp.split(z, SPLITS, axis=-1)
        a_out = spatial_gating(jax.nn.gelu(u), jax.nn.gelu(v), sgu_ln_g[l], sgu_ln_b[l],
                               sgu_w[l], sgu_b[l])
        a_out = a_out * jax.nn.silu(g_a)
        lam = (jnp.exp(jnp.sum(lambda_q1[l].astype(jnp.float32) * lambda_k1[l].astype(jnp.float32)))
               - jnp.exp(jnp.sum(lambda_q2[l].astype(jnp.float32) * lambda_k2[l].astype(jnp.float32)))
               + lambda_init)
        q = q.reshape(B, S, ATT_HEADS, 2, ATT_QKDIM)
        k = k.reshape(B, S, ATT_HEADS, 2, ATT_QKDIM)
        v_att = v_att.reshape(B, S, ATT_HEADS, ATT_VDIM)
        o = diff_attention(q, k, v_att, lam, rel_bias)
        o = rms_norm(o, subln_g[l]) * (1.0 - lambda_init)
        b_out = o.reshape(B, S, D_ATT) * jax.nn.silu(g_b)
        y = jnp.concatenate([a_out, b_out], axis=-1)
        x = x + jnp.einsum('bse,ed->bsd', y, w_out[l])
    return rms_norm(x, final_g)
```

```python
import math
from contextlib import ExitStack

import numpy as np
import ml_dtypes

import concourse.bass as bass
import concourse.mybir as mybir
from concourse.bass_utils import run_bass_kernel_spmd

F32 = mybir.dt.float32
BF16 = mybir.dt.bfloat16
AF = mybir.ActivationFunctionType
ALU = mybir.AluOpType

D = 2048
DIN = 7168
NH = 8
NR = 1151
EPS = 1e-6
NEG = -30000.0
ENG = ("pe", "act", "dve", "pool", "sp")
NDS = 90


class _Rec:
    def __getattr__(self, name):
        return lambda *a, **k: (name, a, k)


R = _Rec()


class Builder:
    def __init__(self, nc, stack):
        self.nc = nc
        self.streams = {e: [] for e in ENG}
        self.esem = {e: stack.enter_context(nc.semaphore("s_" + e)) for e in ENG}
        self.cnt = {e: 0 for e in ENG}
        self.pool = [stack.enter_context(nc.semaphore("d%d" % i)) for i in range(NDS)]
        self.dsem = {}
        self.waited = {e: {} for e in ENG}
        self.res = {}

    def _semof(self, key):
        if isinstance(key, str) and key in self.esem:
            return self.esem[key]
        return self.dsem[key][0]

    def _wait(self, eng, ev):
        key, val = ev
        if key == "pe" and eng == "pe":
            return
        w = self.waited[eng]
        if w.get(key, 0) >= val:
            return
        w[key] = val
        sem = self._semof(key)
        self.streams[eng].append(("wait", sem, val))

    def _deps(self, eng, reads, writes):
        for r in reads:
            st = self.res.get(r)
            if st and st[0] is not None:
                self._wait(eng, st[0])
        for w in writes:
            st = self.res.get(w)
            if st:
                if st[0] is not None:
                    self._wait(eng, st[0])
                for k, v in st[1].items():
                    self._wait(eng, (k, v))

    def _update(self, ev, reads, writes):
        for r in reads:
            st = self.res.setdefault(r, [None, {}])
            if st[1].get(ev[0], 0) < ev[1]:
                st[1][ev[0]] = ev[1]
        for w in writes:
            self.res[w] = [ev, {}]

    def op(self, eng, fn, reads=(), writes=()):
        self._deps(eng, reads, writes)
        self.cnt[eng] += 1
        ev = (eng, self.cnt[eng])
        sem = self.esem[eng]
        name, args, kwargs = fn
        self.streams[eng].append(("op", name, args, kwargs, sem, 1))
        self._update(ev, reads, writes)
        return ev

    def dma(self, q, key, out, in_, reads=(), writes=()):
        self._deps(q, reads, writes)
        if key not in self.dsem:
            self.dsem[key] = [self.pool.pop(), 0]
        d = self.dsem[key]
        d[1] += 16
        ev = (key, d[1])
        sem = d[0]
        self.streams[q].append(("op", "dma_start", (), dict(out=out, in_=in_), sem, 16))
        self._update(ev, reads, writes)
        return ev

    def barrier(self):
        evs = [(e, self.cnt[e]) for e in ENG if self.cnt[e] > 0]
        evs += [(k, d[1]) for k, d in self.dsem.items() if d[1] > 0]
        for eng in ENG:
            for ev in evs:
                self._wait(eng, ev)
        self.res = {}


class Mem:
    def __init__(self, big, cap):
        self.big = big
        self.cap = cap
        self.ptr = 0

    def alloc(self, free_shape, dtype):
        esz = 4 if dtype == F32 else 2
        n = 1
        for s in free_shape:
            n *= s
        nb = n * esz
        off = (self.ptr + 63) // 64 * 64
        assert off + nb <= self.cap, ("SBUF overflow", off, nb, self.cap)
        self.ptr = off + nb
        v = self.big[:, off // 2:(off + nb) // 2]
        if dtype == F32:
            v = v.bitcast(F32)
        if len(free_shape) == 2:
            v = v.rearrange("p (a b) -> p a b", a=free_shape[0])
        elif len(free_shape) == 3:
            v = v.rearrange("p (a b c) -> p a b c", a=free_shape[0], b=free_shape[1])
        return v


def build_program(NSLOT):
    NT = 16 * NSLOT
    NG = 4 * NSLOT
    NOWN = NSLOT * 512

    nc = bass.Bass("TRN2", target_bir_lowering=False)

    def din(name, shape, dt=F32):
        return nc.dram_tensor(name, list(shape), dt, kind="ExternalInput").ap()

    def dscr(name, shape, dt):
        return nc.dram_tensor(name, list(shape), dt, kind="Internal").ap()

    xv = din("xv", [NT * 128, D])
    w_in = din("w_in", [D, DIN])
    w_out = din("w_out", [D, D])
    norm_g = din("norm_g", [D])
    final_g = din("final_g", [D])
    ln_g = din("sgu_ln_g", [1024])
    ln_b = din("sgu_ln_b", [1024])
    sgu_w = din("sgu_w", [8, 128, 128])
    sgu_b = din("sgu_b", [8, 128])
    lam_in = din("lam_in", [4, 64])
    subln_g = din("subln_g", [128])
    rel_bias = din("rel_bias", [32, 8])
    c_ident = din("c_ident", [128, 128], BF16)
    c_identf = din("c_identf", [128, 128])
    c_rev = din("c_rev", [128, 128])
    c_onehot = din("c_onehot", [32, NR])
    c_masks = din("c_masks", [128, 4 * 512])
    c_sgumask = din("c_sgumask", [128, 128])
    c_kvalid = din("c_kvalid", [128, NT], BF16)
    out_own = nc.dram_tensor("out_own", [NOWN, D], F32, kind="ExternalOutput").ap()

    KTd = dscr("KTd", [NH, 128, NT * 128], BF16)
    Vd = dscr("Vd", [NH, 128, NT, 128], BF16)
    hTd = dscr("hTd", [NSLOT, 128, 16, 512], BF16)
    QTd = dscr("QTd", [NH, 2, 128, NOWN], BF16)
    GBd = dscr("GBd", [NSLOT * 4, 128, 1024], F32)
    YTd = dscr("YTd", [16, 128, NOWN], BF16)
    Gd = dscr("Gd", [NH, NR], F32)
    BMd = dscr("BMd", [NH, 128, 5 * 512], F32)

    def bcast(ap1d, n, offset=0):
        return bass.AP(ap1d.tensor, offset, [[0, 128], [1, n]])

    CAP = 204800
    with ExitStack() as stack:
        big = stack.enter_context(nc.sbuf_tensor("big", [128, CAP // 2], BF16))
        ps = stack.enter_context(nc.psum_tensor("ps", [128, 8, 512], F32))
        ps2 = ps.rearrange("p b n -> p (b n)")
        B = Builder(nc, stack)
        mem = Mem(big, CAP)

        def psb16(bank, nbanks=1):
            return ps2[:, bank * 512:(bank + nbanks) * 512].bitcast(BF16)

        ident = mem.alloc([128], BF16)
        gbc = mem.alloc([D], F32)
        epsb = mem.alloc([1], F32)
        chb = mem.alloc([8], F32)
        lamb = mem.alloc([1], F32)
        sgl = mem.alloc([128], F32)
        kval = mem.alloc([NT], BF16)
        small = mem.alloc([64], F32)
        pmark = mem.ptr

        rb = mem.alloc([8], F32)
        oh = mem.alloc([NR], F32)
        rev = mem.alloc([128], F32)
        msk = mem.alloc([4, 512], F32)
        lv = mem.alloc([4, 64], F32)
        lj = mem.alloc([64], F32)
        gsb = mem.alloc([NR], F32)
        Xh = [mem.alloc([512], F32) for _ in range(2)]
        bmst = [mem.alloc([5, 512], F32) for _ in range(2)]

        B.dma("sp", "pl", ident, c_ident, writes=["ident"])
        B.dma("sp", "pl", gbc, bcast(norm_g, D), writes=["gbc"])
        B.dma("sp", "pl", chb, bcast(rel_bias, 8, 15 * 8), writes=["chb"])
        B.dma("sp", "pl", sgl, bcast(subln_g, 128), writes=["sgl"])
        B.dma("sp", "pl", kval, c_kvalid, writes=["kval"])
        B.dma("sp", "pl", rb[0:32, :], rel_bias, writes=["rb"])
        B.dma("sp", "pl", oh[0:32, :], c_onehot, writes=["oh"])
        B.dma("sp", "pl", rev, c_rev, writes=["rev"])
        B.dma("sp", "pl", msk.rearrange("p a b -> p (a b)"), c_masks, writes=["msk"])
        for k in range(4):
            B.dma("sp", "pl", lv[:, k, :], bcast(lam_in, 64, k * 64), writes=[("lv", k)])
        B.barrier()

        B.op("dve", R.memset(epsb, EPS), writes=["epsb"])
        B.op("dve", R.tensor_scalar(out=sgl, in0=sgl, scalar1=0.4, scalar2=None, op0=ALU.mult),
             reads=["sgl"], writes=["sgl"])
        B.op("dve", R.scalar_tensor_tensor(out=lj, in0=lv[:, 0, :], scalar=1.0, in1=lv[:, 1, :],
                                                     op0=ALU.mult, op1=ALU.mult, accum_out=small[:, 0:1]),
             writes=["lj", "s0"])
        B.op("dve", R.scalar_tensor_tensor(out=lj, in0=lv[:, 2, :], scalar=1.0, in1=lv[:, 3, :],
                                                     op0=ALU.mult, op1=ALU.mult, accum_out=small[:, 1:2]),
             reads=[], writes=["lj", "s1"])
        B.op("act", R.activation(out=small[:, 2:4], in_=small[:, 0:2], func=AF.Exp),
             reads=["s0", "s1"], writes=["s23"])
        B.op("dve", R.tensor_tensor(out=small[:, 4:5], in0=small[:, 2:3], in1=small[:, 3:4],
                                              op=ALU.subtract), reads=["s23"], writes=["s4"])
        B.op("dve", R.tensor_scalar(out=lamb, in0=small[:, 4:5], scalar1=0.2, scalar2=None,
                                              op0=ALU.add), reads=["s4"], writes=["lamb"])
        for ci, (c0, n) in enumerate([(0, 512), (512, 512), (1024, NR - 1024)]):
            B.op("pe", R.matmul(ps[0:8, ci, 0:n], lhsT=rb[0:32, 0:8],
                                                              rhs=oh[0:32, c0:c0 + n], start=True, stop=True),
                 writes=[("ps", ci)])
            B.op("act", R.activation(out=gsb[0:8, c0:c0 + n], in_=ps[0:8, ci, 0:n],
                                                                  func=AF.Copy),
                 reads=[("ps", ci)], writes=[("gsb", ci)])
        B.dma("sp", "gd", Gd, gsb[0:8, :], reads=[("gsb", 0), ("gsb", 1), ("gsb", 2)], writes=["Gd"])
        cnt = 0
        for h in range(NH):
            bs = h % 2
            for r in range(-1, 4):
                xs = cnt % 2
                bk = 4 + cnt % 4
                cnt += 1
                src = bass.AP(Gd.tensor, h * NR + 128 * (3 - r), [[1, 128], [1, 512]])
                B.dma("sp", ("xh", xs), Xh[xs], src, reads=["Gd"], writes=[("xh", xs)])
                B.op("pe", R.matmul(ps[:, bk, :], lhsT=rev, rhs=Xh[xs], start=True, stop=True),
                     reads=[("xh", xs)], writes=[("ps", bk)])
                if r >= 0:
                    B.op("dve", R.tensor_tensor(out=bmst[bs][:, r + 1, :], in0=ps[:, bk, :],
                                                                            in1=msk[:, r, :], op=ALU.add),
                         reads=[("ps", bk)], writes=[("bmst", bs, r)])
                else:
                    B.op("dve", R.tensor_copy(out=bmst[bs][:, 0, :], in_=ps[:, bk, :]),
                         reads=[("ps", bk)], writes=[("bmst", bs, r)])
            B.dma("pool", ("bmst", bs), BMd[h], bmst[bs].rearrange("p a b -> p (a b)"),
                  reads=[("bmst", bs, r) for r in range(-1, 4)])
        B.barrier()

        mem.ptr = pmark
        Wkv = mem.alloc([16, 2048], BF16)
        xt = [mem.alloc([D], F32) for _ in range(3)]
        sqj = mem.alloc([D], BF16)
        hb = [mem.alloc([D], BF16) for _ in range(2)]
        hT = [mem.alloc([16, 512], BF16) for _ in range(2)]
        kst = [mem.alloc([8, 512], BF16) for _ in range(2)]
        vst = [mem.alloc([4, 1024], BF16) for _ in range(2)]

        for dc in range(16):
            B.dma("pool", "w", Wkv[:, dc, :], w_in[dc * 128:(dc + 1) * 128, 4096:6144], writes=["W"])

        def hT_keys(gs):
            return [("hT", gs, t, hf) for t in range(4) for hf in range(2)]

        def rms_stats(src, skey, col):
            B.op("act", R.activation(out=sqj, in_=src, func=AF.Square, accum_out=small[:, col:col + 1]),
                 reads=[skey], writes=["sqj", ("st", col)])
            B.op("act", R.activation(out=small[:, col + 1:col + 2], in_=small[:, col:col + 1], func=AF.Ln,
                                               scale=1.0 / D, bias=epsb),
                 reads=[("st", col)], writes=[("st", col + 1)])
            B.op("act", R.activation(out=small[:, col + 2:col + 3], in_=small[:, col + 1:col + 2],
                                               func=AF.Exp, scale=-0.5),
                 reads=[("st", col + 1)], writes=[("st", col + 2)])
            return ("st", col + 2), small[:, col + 2:col + 3]

        def frontend(T):
            g, tt = divmod(T, 4)
            xs, hs, ts, gs = T % 3, T % 2, T % 2, g % 2
            B.dma("sp", ("xt", xs), xt[xs], xv[T * 128:(T + 1) * 128, :], writes=[("xt", xs)])
            rkey, rstd = rms_stats(xt[xs], ("xt", xs), 8 + 4 * (T % 2))
            B.op("dve", R.scalar_tensor_tensor(out=hb[hs], in0=xt[xs], scalar=rstd, in1=gbc,
                                                         op0=ALU.mult, op1=ALU.mult),
                 reads=[("xt", xs), rkey], writes=[("hb", hs)])
            tpv = psb16(2 * ts, 2)
            for dc in range(16):
                B.op("pe", R.transpose(tpv[:, dc * 128:(dc + 1) * 128],
                                                        hb[hs][:, dc * 128:(dc + 1) * 128], ident),
                     reads=[("hb", hs)], writes=[("ps", 2 * ts + dc // 8)])
            B.op("act", R.activation(out=hT[gs][:, 0:8, tt * 128:(tt + 1) * 128],
                                               in_=tpv[:, 0:1024].rearrange("p (c n) -> p c n", c=8), func=AF.Copy),
                 reads=[("ps", 2 * ts)], writes=[("hT", gs, tt, 0)])
            B.op("dve", R.tensor_copy(out=hT[gs][:, 8:16, tt * 128:(tt + 1) * 128],
                                                in_=tpv[:, 1024:2048].rearrange("p (c n) -> p c n", c=8)),
                 reads=[("ps", 2 * ts + 1)], writes=[("hT", gs, tt, 1)])

        chain_no = [0]

        def backend_chains(g):
            gs, ks, vs = g % 2, g % 2, g % 2
            chains = []

            def kchain(fc):
                bk = 4 + chain_no[0] % 4
                chain_no[0] += 1
                for dc in range(16):
                    B.op("pe", R.matmul(ps[:, bk, :], lhsT=Wkv[:, dc, fc * 128:(fc + 1) * 128],
                                                         rhs=hT[gs][:, dc, :], start=(dc == 0), stop=(dc == 15)),
                         reads=hT_keys(gs) + ["W"], writes=[("ps", bk)])
                if fc % 2 == 0:
                    B.op("act", R.activation(out=kst[ks][:, fc, :], in_=ps[:, bk, :], func=AF.Copy),
                         reads=[("ps", bk)], writes=[("kst", ks, fc)])
                else:
                    B.op("dve", R.tensor_copy(out=kst[ks][:, fc, :], in_=ps[:, bk, :]),
                         reads=[("ps", bk)], writes=[("kst", ks, fc)])
                if fc == 7:
                    B.dma("pool", ("kst", ks), KTd[:, :, g * 512:(g + 1) * 512].rearrange("h p t -> p h t"), kst[ks],
                          reads=[("kst", ks, f) for f in range(8)])

            def vchain(tt, hf):
                bk = 4 + chain_no[0] % 4
                chain_no[0] += 1
                for dc in range(16):
                    B.op("pe", R.matmul(ps[:, bk, :], lhsT=hT[gs][:, dc, tt * 128:(tt + 1) * 128],
                                                         rhs=Wkv[:, dc, 1024 + hf * 512:1024 + (hf + 1) * 512],
                                                         start=(dc == 0), stop=(dc == 15)),
                         reads=[("hT", gs, tt, 0), ("hT", gs, tt, 1), "W"], writes=[("ps", bk)])
                if hf == 0:
                    B.op("act", R.activation(out=vst[vs][:, tt, 0:512], in_=ps[:, bk, :], func=AF.Copy),
                         reads=[("ps", bk)], writes=[("vst", vs, tt, hf)])
                else:
                    B.op("dve", R.tensor_copy(out=vst[vs][:, tt, 512:1024], in_=ps[:, bk, :]),
                         reads=[("ps", bk)], writes=[("vst", vs, tt, hf)])
                if tt == 3 and hf == 1:
                    for t4 in range(4):
                        B.dma("pool", ("vst", vs),
                              Vd[:, :, 4 * g + t4, :].rearrange("h p d -> p h d"),
                              vst[vs][:, t4, :].rearrange("p (h d) -> p h d", h=8),
                              reads=[("vst", vs, t, f) for t in range(4) for f in range(2)])
                    if g % 4 == 3:
                        B.dma("pool", ("hTd", gs), hTd[g // 4], hT[gs], reads=hT_keys(gs))

            for fc in range(8):
                chains.append(lambda fc=fc: kchain(fc))
            for tt in range(4):
                for hf in range(2):
                    chains.append(lambda tt=tt, hf=hf: vchain(tt, hf))
            return chains

        for T in range(4):
            frontend(T)
        for g in range(NG):
            chains = backend_chains(g)
            for ci, ch in enumerate(chains):
                ch()
                if ci % 4 == 3 and g + 1 < NG:
                    frontend(4 * (g + 1) + ci // 4)
        B.barrier()

        mem.ptr = pmark
        Wqg = mem.alloc([16, 2048], BF16)
        hT = [mem.alloc([16, 512], BF16) for _ in range(2)]
        qst = [mem.alloc([2, 8, 512], BF16) for _ in range(2)]
        gst = [mem.alloc([1024], F32) for _ in range(2)]
        th = [mem.alloc([512], F32) for _ in range(2)]
        for dc in range(16):
            B.dma("pool", "w", Wqg[:, dc, 0:1024], w_in[dc * 128:(dc + 1) * 128, 3072:4096], writes=["W"])
            B.dma("pool", "w", Wqg[:, dc, 1024:2048], w_in[dc * 128:(dc + 1) * 128, 6144:7168], writes=["W"])
        for s in range(2):
            B.op("dve", R.memset(qst[s].rearrange("p a b c -> p (a b c)"), 0.0),
                 writes=[("qst", s, m, f) for m in range(2) for f in range(8)])
        cno = 0
        tno = 0
        for i in range(NSLOT):
            hs = i % 2
            qs = i % 2
            B.dma("sp", ("hTl", hs), hT[hs], hTd[i], writes=[("hT", hs)])
            for fc in range(8):
                bk = cno % 4
                cno += 1
                for dc in range(16):
                    B.op("pe", R.matmul(ps[:, bk, :], lhsT=Wqg[:, dc, fc * 128:(fc + 1) * 128],
                                                                      rhs=hT[hs][:, dc, :], start=(dc == 0), stop=(dc == 15)),
                         reads=[("hT", hs), "W"], writes=[("ps", bk)])
                B.op("act", R.activation(out=qst[qs][0:64, 0, fc, :], in_=ps[0:64, bk, :], func=AF.Copy),
                     reads=[("ps", bk)], writes=[("qst", qs, 0, fc)])
                B.op("dve", R.tensor_copy(out=qst[qs][64:128, 1, fc, :], in_=ps[64:128, bk, :]),
                     reads=[("ps", bk)], writes=[("qst", qs, 1, fc)])
            for m2 in range(2):
                B.dma("pool", ("qst", qs), QTd[:, m2, :, i * 512:(i + 1) * 512].rearrange("h p t -> p h t"),
                      qst[qs][:, m2, :, :], reads=[("qst", qs, m, f) for m in range(2) for f in range(8)])
            for tt in range(4):
                gsl = (4 * i + tt) % 2
                for hf in range(2):
                    bk = 4 + cno % 4
                    cno += 1
                    tsl = tno % 2
                    tno += 1
                    for dc in range(16):
                        B.op("pe", R.matmul(
                            ps[:, bk, :], lhsT=hT[hs][:, dc, tt * 128:(tt + 1) * 128],
                            rhs=Wqg[:, dc, 1024 + hf * 512:1024 + (hf + 1) * 512], start=(dc == 0), stop=(dc == 15)),
                            reads=[("hT", hs), "W"], writes=[("ps", bk)])
                    B.op("act", R.activation(out=th[tsl], in_=ps[:, bk, :], func=AF.Tanh, scale=0.5),
                         reads=[("ps", bk)], writes=[("th", tsl)])
                    B.op("dve", R.scalar_tensor_tensor(
                        out=gst[gsl][:, hf * 512:(hf + 1) * 512], in0=th[tsl], scalar=1.0, in1=ps[:, bk, :],
                        op0=ALU.add, op1=ALU.mult),
                        reads=[("th", tsl), ("ps", bk)], writes=[("gst", gsl, hf)])
                B.dma("pool", ("gst", gsl), GBd[4 * i + tt], gst[gsl], reads=[("gst", gsl, 0), ("gst", gsl, 1)])
        B.barrier()

        mem.ptr = pmark
        Wsg = mem.alloc([16, 3072], BF16)
        hT = [mem.alloc([16, 512], BF16) for _ in range(2)]
        wnat = mem.alloc([8, 128], F32)
        identf = mem.alloc([128], F32)
        smk = mem.alloc([128], F32)
        bnat = mem.alloc([128], F32)
        wT = mem.alloc([8, 128], BF16)
        bT = mem.alloc([8], F32)
        lngb = mem.alloc([1024], F32)
        lnbb = mem.alloc([1024], F32)
        gu = mem.alloc([1024], F32)
        gv = mem.alloc([1024], F32)
        t1 = mem.alloc([1024], F32)
        vn = mem.alloc([1024], BF16)
        tha = mem.alloc([1024], F32)
        sa2 = mem.alloc([1024], F32)
        aa = mem.alloc([1024], F32)
        aout = mem.alloc([1024], BF16)
        bst = mem.alloc([2, 6], F32)
        yst = [mem.alloc([8, 512], BF16) for _ in range(2)]
        for dc in range(16):
            B.dma("pool", "w", Wsg[:, dc, :], w_in[dc * 128:(dc + 1) * 128, 0:3072], writes=["W"])
        B.dma("sp", "pl", wnat, sgu_w.rearrange("g t s -> t g s"), writes=["wnat"])
        B.dma("sp", "pl", identf, c_identf, writes=["identf"])
        B.dma("sp", "pl", smk, c_sgumask, writes=["smk"])
        B.dma("sp", "pl", bnat[0:8, :], sgu_b, writes=["bnat"])
        B.dma("sp", "pl", lngb, bcast(ln_g, 1024), writes=["lngb"])
        B.dma("sp", "pl", lnbb, bcast(ln_b, 1024), writes=["lnbb"])
        B.barrier()
        for g8 in range(8):
            bk = g8 % 4
            B.op("pe", R.transpose(ps[:, bk, 0:128], wnat[:, g8, :], identf),
                 reads=["wnat", "identf"], writes=[("ps", bk)])
            B.op("dve", R.tensor_tensor(out=wT[:, g8, :], in0=ps[:, bk, 0:128], in1=smk, op=ALU.mult),
                 reads=[("ps", bk), "smk"], writes=[("wT", g8)])
        B.op("pe", R.transpose(ps[:, 4, 0:8], bnat[0:8, :], identf[0:8, 0:8]),
             reads=["bnat", "identf"], writes=[("ps", 4)])
        B.op("dve", R.tensor_copy(out=bT, in_=ps[:, 4, 0:8]), reads=[("ps", 4)], writes=["bT"])
        B.barrier()

        for i in range(NSLOT):
            hs = i % 2
            ys = i % 2
            B.dma("sp", ("hTl", hs), hT[hs], hTd[i], writes=[("hT", hs)])
            for tt in range(4):
                for cb in range(6):
                    for dc in range(16):
                        B.op("pe", R.matmul(
                            ps[:, cb, :], lhsT=hT[hs][:, dc, tt * 128:(tt + 1) * 128],
                            rhs=Wsg[:, dc, cb * 512:(cb + 1) * 512], start=(dc == 0), stop=(dc == 15)),
                            reads=[("hT", hs), "W"], writes=[("ps", cb)])
                B.op("act", R.activation(out=gu.rearrange("p (a b) -> p a b", a=2), in_=ps[:, 0:2, :],
                                                   func=AF.Gelu_apprx_tanh),
                     reads=[("ps", 0), ("ps", 1)], writes=["gu"])
                B.op("act", R.activation(out=gv.rearrange("p (a b) -> p a b", a=2), in_=ps[:, 2:4, :],
                                                   func=AF.Gelu_apprx_tanh),
                     reads=[("ps", 2), ("ps", 3)], writes=["gv"])
                B.op("act", R.activation(out=tha.rearrange("p (a b) -> p a b", a=2), in_=ps[:, 4:6, :],
                                                   func=AF.Tanh, scale=0.5),
                     reads=[("ps", 4), ("ps", 5)], writes=["tha"])
                B.op("dve", R.bn_stats(out=bst[:, 0, :], in_=gv[:, 0:512]), reads=["gv"], writes=["bst0"])
                B.op("dve", R.bn_stats(out=bst[:, 1, :], in_=gv[:, 512:1024]), reads=["gv"], writes=["bst1"])
                B.op("dve", R.bn_aggr(out=small[:, 16:18], in_=bst.rearrange("p a b -> p (a b)")),
                     reads=["bst0", "bst1"], writes=["mv"])
                B.op("act", R.activation(out=small[:, 18:19], in_=small[:, 17:18], func=AF.Ln, bias=epsb, scale=1.0),
                     reads=["mv"], writes=["lnv"])
                B.op("act", R.activation(out=small[:, 19:20], in_=small[:, 18:19], func=AF.Exp, scale=-0.5),
                     reads=["lnv"], writes=["lrs"])
                B.op("dve", R.tensor_scalar(out=t1, in0=gv, scalar1=small[:, 16:17], scalar2=small[:, 19:20],
                                                      op0=ALU.subtract, op1=ALU.mult),
                     reads=["gv", "mv", "lrs"], writes=["t1"])
                B.op("dve", R.tensor_tensor(out=t1, in0=t1, in1=lngb, op=ALU.mult), reads=["t1"], writes=["t1"])
                B.op("dve", R.tensor_tensor(out=vn, in0=t1, in1=lnbb, op=ALU.add), reads=["t1"], writes=["vn"])
                for g8 in range(8):
                    B.op("pe", R.matmul(ps[:, 6 + g8 // 4, (g8 % 4) * 128:(g8 % 4 + 1) * 128],
                                                         lhsT=wT[:, g8, :], rhs=vn[:, g8 * 128:(g8 + 1) * 128],
                                                         start=True, stop=True),
                         reads=["vn"], writes=[("ps", 6 + g8 // 4)])
                for hf in range(2):
                    B.op("dve", R.scalar_tensor_tensor(
                        out=sa2[:, hf * 512:(hf + 1) * 512], in0=tha[:, hf * 512:(hf + 1) * 512], scalar=1.0,
                        in1=ps[:, 4 + hf, :], op0=ALU.add, op1=ALU.mult),
                        reads=["tha", ("ps", 4 + hf)], writes=[("sa2", hf)])
                for g8 in range(8):
                    B.op("dve", R.scalar_tensor_tensor(
                        out=aa[:, g8 * 128:(g8 + 1) * 128], in0=ps[:, 6 + g8 // 4, (g8 % 4) * 128:(g8 % 4 + 1) * 128],
                        scalar=bT[:, g8:g8 + 1], in1=gu[:, g8 * 128:(g8 + 1) * 128], op0=ALU.add, op1=ALU.mult),
                        reads=[("ps", 6 + g8 // 4), "gu"], writes=[("aa", g8)])
                B.op("dve", R.scalar_tensor_tensor(out=aout, in0=aa, scalar=0.5, in1=sa2, op0=ALU.mult, op1=ALU.mult),
                     reads=[("aa", g8) for g8 in range(8)] + [("sa2", 0), ("sa2", 1)], writes=["aout"])
                tpb = psb16(0, 1)
                for ec in range(8):
                    B.op("pe", R.transpose(tpb[:, ec * 128:(ec + 1) * 128], aout[:, ec * 128:(ec + 1) * 128], ident),
                         reads=["aout"], writes=[("ps", 0)])
                B.op("act", R.activation(out=yst[ys][:, :, tt * 128:(tt + 1) * 128],
                                                          in_=tpb.rearrange("p (c n) -> p c n", c=8), func=AF.Copy),
                     reads=[("ps", 0)], writes=[("yst", ys, tt)])
            B.dma("pool", ("yst", ys), YTd[0:8, :, i * 512:(i + 1) * 512].rearrange("c p t -> p c t"), yst[ys],
                  reads=[("yst", ys, t) for t in range(4)])
        B.barrier()

        mem.ptr = pmark
        kT = [mem.alloc([NT * 128], BF16) for _ in range(2)]
        vA = [mem.alloc([NT, 130], BF16) for _ in range(2)]
        bm = [mem.alloc([5, 512], F32) for _ in range(2)]
        qT = [mem.alloc([2, 512], BF16) for _ in range(2)]
        gt = [mem.alloc([4, 128], F32) for _ in range(2)]
        pt = [mem.alloc([512], BF16) for _ in range(4)]
        tmpf = [mem.alloc([512], F32) for _ in range(2)]
        accs = [mem.alloc([4, 258], F32) for _ in range(2)]
        et2 = [mem.alloc([128], F32) for _ in range(2)]
        eo = [mem.alloc([128], F32) for _ in range(2)]
        ej = mem.alloc([128], F32)
        eon = [mem.alloc([128], F32) for _ in range(2)]
        ebo = [mem.alloc([128], BF16) for _ in range(2)]
        ystc = [mem.alloc([512], BF16) for _ in range(2)]

        for s in range(2):
            B.op("dve", R.tensor_copy(out=vA[s][:, :, 128:129], in_=kval.rearrange("p (t o) -> p t o", o=1)),
                 reads=["kval"], writes=[("vones", s)])
        for a in range(4):
            B.op("dve", R.memset(ps[:, 3 + a, 0:258], 0.0), writes=[("ps", 3 + a)])

        def load_head(h):
            hs = h % 2
            B.dma("sp", ("kT", hs), kT[hs], KTd[h], writes=[("kT", hs)])
            B.dma("sp", ("vA", hs), vA[hs][:, :, 0:128], Vd[h], writes=[("vA", hs)])
            B.dma("sp", ("bm", hs), bm[hs].rearrange("p a b -> p (a b)"), BMd[h], writes=[("bm", hs)])

        ucount = [0]
        spc = [0]
        slotc = [0]
        ecount = [0]
        pending = []

        def epilogue_part(h, i, sl, a, gs, ys):
            k = ecount[0] % 2
            ecount[0] += 1
            c = 24 + 8 * k
            acc = accs[sl]
            B.op("dve", R.reciprocal(out=small[:, c:c + 1], in_=acc[:, a, 128:129]),
                 reads=[("accs", sl, a)], writes=[("e", c)])
            B.op("dve", R.reciprocal(out=small[:, c + 1:c + 2], in_=acc[:, a, 257:258]),
                 reads=[("accs", sl, a)], writes=[("e", c + 1)])
            B.op("dve", R.tensor_tensor(out=small[:, c + 2:c + 3], in0=small[:, c + 1:c + 2], in1=lamb, op=ALU.mult),
                 reads=[("e", c + 1)], writes=[("e", c + 2)])
            B.op("dve", R.tensor_scalar(out=et2[k], in0=acc[:, a, 129:257], scalar1=small[:, c + 2:c + 3],
                                                  scalar2=None, op0=ALU.mult),
                 reads=[("accs", sl, a), ("e", c + 2)], writes=[("et2", k)])
            B.op("dve", R.scalar_tensor_tensor(out=eo[k], in0=acc[:, a, 0:128], scalar=small[:, c:c + 1],
                                                         in1=et2[k], op0=ALU.mult, op1=ALU.subtract),
                 reads=[("accs", sl, a), ("e", c), ("et2", k)], writes=[("eo", k)])
            B.op("dve", R.scalar_tensor_tensor(out=ej, in0=eo[k], scalar=1.0, in1=eo[k], op0=ALU.mult,
                                                         op1=ALU.mult, accum_out=small[:, c + 3:c + 4]),
                 reads=[("eo", k)], writes=["ej", ("e", c + 3)])
            B.op("act", R.activation(out=small[:, c + 4:c + 5], in_=small[:, c + 3:c + 4], func=AF.Ln,
                                               scale=1.0 / 128, bias=epsb),
                 reads=[("e", c + 3)], writes=[("e", c + 4)])
            B.op("act", R.activation(out=small[:, c + 5:c + 6], in_=small[:, c + 4:c + 5], func=AF.Exp, scale=-0.5),
                 reads=[("e", c + 4)], writes=[("e", c + 5)])
            B.op("dve", R.scalar_tensor_tensor(out=eon[k], in0=eo[k], scalar=small[:, c + 5:c + 6], in1=sgl,
                                                         op0=ALU.mult, op1=ALU.mult),
                 reads=[("eo", k), ("e", c + 5)], writes=[("eon", k)])
            B.op("dve", R.tensor_tensor(out=ebo[k], in0=eon[k], in1=gt[gs][:, a, :], op=ALU.mult),
                 reads=[("eon", k), ("gt", gs)], writes=[("ebo", k)])
            tpc = psb16(7, 1)
            B.op("pe", R.transpose(tpc[:, a * 128:(a + 1) * 128], ebo[k], ident),
                 reads=[("ebo", k)], writes=[("ps", 7)])
            if a == 3:
                B.op("dve", R.tensor_copy(out=ystc[ys], in_=tpc[:, 0:512]), reads=[("ps", 7)], writes=[("ystc", ys)])
                B.dma("pool", ("ystc", ys), YTd[8 + h, :, i * 512:(i + 1) * 512], ystc[ys], reads=[("ystc", ys)])

        load_head(0)
        for h in range(NH):
            hs = h % 2
            if h + 1 < NH:
                load_head(h + 1)
            for i in range(NSLOT):
                sc = slotc[0]
                slotc[0] += 1
                qs = sc % 2
                gs = sc % 2
                sl = sc % 2
                ys = sc % 2
                B.dma("sp", ("qT", qs), qT[qs], QTd[h, :, :, i * 512:(i + 1) * 512].rearrange("m p t -> p m t"),
                      writes=[("qT", qs)])
                B.dma("sp", ("gt", gs), gt[gs], GBd[4 * i:4 * i + 4, :, h * 128:(h + 1) * 128].rearrange("t p c -> p t c"),
                      writes=[("gt", gs)])
                n = 16 * (i + 1)
                units = [(v, m) for v in range(n) for m in range(2)]

                def emit_S(v, m, u):
                    sb = u % 3
                    B.op("pe", R.matmul(ps[:, sb, :], lhsT=kT[hs][:, v * 128:(v + 1) * 128], rhs=qT[qs][:, m, :],
                                                  start=True, stop=True),
                         reads=[("kT", hs), ("qT", qs)], writes=[("ps", sb)])

                emit_S(units[0][0], units[0][1], ucount[0])
                for ui, (v, m) in enumerate(units):
                    u = ucount[0]
                    ucount[0] += 1
                    sb = u % 3
                    pslot = u % 4
                    if ui + 1 < len(units):
                        emit_S(units[ui + 1][0], units[ui + 1][1], u + 1)
                    r = v - (n - 4)
                    if r < -1:
                        B.op("act", R.activation(out=pt[pslot], in_=ps[:, sb, :], func=AF.Exp, scale=0.125,
                                                           bias=chb[:, h:h + 1]),
                             reads=[("ps", sb)], writes=[("pt", pslot)])
                    else:
                        tsl = spc[0] % 2
                        spc[0] += 1
                        B.op("dve", R.scalar_tensor_tensor(out=tmpf[tsl], in0=ps[:, sb, :], scalar=0.125,
                                                                     in1=bm[hs][:, r + 1, :], op0=ALU.mult, op1=ALU.add),
                             reads=[("ps", sb), ("bm", hs)], writes=[("tmpf", tsl)])
                        B.op("act", R.activation(out=pt[pslot], in_=tmpf[tsl], func=AF.Exp),
                             reads=[("tmpf", tsl)], writes=[("pt", pslot)])
                    for a in range(max(r, 0), 4):
                        B.op("pe", R.matmul(ps[:, 3 + a, m * 129:(m + 1) * 129],
                                                           lhsT=pt[pslot][:, a * 128:(a + 1) * 128],
                                                           rhs=vA[hs][:, v, 0:129], start=False, stop=False,
                                                           skip_group_check=True),
                             reads=[("pt", pslot), ("vA", hs), ("vones", hs)], writes=[("ps", 3 + a)])
                    if pending and ui in (6, 12, 18, 24):
                        pending.pop(0)()
                while pending:
                    pending.pop(0)()
                for a in range(4):
                    B.op("dve", R.tensor_copy(out=accs[sl][:, a, :], in_=ps[:, 3 + a, 0:258]),
                         reads=[("ps", 3 + a)], writes=[("accs", sl, a)])
                    B.op("dve", R.memset(ps[:, 3 + a, 0:258], 0.0), writes=[("ps", 3 + a)])
                for a in range(4):
                    pending.append(lambda h=h, i=i, sl=sl, a=a, gs=gs, ys=ys: epilogue_part(h, i, sl, a, gs, ys))
        while pending:
            pending.pop(0)()
        B.barrier()

        mem.ptr = pmark
        Wo = mem.alloc([16, 2048], BF16)
        fgb = mem.alloc([D], F32)
        yT = [mem.alloc([16, 512], BF16) for _ in range(2)]
        xo = [mem.alloc([D], F32) for _ in range(2)]
        rr = [mem.alloc([D], F32) for _ in range(2)]
        ot = [mem.alloc([D], F32) for _ in range(2)]
        sqj = mem.alloc([D], BF16)
        for dc in range(16):
            B.dma("pool", "w", Wo[:, dc, :], w_out[dc * 128:(dc + 1) * 128, :], writes=["W"])
        B.dma("sp", "pl", fgb, bcast(final_g, D), writes=["fgb"])
        tcount = 0
        for i in range(NSLOT):
            ysl = i % 2
            B.dma("sp", ("yT", ysl), yT[ysl], YTd[:, :, i * 512:(i + 1) * 512].rearrange("c p t -> p c t"),
                  writes=[("yT", ysl)])
            for tt in range(4):
                k = tcount % 2
                tcount += 1
                Tv = 16 * i + 12 + tt
                B.dma("sp", ("xo", k), xo[k], xv[Tv * 128:(Tv + 1) * 128, :], writes=[("xo", k)])
                for nb in range(4):
                    bk = 4 * k + nb
                    for ec in range(16):
                        B.op("pe", R.matmul(
                            ps[:, bk, :], lhsT=yT[ysl][:, ec, tt * 128:(tt + 1) * 128],
                            rhs=Wo[:, ec, nb * 512:(nb + 1) * 512], start=(ec == 0), stop=(ec == 15)),
                            reads=[("yT", ysl), "W"], writes=[("ps", bk)])
                    B.op("dve", R.tensor_tensor(out=rr[k][:, nb * 512:(nb + 1) * 512], in0=ps[:, bk, :],
                                                                      in1=xo[k][:, nb * 512:(nb + 1) * 512], op=ALU.add),
                         reads=[("ps", bk), ("xo", k)], writes=[("rr", k, nb)])
                col = 8 + 4 * k
                B.op("act", R.activation(out=sqj, in_=rr[k], func=AF.Square, accum_out=small[:, col:col + 1]),
                     reads=[("rr", k, nb) for nb in range(4)], writes=["sqj", ("st", col)])
                B.op("act", R.activation(out=small[:, col + 1:col + 2], in_=small[:, col:col + 1], func=AF.Ln,
                                                   scale=1.0 / D, bias=epsb),
                     reads=[("st", col)], writes=[("st", col + 1)])
                B.op("act", R.activation(out=small[:, col + 2:col + 3], in_=small[:, col + 1:col + 2], func=AF.Exp,
                                                   scale=-0.5),
                     reads=[("st", col + 1)], writes=[("st", col + 2)])
                B.op("dve", R.scalar_tensor_tensor(out=ot[k], in0=rr[k], scalar=small[:, col + 2:col + 3], in1=fgb,
                                                             op0=ALU.mult, op1=ALU.mult),
                     reads=[("rr", k, nb) for nb in range(4)] + [("st", col + 2), "fgb"], writes=[("ot", k)])
                B.dma("pool", ("ot", k), out_own[(4 * i + tt) * 128:(4 * i + tt + 1) * 128, :], ot[k], reads=[("ot", k)])
        B.barrier()

        def replay(e, stream):
            for ent in stream:
                if ent[0] == "wait":
                    e.wait_ge(ent[1], ent[2])
                else:
                    _, name, args, kwargs, sem, inc = ent
                    getattr(e, name)(*args, **kwargs).then_inc(sem, inc)

        with nc.Block() as block:
            @block.tensor
            def _(e):
                replay(e, B.streams["pe"])

            @block.scalar
            def _(e):
                replay(e, B.streams["act"])

            @block.vector
            def _(e):
                replay(e, B.streams["dve"])

            @block.gpsimd
            def _(e):
                replay(e, B.streams["pool"])

            @block.sync
            def _(e):
                replay(e, B.streams["sp"])
        print("instr counts", {k: len(v) for k, v in B.streams.items()}, "sems", len(B.dsem))
    return nc


def _t5_bucket_np(rel):
    try:
        import jax
        import jax.numpy as jnp
        with jax.default_device(jax.devices("cpu")[0]):
            rel_j = jnp.asarray(rel, dtype=jnp.int32)
            nb = 16
            max_exact = 8
            side = jnp.where(rel_j > 0, nb, 0)
            n = jnp.abs(rel_j)
            nf = jnp.maximum(n, 1).astype(jnp.float32)
            large = max_exact + (jnp.log(nf / max_exact) / math.log(128 / max_exact) * (nb - max_exact)).astype(jnp.int32)
            large = jnp.minimum(large, nb - 1)
            return np.asarray(side + jnp.where(n < max_exact, n, large)).astype(np.int64)
    except Exception:
        rel = np.asarray(rel, dtype=np.int64)
        side = np.where(rel > 0, 16, 0)
        n = np.abs(rel)
        nf = np.maximum(n, 1).astype(np.float32)
        large = 8 + (np.log(nf / np.float32(8)) / np.float32(math.log(16.0)) * np.float32(8)).astype(np.int32)
        large = np.minimum(large, 15)
        return side + np.where(n < 8, n, large)


_PROG_CACHE = {}


def kernel(x, norm_g, w_in, sgu_ln_g, sgu_ln_b, sgu_w, sgu_b, lambda_q1, lambda_k1,
           lambda_q2, lambda_k2, subln_g, rel_bias, w_out, final_g):
    x = np.asarray(x, dtype=np.float32)
    Bn, S, _ = x.shape
    assert Bn == 2 and S % 2048 == 0
    NSLOT = S // 2048
    NT = 16 * NSLOT
    bf = ml_dtypes.bfloat16

    u = np.arange(NR)
    bucket = _t5_bucket_np(511 - u)
    onehot = np.zeros((32, NR), np.float32)
    onehot[bucket, u] = 1.0
    kk = np.arange(128)[:, None]
    qq = np.arange(512)[None, :]
    masks = np.zeros((128, 4, 512), np.float32)
    for r in range(4):
        allowed = ((128 * r + kk) // 64) <= (qq // 64)
        masks[:, r, :] = np.where(allowed, 0.0, NEG)
    ss_ = np.arange(128)[:, None]
    tt_ = np.arange(128)[None, :]
    sgumask = ((ss_ // 64) <= (tt_ // 64)).astype(np.float32)
    ident = np.eye(128, dtype=np.float32)
    rev = np.ascontiguousarray(ident[::-1])
    lam_in = np.stack([np.asarray(a, np.float32).reshape(64) for a in (lambda_q1, lambda_k1, lambda_q2, lambda_k2)])

    common = {
        "w_in": np.ascontiguousarray(np.asarray(w_in, np.float32).reshape(D, DIN)),
        "w_out": np.ascontiguousarray(np.asarray(w_out, np.float32).reshape(D, D)),
        "norm_g": np.asarray(norm_g, np.float32).reshape(D),
        "final_g": np.asarray(final_g, np.float32).reshape(D),
        "sgu_ln_g": np.asarray(sgu_ln_g, np.float32).reshape(1024),
        "sgu_ln_b": np.asarray(sgu_ln_b, np.float32).reshape(1024),
        "sgu_w": np.ascontiguousarray(np.asarray(sgu_w, np.float32).reshape(8, 128, 128)),
        "sgu_b": np.ascontiguousarray(np.asarray(sgu_b, np.float32).reshape(8, 128)),
        "lam_in": lam_in,
        "subln_g": np.asarray(subln_g, np.float32).reshape(128),
        "rel_bias": np.ascontiguousarray(np.asarray(rel_bias, np.float32).reshape(32, 8)),
        "c_ident": ident.astype(bf),
        "c_identf": ident,
        "c_rev": rev,
        "c_onehot": onehot,
        "c_masks": np.ascontiguousarray(masks.reshape(128, 4 * 512)),
        "c_sgumask": sgumask,
    }
    in_maps = []
    for c in range(8):
        b, j = divmod(c, 4)
        npad = 12 - 4 * j
        xvirt = np.zeros((NT * 128, D), np.float32)
        xvirt[npad * 128:] = x[b, :(NT - npad) * 128]
        kvalid = np.zeros((128, NT), np.float32)
        kvalid[:, npad:] = 1.0
        m = dict(common)
        m["xv"] = xvirt
        m["c_kvalid"] = kvalid.astype(bf)
        in_maps.append(m)

    if NSLOT not in _PROG_CACHE:
        _PROG_CACHE[NSLOT] = build_program(NSLOT)
    nc = _PROG_CACHE[NSLOT]
    res = run_bass_kernel_spmd(nc, in_maps, core_ids=list(range(8)))
    out = np.empty((Bn, S, D), np.float32)
    for c in range(8):
        b, j = divmod(c, 4)
        o = np.asarray(res.results[c]["out_own"], np.float32)
        for i in range(NSLOT):
            q0 = (4 * i + j) * 512
            out[b, q0:q0 + 512] = o[i * 512:(i + 1) * 512]
    return out
```

```python
import math
from contextlib import ExitStack

import numpy as np
import ml_dtypes

import concourse.bass as bass
import concourse.mybir as mybir
from concourse.bass_utils import run_bass_kernel_spmd

F32 = mybir.dt.float32
BF16 = mybir.dt.bfloat16
AF = mybir.ActivationFunctionType
ALU = mybir.AluOpType

D = 2048
DIN = 7168
NH = 8
NR = 1151
EPS = 1e-6
NEG = -30000.0
ENG = ("pe", "act", "dve", "pool", "sp")
NDS = 90


class _Rec:
    def __getattr__(self, name):
        return lambda *a, **k: (name, a, k)


R = _Rec()


class Builder:
    def __init__(self, nc, stack):
        self.nc = nc
        self.streams = {e: [] for e in ENG}
        self.esem = {e: stack.enter_context(nc.semaphore("s_" + e)) for e in ENG}
        self.cnt = {e: 0 for e in ENG}
        self.pool = [stack.enter_context(nc.semaphore("d%d" % i)) for i in range(NDS)]
        self.dsem = {}
        self.waited = {e: {} for e in ENG}
        self.res = {}

    def _semof(self, key):
        if isinstance(key, str) and key in self.esem:
            return self.esem[key]
        return self.dsem[key][0]

    def _wait(self, eng, ev):
        key, val = ev
        if key == "pe" and eng == "pe":
            return
        w = self.waited[eng]
        if w.get(key, 0) >= val:
            return
        w[key] = val
        sem = self._semof(key)
        self.streams[eng].append(("wait", sem, val))

    def _deps(self, eng, reads, writes):
        for r in reads:
            st = self.res.get(r)
            if st and st[0] is not None:
                self._wait(eng, st[0])
        for w in writes:
            st = self.res.get(w)
            if st:
                if st[0] is not None:
                    self._wait(eng, st[0])
                for k, v in st[1].items():
                    self._wait(eng, (k, v))

    def _update(self, ev, reads, writes):
        for r in reads:
            st = self.res.setdefault(r, [None, {}])
            if st[1].get(ev[0], 0) < ev[1]:
                st[1][ev[0]] = ev[1]
        for w in writes:
            self.res[w] = [ev, {}]

    def op(self, eng, fn, reads=(), writes=()):
        self._deps(eng, reads, writes)
        self.cnt[eng] += 1
        ev = (eng, self.cnt[eng])
        sem = self.esem[eng]
        name, args, kwargs = fn
        self.streams[eng].append(("op", name, args, kwargs, sem, 1))
        self._update(ev, reads, writes)
        return ev

    def dma(self, q, key, out, in_, reads=(), writes=()):
        self._deps(q, reads, writes)
        if key not in self.dsem:
            self.dsem[key] = [self.pool.pop(), 0]
        d = self.dsem[key]
        d[1] += 16
        ev = (key, d[1])
        sem = d[0]
        self.streams[q].append(("op", "dma_start", (), dict(out=out, in_=in_), sem, 16))
        self._update(ev, reads, writes)
        return ev

    def barrier(self):
        evs = [(e, self.cnt[e]) for e in ENG if self.cnt[e] > 0]
        evs += [(k, d[1]) for k, d in self.dsem.items() if d[1] > 0]
        for eng in ENG:
            for ev in evs:
                self._wait(eng, ev)
        self.res = {}


class Mem:
    def __init__(self, big, cap):
        self.big = big
        self.cap = cap
        self.ptr = 0

    def alloc(self, free_shape, dtype):
        esz = 4 if dtype == F32 else 2
        n = 1
        for s in free_shape:
            n *= s
        nb = n * esz
        off = (self.ptr + 63) // 64 * 64
        assert off + nb <= self.cap, ("SBUF overflow", off, nb, self.cap)
        self.ptr = off + nb
        v = self.big[:, off // 2:(off + nb) // 2]
        if dtype == F32:
            v = v.bitcast(F32)
        if len(free_shape) == 2:
            v = v.rearrange("p (a b) -> p a b", a=free_shape[0])
        elif len(free_shape) == 3:
            v = v.rearrange("p (a b c) -> p a b c", a=free_shape[0], b=free_shape[1])
        return v


def build_program(NSLOT):
    NT = 16 * NSLOT
    NG = 4 * NSLOT
    NOWN = NSLOT * 512

    nc = bass.Bass("TRN2", target_bir_lowering=False)

    def din(name, shape, dt=F32):
        return nc.dram_tensor(name, list(shape), dt, kind="ExternalInput").ap()

    def dscr(name, shape, dt):
        return nc.dram_tensor(name, list(shape), dt, kind="Internal").ap()

    xv = din("xv", [NT * 128, D])
    w_in = din("w_in", [D, DIN])
    w_out = din("w_out", [D, D])
    norm_g = din("norm_g", [D])
    final_g = din("final_g", [D])
    ln_g = din("sgu_ln_g", [1024])
    ln_b = din("sgu_ln_b", [1024])
    sgu_w = din("sgu_w", [8, 128, 128])
    sgu_b = din("sgu_b", [8, 128])
    lam_in = din("lam_in", [4, 64])
    subln_g = din("subln_g", [128])
    rel_bias = din("rel_bias", [32, 8])
    c_ident = din("c_ident", [128, 128], BF16)
    c_identf = din("c_identf", [128, 128])
    c_rev = din("c_rev", [128, 128])
    c_onehot = din("c_onehot", [32, NR])
    c_masks = din("c_masks", [128, 4 * 512])
    c_sgumask = din("c_sgumask", [128, 128])
    c_kvalid = din("c_kvalid", [128, NT], BF16)
    out_own = nc.dram_tensor("out_own", [NOWN, D], F32, kind="ExternalOutput").ap()

    KTd = dscr("KTd", [NH, 128, NT * 128], BF16)
    Vd = dscr("Vd", [NH, 128, NT, 128], BF16)
    hTd = dscr("hTd", [NSLOT, 128, 16, 512], BF16)
    QTd = dscr("QTd", [NH, 2, 128, NOWN], BF16)
    GTd = dscr("GTd", [NH, 128, NOWN], F32)
    YTd = dscr("YTd", [16, 128, NOWN], BF16)
    Gd = dscr("Gd", [NH, NR], F32)
    BMd = dscr("BMd", [NH, 128, 5 * 512], F32)

    def bcast(ap1d, n, offset=0):
        return bass.AP(ap1d.tensor, offset, [[0, 128], [1, n]])

    CAP = 210944
    with ExitStack() as stack:
        big = stack.enter_context(nc.sbuf_tensor("big", [128, CAP // 2], BF16))
        ps = stack.enter_context(nc.psum_tensor("ps", [128, 8, 512], F32))
        ps2 = ps.rearrange("p b n -> p (b n)")
        B = Builder(nc, stack)
        mem = Mem(big, CAP)

        def psb16(bank, nbanks=1):
            return ps2[:, bank * 512:(bank + nbanks) * 512].bitcast(BF16)

        ident = mem.alloc([128], BF16)
        gbc = mem.alloc([D], F32)
        epsb = mem.alloc([1], F32)
        chb = mem.alloc([8], F32)
        lamb = mem.alloc([1], F32)
        sgl = mem.alloc([1], F32)
        kval = mem.alloc([NT], BF16)
        small = mem.alloc([64], F32)
        pmark = mem.ptr

        rb = mem.alloc([8], F32)
        oh = mem.alloc([NR], F32)
        rev = mem.alloc([128], F32)
        msk = mem.alloc([4, 512], F32)
        lv = mem.alloc([4, 64], F32)
        lj = mem.alloc([64], F32)
        gsb = mem.alloc([NR], F32)
        Xh = [mem.alloc([512], F32) for _ in range(2)]
        bmst = [mem.alloc([5, 512], F32) for _ in range(2)]

        B.dma("sp", "pl", ident, c_ident, writes=["ident"])
        B.dma("sp", "pl", gbc, bcast(norm_g, D), writes=["gbc"])
        B.dma("sp", "pl", chb, bcast(rel_bias, 8, 15 * 8), writes=["chb"])
        B.dma("sp", "pl", sgl, bass.AP(subln_g.tensor, 0, [[1, 128], [1, 1]]), writes=["sgl"])
        B.dma("sp", "pl", kval, c_kvalid, writes=["kval"])
        B.dma("sp", "pl", rb[0:32, :], rel_bias, writes=["rb"])
        B.dma("sp", "pl", oh[0:32, :], c_onehot, writes=["oh"])
        B.dma("sp", "pl", rev, c_rev, writes=["rev"])
        B.dma("sp", "pl", msk.rearrange("p a b -> p (a b)"), c_masks, writes=["msk"])
        for k in range(4):
            B.dma("sp", "pl", lv[:, k, :], bcast(lam_in, 64, k * 64), writes=[("lv", k)])
        B.barrier()

        B.op("dve", R.memset(epsb, EPS), writes=["epsb"])
        B.op("dve", R.tensor_scalar(out=sgl, in0=sgl, scalar1=0.4, scalar2=None, op0=ALU.mult),
             reads=["sgl"], writes=["sgl"])
        B.op("dve", R.scalar_tensor_tensor(out=lj, in0=lv[:, 0, :], scalar=1.0, in1=lv[:, 1, :],
                                                     op0=ALU.mult, op1=ALU.mult, accum_out=small[:, 0:1]),
             writes=["lj", "s0"])
        B.op("dve", R.scalar_tensor_tensor(out=lj, in0=lv[:, 2, :], scalar=1.0, in1=lv[:, 3, :],
                                                     op0=ALU.mult, op1=ALU.mult, accum_out=small[:, 1:2]),
             reads=[], writes=["lj", "s1"])
        B.op("act", R.activation(out=small[:, 2:4], in_=small[:, 0:2], func=AF.Exp),
             reads=["s0", "s1"], writes=["s23"])
        B.op("dve", R.tensor_tensor(out=small[:, 4:5], in0=small[:, 2:3], in1=small[:, 3:4],
                                              op=ALU.subtract), reads=["s23"], writes=["s4"])
        B.op("dve", R.tensor_scalar(out=lamb, in0=small[:, 4:5], scalar1=0.2, scalar2=None,
                                              op0=ALU.add), reads=["s4"], writes=["lamb"])
        for ci, (c0, n) in enumerate([(0, 512), (512, 512), (1024, NR - 1024)]):
            B.op("pe", R.matmul(ps[0:8, ci, 0:n], lhsT=rb[0:32, 0:8],
                                                              rhs=oh[0:32, c0:c0 + n], start=True, stop=True),
                 writes=[("ps", ci)])
            B.op("act", R.activation(out=gsb[0:8, c0:c0 + n], in_=ps[0:8, ci, 0:n],
                                                                  func=AF.Copy),
                 reads=[("ps", ci)], writes=[("gsb", ci)])
        B.dma("sp", "gd", Gd, gsb[0:8, :], reads=[("gsb", 0), ("gsb", 1), ("gsb", 2)], writes=["Gd"])
        cnt = 0
        for h in range(NH):
            bs = h % 2
            for r in range(-1, 4):
                xs = cnt % 2
                bk = 4 + cnt % 4
                cnt += 1
                src = bass.AP(Gd.tensor, h * NR + 128 * (3 - r), [[1, 128], [1, 512]])
                B.dma("sp", ("xh", xs), Xh[xs], src, reads=["Gd"], writes=[("xh", xs)])
                B.op("pe", R.matmul(ps[:, bk, :], lhsT=rev, rhs=Xh[xs], start=True, stop=True),
                     reads=[("xh", xs)], writes=[("ps", bk)])
                if r >= 0:
                    B.op("dve", R.tensor_tensor(out=bmst[bs][:, r + 1, :], in0=ps[:, bk, :],
                                                                            in1=msk[:, r, :], op=ALU.add),
                         reads=[("ps", bk)], writes=[("bmst", bs, r)])
                else:
                    B.op("dve", R.tensor_copy(out=bmst[bs][:, 0, :], in_=ps[:, bk, :]),
                         reads=[("ps", bk)], writes=[("bmst", bs, r)])
            B.dma("pool", ("bmst", bs), BMd[h], bmst[bs].rearrange("p a b -> p (a b)"),
                  reads=[("bmst", bs, r) for r in range(-1, 4)])
        B.barrier()

        mem.ptr = pmark
        Wkv = mem.alloc([16, 2048], BF16)
        xt = [mem.alloc([D], F32) for _ in range(3)]
        sqj = mem.alloc([D], BF16)
        hb = [mem.alloc([D], BF16) for _ in range(2)]
        hT = [mem.alloc([16, 512], BF16) for _ in range(2)]
        kst = [mem.alloc([8, 512], BF16) for _ in range(2)]
        vst = [mem.alloc([4, 1024], BF16) for _ in range(2)]

        for dc in range(16):
            B.dma("pool", "w", Wkv[:, dc, :], w_in[dc * 128:(dc + 1) * 128, 4096:6144], writes=["W"])

        def hT_keys(gs):
            return [("hT", gs, t, hf) for t in range(4) for hf in range(2)]

        def rms_stats(src, skey, col):
            B.op("act", R.activation(out=sqj, in_=src, func=AF.Square, accum_out=small[:, col:col + 1]),
                 reads=[skey], writes=["sqj", ("st", col)])
            B.op("act", R.activation(out=small[:, col + 1:col + 2], in_=small[:, col:col + 1], func=AF.Ln,
                                               scale=1.0 / D, bias=epsb),
                 reads=[("st", col)], writes=[("st", col + 1)])
            B.op("act", R.activation(out=small[:, col + 2:col + 3], in_=small[:, col + 1:col + 2],
                                               func=AF.Exp, scale=-0.5),
                 reads=[("st", col + 1)], writes=[("st", col + 2)])
            return ("st", col + 2), small[:, col + 2:col + 3]

        def frontend(T):
            g, tt = divmod(T, 4)
            xs, hs, ts, gs = T % 3, T % 2, T % 2, g % 2
            B.dma("sp", ("xt", xs), xt[xs], xv[T * 128:(T + 1) * 128, :], writes=[("xt", xs)])
            rkey, rstd = rms_stats(xt[xs], ("xt", xs), 8 + 4 * (T % 2))
            B.op("dve", R.scalar_tensor_tensor(out=hb[hs], in0=xt[xs], scalar=rstd, in1=gbc,
                                                         op0=ALU.mult, op1=ALU.mult),
                 reads=[("xt", xs), rkey], writes=[("hb", hs)])
            tpv = psb16(2 * ts, 2)
            for dc in range(16):
                B.op("pe", R.transpose(tpv[:, dc * 128:(dc + 1) * 128],
                                                        hb[hs][:, dc * 128:(dc + 1) * 128], ident),
                     reads=[("hb", hs)], writes=[("ps", 2 * ts + dc // 8)])
            B.op("act", R.activation(out=hT[gs][:, 0:8, tt * 128:(tt + 1) * 128],
                                               in_=tpv[:, 0:1024].rearrange("p (c n) -> p c n", c=8), func=AF.Copy),
                 reads=[("ps", 2 * ts)], writes=[("hT", gs, tt, 0)])
            B.op("dve", R.tensor_copy(out=hT[gs][:, 8:16, tt * 128:(tt + 1) * 128],
                                                in_=tpv[:, 1024:2048].rearrange("p (c n) -> p c n", c=8)),
                 reads=[("ps", 2 * ts + 1)], writes=[("hT", gs, tt, 1)])

        chain_no = [0]

        def backend_chains(g):
            gs, ks, vs = g % 2, g % 2, g % 2
            chains = []

            def kchain(fc):
                bk = 4 + chain_no[0] % 4
                chain_no[0] += 1
                for dc in range(16):
                    B.op("pe", R.matmul(ps[:, bk, :], lhsT=Wkv[:, dc, fc * 128:(fc + 1) * 128],
                                                         rhs=hT[gs][:, dc, :], start=(dc == 0), stop=(dc == 15)),
                         reads=hT_keys(gs) + ["W"], writes=[("ps", bk)])
                if fc % 2 == 0:
                    B.op("act", R.activation(out=kst[ks][:, fc, :], in_=ps[:, bk, :], func=AF.Copy),
                         reads=[("ps", bk)], writes=[("kst", ks, fc)])
                else:
                    B.op("dve", R.tensor_copy(out=kst[ks][:, fc, :], in_=ps[:, bk, :]),
                         reads=[("ps", bk)], writes=[("kst", ks, fc)])
                if fc == 7:
                    B.dma("pool", ("kst", ks), KTd[:, :, g * 512:(g + 1) * 512].rearrange("h p t -> p h t"), kst[ks],
                          reads=[("kst", ks, f) for f in range(8)])

            def vchain(tt, hf):
                bk = 4 + chain_no[0] % 4
                chain_no[0] += 1
                for dc in range(16):
                    B.op("pe", R.matmul(ps[:, bk, :], lhsT=hT[gs][:, dc, tt * 128:(tt + 1) * 128],
                                                         rhs=Wkv[:, dc, 1024 + hf * 512:1024 + (hf + 1) * 512],
                                                         start=(dc == 0), stop=(dc == 15)),
                         reads=[("hT", gs, tt, 0), ("hT", gs, tt, 1), "W"], writes=[("ps", bk)])
                if hf == 0:
                    B.op("act", R.activation(out=vst[vs][:, tt, 0:512], in_=ps[:, bk, :], func=AF.Copy),
                         reads=[("ps", bk)], writes=[("vst", vs, tt, hf)])
                else:
                    B.op("dve", R.tensor_copy(out=vst[vs][:, tt, 512:1024], in_=ps[:, bk, :]),
                         reads=[("ps", bk)], writes=[("vst", vs, tt, hf)])
                if tt == 3 and hf == 1:
                    for t4 in range(4):
                        B.dma("pool", ("vst", vs),
                              Vd[:, :, 4 * g + t4, :].rearrange("h p d -> p h d"),
                              vst[vs][:, t4, :].rearrange("p (h d) -> p h d", h=8),
                              reads=[("vst", vs, t, f) for t in range(4) for f in range(2)])
                    if g % 4 == 3:
                        B.dma("pool", ("hTd", gs), hTd[g // 4], hT[gs], reads=hT_keys(gs))

            for fc in range(8):
                chains.append(lambda fc=fc: kchain(fc))
            for tt in range(4):
                for hf in range(2):
                    chains.append(lambda tt=tt, hf=hf: vchain(tt, hf))
            return chains

        for T in range(4):
            frontend(T)
        for g in range(NG):
            chains = backend_chains(g)
            for ci, ch in enumerate(chains):
                ch()
                if ci % 4 == 3 and g + 1 < NG:
                    frontend(4 * (g + 1) + ci // 4)
        B.barrier()

        mem.ptr = pmark
        Wqg = mem.alloc([16, 2048], BF16)
        hT = [mem.alloc([16, 512], BF16) for _ in range(2)]
        qst = [mem.alloc([2, 8, 512], BF16) for _ in range(2)]
        gst = [mem.alloc([512], F32) for _ in range(2)]
        th = [mem.alloc([512], F32) for _ in range(2)]
        for dc in range(16):
            B.dma("pool", "w", Wqg[:, dc, 0:1024], w_in[dc * 128:(dc + 1) * 128, 3072:4096], writes=["W"])
            B.dma("pool", "w", Wqg[:, dc, 1024:2048], w_in[dc * 128:(dc + 1) * 128, 6144:7168], writes=["W"])
        for s in range(2):
            B.op("dve", R.memset(qst[s].rearrange("p a b c -> p (a b c)"), 0.0),
                 writes=[("qst", s, m, f) for m in range(2) for f in range(8)])
        cno = 0
        tno = 0
        for i in range(NSLOT):
            hs = i % 2
            qs = i % 2
            B.dma("sp", ("hTl", hs), hT[hs], hTd[i], writes=[("hT", hs)])
            for fc in range(8):
                bk = cno % 4
                cno += 1
                for dc in range(16):
                    B.op("pe", R.matmul(ps[:, bk, :], lhsT=Wqg[:, dc, fc * 128:(fc + 1) * 128],
                                                                      rhs=hT[hs][:, dc, :], start=(dc == 0), stop=(dc == 15)),
                         reads=[("hT", hs), "W"], writes=[("ps", bk)])
                B.op("act", R.activation(out=qst[qs][0:64, 0, fc, :], in_=ps[0:64, bk, :], func=AF.Copy),
                     reads=[("ps", bk)], writes=[("qst", qs, 0, fc)])
                B.op("dve", R.tensor_copy(out=qst[qs][64:128, 1, fc, :], in_=ps[64:128, bk, :]),
                     reads=[("ps", bk)], writes=[("qst", qs, 1, fc)])
            for m2 in range(2):
                B.dma("pool", ("qst", qs), QTd[:, m2, :, i * 512:(i + 1) * 512].rearrange("h p t -> p h t"),
                      qst[qs][:, m2, :, :], reads=[("qst", qs, m, f) for m in range(2) for f in range(8)])
            for hh in range(8):
                gsl = hh % 2
                bk = 4 + cno % 4
                cno += 1
                tsl = tno % 2
                tno += 1
                for dc in range(16):
                    B.op("pe", R.matmul(ps[:, bk, :], lhsT=Wqg[:, dc, 1024 + hh * 128:1024 + (hh + 1) * 128],
                                        rhs=hT[hs][:, dc, :], start=(dc == 0), stop=(dc == 15)),
                         reads=[("hT", hs), "W"], writes=[("ps", bk)])
                B.op("act", R.activation(out=th[tsl], in_=ps[:, bk, :], func=AF.Tanh, scale=0.5),
                     reads=[("ps", bk)], writes=[("th", tsl)])
                B.op("dve", R.scalar_tensor_tensor(out=gst[gsl], in0=th[tsl], scalar=1.0, in1=ps[:, bk, :],
                                                   op0=ALU.add, op1=ALU.mult),
                     reads=[("th", tsl), ("ps", bk)], writes=[("gst", gsl)])
                B.dma("pool", ("gst", gsl), GTd[hh, :, i * 512:(i + 1) * 512], gst[gsl], reads=[("gst", gsl)])
        B.barrier()

        mem.ptr = pmark
        Wsg = mem.alloc([16, 3072], BF16)
        hT = [mem.alloc([16, 512], BF16) for _ in range(2)]
        wnat = mem.alloc([8, 128], F32)
        identf = mem.alloc([128], F32)
        smk = mem.alloc([128], F32)
        bnat = mem.alloc([128], F32)
        wT = mem.alloc([8, 128], BF16)
        bT = mem.alloc([8], F32)
        lngb = mem.alloc([1024], F32)
        lnbb = mem.alloc([1024], F32)
        gu = mem.alloc([1024], F32)
        gv = mem.alloc([1024], F32)
        t1 = mem.alloc([1024], F32)
        vn = mem.alloc([1024], BF16)
        tha = mem.alloc([1024], F32)
        sa2 = mem.alloc([1024], F32)
        aa = mem.alloc([1024], F32)
        aout = mem.alloc([1024], BF16)
        bst = mem.alloc([2, 6], F32)
        yst = [mem.alloc([8, 512], BF16) for _ in range(2)]
        for dc in range(16):
            B.dma("pool", "w", Wsg[:, dc, :], w_in[dc * 128:(dc + 1) * 128, 0:3072], writes=["W"])
        B.dma("sp", "pl", wnat, sgu_w.rearrange("g t s -> t g s"), writes=["wnat"])
        B.dma("sp", "pl", identf, c_identf, writes=["identf"])
        B.dma("sp", "pl", smk, c_sgumask, writes=["smk"])
        B.dma("sp", "pl", bnat[0:8, :], sgu_b, writes=["bnat"])
        B.dma("sp", "pl", lngb, bcast(ln_g, 1024), writes=["lngb"])
        B.dma("sp", "pl", lnbb, bcast(ln_b, 1024), writes=["lnbb"])
        B.barrier()
        for g8 in range(8):
            bk = g8 % 4
            B.op("pe", R.transpose(ps[:, bk, 0:128], wnat[:, g8, :], identf),
                 reads=["wnat", "identf"], writes=[("ps", bk)])
            B.op("dve", R.tensor_tensor(out=wT[:, g8, :], in0=ps[:, bk, 0:128], in1=smk, op=ALU.mult),
                 reads=[("ps", bk), "smk"], writes=[("wT", g8)])
        B.op("pe", R.transpose(ps[:, 4, 0:8], bnat[0:8, :], identf[0:8, 0:8]),
             reads=["bnat", "identf"], writes=[("ps", 4)])
        B.op("dve", R.tensor_copy(out=bT, in_=ps[:, 4, 0:8]), reads=[("ps", 4)], writes=["bT"])
        B.barrier()

        for i in range(NSLOT):
            hs = i % 2
            ys = i % 2
            B.dma("sp", ("hTl", hs), hT[hs], hTd[i], writes=[("hT", hs)])
            for tt in range(4):
                for cb in range(6):
                    for dc in range(16):
                        B.op("pe", R.matmul(
                            ps[:, cb, :], lhsT=hT[hs][:, dc, tt * 128:(tt + 1) * 128],
                            rhs=Wsg[:, dc, cb * 512:(cb + 1) * 512], start=(dc == 0), stop=(dc == 15)),
                            reads=[("hT", hs), "W"], writes=[("ps", cb)])
                B.op("act", R.activation(out=gu.rearrange("p (a b) -> p a b", a=2), in_=ps[:, 0:2, :],
                                                   func=AF.Gelu_apprx_tanh),
                     reads=[("ps", 0), ("ps", 1)], writes=["gu"])
                B.op("act", R.activation(out=gv.rearrange("p (a b) -> p a b", a=2), in_=ps[:, 2:4, :],
                                                   func=AF.Gelu_apprx_tanh),
                     reads=[("ps", 2), ("ps", 3)], writes=["gv"])
                B.op("act", R.activation(out=tha.rearrange("p (a b) -> p a b", a=2), in_=ps[:, 4:6, :],
                                                   func=AF.Tanh, scale=0.5),
                     reads=[("ps", 4), ("ps", 5)], writes=["tha"])
                B.op("dve", R.bn_stats(out=bst[:, 0, :], in_=gv[:, 0:512]), reads=["gv"], writes=["bst0"])
                B.op("dve", R.bn_stats(out=bst[:, 1, :], in_=gv[:, 512:1024]), reads=["gv"], writes=["bst1"])
                B.op("dve", R.bn_aggr(out=small[:, 16:18], in_=bst.rearrange("p a b -> p (a b)")),
                     reads=["bst0", "bst1"], writes=["mv"])
                B.op("act", R.activation(out=small[:, 18:19], in_=small[:, 17:18], func=AF.Ln, bias=epsb, scale=1.0),
                     reads=["mv"], writes=["lnv"])
                B.op("act", R.activation(out=small[:, 19:20], in_=small[:, 18:19], func=AF.Exp, scale=-0.5),
                     reads=["lnv"], writes=["lrs"])
                B.op("dve", R.tensor_scalar(out=t1, in0=gv, scalar1=small[:, 16:17], scalar2=small[:, 19:20],
                                                      op0=ALU.subtract, op1=ALU.mult),
                     reads=["gv", "mv", "lrs"], writes=["t1"])
                B.op("dve", R.tensor_tensor(out=t1, in0=t1, in1=lngb, op=ALU.mult), reads=["t1"], writes=["t1"])
                B.op("dve", R.tensor_tensor(out=vn, in0=t1, in1=lnbb, op=ALU.add), reads=["t1"], writes=["vn"])
                for g8 in range(8):
                    B.op("pe", R.matmul(ps[:, 6 + g8 // 4, (g8 % 4) * 128:(g8 % 4 + 1) * 128],
                                                         lhsT=wT[:, g8, :], rhs=vn[:, g8 * 128:(g8 + 1) * 128],
                                                         start=True, stop=True),
                         reads=["vn"], writes=[("ps", 6 + g8 // 4)])
                for hf in range(2):
                    B.op("dve", R.scalar_tensor_tensor(
                        out=sa2[:, hf * 512:(hf + 1) * 512], in0=tha[:, hf * 512:(hf + 1) * 512], scalar=1.0,
                        in1=ps[:, 4 + hf, :], op0=ALU.add, op1=ALU.mult),
                        reads=["tha", ("ps", 4 + hf)], writes=[("sa2", hf)])
                for g8 in range(8):
                    B.op("dve", R.scalar_tensor_tensor(
                        out=aa[:, g8 * 128:(g8 + 1) * 128], in0=ps[:, 6 + g8 // 4, (g8 % 4) * 128:(g8 % 4 + 1) * 128],
                        scalar=bT[:, g8:g8 + 1], in1=gu[:, g8 * 128:(g8 + 1) * 128], op0=ALU.add, op1=ALU.mult),
                        reads=[("ps", 6 + g8 // 4), "gu"], writes=[("aa", g8)])
                B.op("dve", R.scalar_tensor_tensor(out=aout, in0=aa, scalar=0.5, in1=sa2, op0=ALU.mult, op1=ALU.mult),
                     reads=[("aa", g8) for g8 in range(8)] + [("sa2", 0), ("sa2", 1)], writes=["aout"])
                tpb = psb16(0, 1)
                for ec in range(8):
                    B.op("pe", R.transpose(tpb[:, ec * 128:(ec + 1) * 128], aout[:, ec * 128:(ec + 1) * 128], ident),
                         reads=["aout"], writes=[("ps", 0)])
                B.op("act", R.activation(out=yst[ys][:, :, tt * 128:(tt + 1) * 128],
                                                          in_=tpb.rearrange("p (c n) -> p c n", c=8), func=AF.Copy),
                     reads=[("ps", 0)], writes=[("yst", ys, tt)])
            B.dma("pool", ("yst", ys), YTd[0:8, :, i * 512:(i + 1) * 512].rearrange("c p t -> p c t"), yst[ys],
                  reads=[("yst", ys, t) for t in range(4)])
        B.barrier()

        mem.ptr = pmark
        kT = [mem.alloc([NT * 128], BF16) for _ in range(2)]
        vA = [mem.alloc([NT, 128], BF16) for _ in range(2)]
        bm = mem.alloc([5, 512], F32)
        qT = [mem.alloc([2, 512], BF16) for _ in range(2)]
        gtT = [mem.alloc([512], F32) for _ in range(2)]
        pt = [mem.alloc([2, 512], BF16) for _ in range(3)]
        tmpf = mem.alloc([2, 512], F32)
        rsd = [mem.alloc([2, 512], F32) for _ in range(2)]
        rsp = [mem.alloc([2, 512], F32) for _ in range(2)]
        onesf = mem.alloc([128], F32)
        kvf = mem.alloc([NT], F32)
        nlam = mem.alloc([1], F32)
        sbs = mem.alloc([2, 512], F32)
        ea = mem.alloc([512], F32)
        eb = mem.alloc([512], F32)
        eo = mem.alloc([512], F32)
        ep = mem.alloc([512], F32)
        ystc = [mem.alloc([512], BF16) for _ in range(2)]

        B.op("dve", R.memset(onesf, 1.0), writes=["onesf"])
        B.op("dve", R.tensor_copy(out=kvf, in_=kval), reads=["kval"], writes=["kvf"])
        B.op("dve", R.tensor_scalar(out=nlam, in0=lamb, scalar1=-1.0, scalar2=None, op0=ALU.mult),
             reads=["lamb"], writes=["nlam"])

        def load_head(h):
            hs = h % 2
            B.dma("sp", ("kT", hs), kT[hs], KTd[h], writes=[("kT", hs)])
            B.dma("sp", ("vA", hs), vA[hs], Vd[h], writes=[("vA", hs)])

        pcount = [0]
        slotc = [0]
        pending = []

        def flat(t):
            return t.rearrange("p a b -> p (a b)")

        def epilogue_parts(h, i, sl, gs, ys, has_pool):
            def part0():
                for m in range(2):
                    B.op("pe", R.matmul(ps[:, 6 + m, :], lhsT=onesf, rhs=rsd[sl][:, m, :], start=True, stop=(not has_pool)),
                         reads=[("rsd", sl), "onesf"], writes=[("ps", 6 + m)])
                    if has_pool:
                        B.op("pe", R.matmul(ps[:, 6 + m, :], lhsT=onesf, rhs=rsp[sl][:, m, :], start=False, stop=True),
                             reads=[("rsp", sl), "onesf"], writes=[("ps", 6 + m)])
                B.op("dve", R.tensor_copy(out=ea, in_=ps[:, 4, :]), reads=[("ps", 4)], writes=["ea"])
                B.op("dve", R.tensor_copy(out=eb, in_=ps[:, 5, :]), reads=[("ps", 5)], writes=["eb"])
                B.op("dve", R.tensor_scalar(out=sbs, in0=ps[:, 6:8, :], scalar1=2.0 ** -14, scalar2=None, op0=ALU.mult),
                     reads=[("ps", 6), ("ps", 7)], writes=["sbs"])

            def part1():
                B.op("dve", R.tensor_tensor(out=ea, in0=ea, in1=sbs[:, 1, :], op=ALU.mult), reads=["ea", "sbs"], writes=["ea"])
                B.op("dve", R.tensor_tensor(out=eb, in0=eb, in1=sbs[:, 0, :], op=ALU.mult), reads=["eb", "sbs"], writes=["eb"])
                B.op("dve", R.scalar_tensor_tensor(out=eo, in0=eb, scalar=nlam, in1=ea, op0=ALU.mult, op1=ALU.add),
                     reads=["ea", "eb", "nlam"], writes=["eo"])
                B.op("dve", R.tensor_tensor(out=eb, in0=eo, in1=eo, op=ALU.mult), reads=["eo"], writes=["eb"])
                B.op("pe", R.matmul(ps[:, 6, :], lhsT=onesf, rhs=eb, start=True, stop=True),
                     reads=["eb", "onesf"], writes=[("ps", 6)])
                B.op("dve", R.tensor_tensor(out=ep, in0=sbs[:, 0, :], in1=sbs[:, 1, :], op=ALU.mult), reads=["sbs"], writes=["ep"])
                B.op("dve", R.scalar_tensor_tensor(out=ep, in0=ep, scalar=EPS, in1=ep, op0=ALU.mult, op1=ALU.mult),
                     reads=["ep"], writes=["ep"])

            def part2():
                B.op("dve", R.scalar_tensor_tensor(out=ea, in0=ps[:, 6, :], scalar=1.0 / 128, in1=ep, op0=ALU.mult, op1=ALU.add),
                     reads=[("ps", 6), "ep"], writes=["ea"])
                B.op("act", R.activation(out=ea, in_=ea, func=AF.Ln), reads=["ea"], writes=["ea"])
                B.op("act", R.activation(out=ea, in_=ea, func=AF.Exp, scale=-0.5), reads=["ea"], writes=["ea"])

            def part3():
                B.op("dve", R.scalar_tensor_tensor(out=eo, in0=eo, scalar=sgl, in1=ea, op0=ALU.mult, op1=ALU.mult),
                     reads=["eo", "ea", "sgl"], writes=["eo"])
                B.op("dve", R.tensor_tensor(out=ystc[ys], in0=eo, in1=gtT[gs], op=ALU.mult),
                     reads=["eo", ("gtT", gs)], writes=[("ystc", ys)])
                B.dma("pool", ("ystc", ys), YTd[8 + h, :, i * 512:(i + 1) * 512], ystc[ys], reads=[("ystc", ys)])
            return [part0, part1, part2, part3]

        load_head(0)
        for h in range(NH):
            hs = h % 2
            if h + 1 < NH:
                load_head(h + 1)
            B.dma("sp", "bm", flat(bm), BMd[h], writes=["bm"])
            for i in range(NSLOT):
                sc = slotc[0]
                slotc[0] += 1
                qs = gs = sl = ys = sc % 2
                B.dma("sp", ("qT", qs), qT[qs], QTd[h, :, :, i * 512:(i + 1) * 512].rearrange("m p t -> p m t"),
                      writes=[("qT", qs)])
                B.dma("sp", ("gtT", gs), gtT[gs], GTd[h, :, i * 512:(i + 1) * 512], writes=[("gtT", gs)])
                n = 16 * (i + 1)

                def emit_S(v, pu):
                    b = pu % 2
                    for m in range(2):
                        B.op("pe", R.matmul(ps[:, 2 * b + m, :], lhsT=kT[hs][:, v * 128:(v + 1) * 128], rhs=qT[qs][:, m, :],
                                            start=True, stop=True),
                             reads=[("kT", hs), ("qT", qs)], writes=[("ps", 2 * b + m)])

                emit_S(0, pcount[0])
                pool_started = False
                for v in range(n):
                    pu = pcount[0]
                    pcount[0] += 1
                    b = pu % 2
                    s3 = pu % 3
                    if v + 1 < n:
                        emit_S(v + 1, pu + 1)
                    r = v - (n - 4)
                    if r < -1:
                        B.op("act", R.activation(out=pt[s3], in_=ps[:, 2 * b:2 * b + 2, :], func=AF.Exp, scale=0.125,
                                                 bias=chb[:, h:h + 1]),
                             reads=[("ps", 2 * b), ("ps", 2 * b + 1)], writes=[("pt", s3)])
                    else:
                        for m in range(2):
                            B.op("dve", R.scalar_tensor_tensor(out=tmpf[:, m, :], in0=ps[:, 2 * b + m, :], scalar=0.125,
                                                               in1=bm[:, r + 1, :], op0=ALU.mult, op1=ALU.add),
                                 reads=[("ps", 2 * b + m), "bm"], writes=[("tmpf", m)])
                        B.op("act", R.activation(out=pt[s3], in_=tmpf, func=AF.Exp),
                             reads=[("tmpf", 0), ("tmpf", 1)], writes=[("pt", s3)])
                    for m in range(2):
                        B.op("pe", R.matmul(ps[:, 4 + m, :], lhsT=vA[hs][:, v, :], rhs=pt[s3][:, m, :],
                                            start=(v == 0), stop=(v == n - 1)),
                             reads=[("pt", s3), ("vA", hs)], writes=[("ps", 4 + m)])
                    if v == 0:
                        B.op("dve", R.tensor_scalar(out=flat(rsd[sl]), in0=flat(pt[s3]), scalar1=kvf[:, 0:1], scalar2=None,
                                                    op0=ALU.mult),
                             reads=[("pt", s3), "kvf"], writes=[("rsd", sl)])
                    elif v < 12:
                        B.op("dve", R.scalar_tensor_tensor(out=flat(rsd[sl]), in0=flat(pt[s3]), scalar=kvf[:, v:v + 1],
                                                           in1=flat(rsd[sl]), op0=ALU.mult, op1=ALU.add),
                             reads=[("pt", s3), "kvf", ("rsd", sl)], writes=[("rsd", sl)])
                    elif v % 4 == 3:
                        if not pool_started:
                            pool_started = True
                            B.op("pool", R.tensor_copy(out=flat(rsp[sl]), in_=flat(pt[s3])),
                                 reads=[("pt", s3)], writes=[("rsp", sl)])
                        else:
                            B.op("pool", R.tensor_tensor(out=flat(rsp[sl]), in0=flat(rsp[sl]), in1=flat(pt[s3]), op=ALU.add),
                                 reads=[("pt", s3), ("rsp", sl)], writes=[("rsp", sl)])
                    else:
                        B.op("dve", R.tensor_tensor(out=flat(rsd[sl]), in0=flat(rsd[sl]), in1=flat(pt[s3]), op=ALU.add),
                             reads=[("pt", s3), ("rsd", sl)], writes=[("rsd", sl)])
                    if pending and v in (2, 5, 8):
                        pending.pop(0)()
                while pending:
                    pending.pop(0)()
                parts = epilogue_parts(h, i, sl, gs, ys, pool_started)
                parts[0]()
                pending.extend(parts[1:])
        while pending:
            pending.pop(0)()
        B.barrier()

        mem.ptr = pmark
        Wo = mem.alloc([16, 2048], BF16)
        fgb = mem.alloc([D], F32)
        yT = [mem.alloc([16, 512], BF16) for _ in range(2)]
        xo = [mem.alloc([D], F32) for _ in range(2)]
        rr = [mem.alloc([D], F32) for _ in range(2)]
        ot = [mem.alloc([D], F32) for _ in range(2)]
        sqj = mem.alloc([D], BF16)
        for dc in range(16):
            B.dma("pool", "w", Wo[:, dc, :], w_out[dc * 128:(dc + 1) * 128, :], writes=["W"])
        B.dma("sp", "pl", fgb, bcast(final_g, D), writes=["fgb"])
        tcount = 0
        for i in range(NSLOT):
            ysl = i % 2
            B.dma("sp", ("yT", ysl), yT[ysl], YTd[:, :, i * 512:(i + 1) * 512].rearrange("c p t -> p c t"),
                  writes=[("yT", ysl)])
            for tt in range(4):
                k = tcount % 2
                tcount += 1
                Tv = 16 * i + 12 + tt
                B.dma("sp", ("xo", k), xo[k], xv[Tv * 128:(Tv + 1) * 128, :], writes=[("xo", k)])
                for nb in range(4):
                    bk = 4 * k + nb
                    for ec in range(16):
                        B.op("pe", R.matmul(
                            ps[:, bk, :], lhsT=yT[ysl][:, ec, tt * 128:(tt + 1) * 128],
                            rhs=Wo[:, ec, nb * 512:(nb + 1) * 512], start=(ec == 0), stop=(ec == 15)),
                            reads=[("yT", ysl), "W"], writes=[("ps", bk)])
                    B.op("dve", R.tensor_tensor(out=rr[k][:, nb * 512:(nb + 1) * 512], in0=ps[:, bk, :],
                                                                      in1=xo[k][:, nb * 512:(nb + 1) * 512], op=ALU.add),
                         reads=[("ps", bk), ("xo", k)], writes=[("rr", k, nb)])
                col = 8 + 4 * k
                B.op("act", R.activation(out=sqj, in_=rr[k], func=AF.Square, accum_out=small[:, col:col + 1]),
                     reads=[("rr", k, nb) for nb in range(4)], writes=["sqj", ("st", col)])
                B.op("act", R.activation(out=small[:, col + 1:col + 2], in_=small[:, col:col + 1], func=AF.Ln,
                                                   scale=1.0 / D, bias=epsb),
                     reads=[("st", col)], writes=[("st", col + 1)])
                B.op("act", R.activation(out=small[:, col + 2:col + 3], in_=small[:, col + 1:col + 2], func=AF.Exp,
                                                   scale=-0.5),
                     reads=[("st", col + 1)], writes=[("st", col + 2)])
                B.op("dve", R.scalar_tensor_tensor(out=ot[k], in0=rr[k], scalar=small[:, col + 2:col + 3], in1=fgb,
                                                             op0=ALU.mult, op1=ALU.mult),
                     reads=[("rr", k, nb) for nb in range(4)] + [("st", col + 2), "fgb"], writes=[("ot", k)])
                B.dma("pool", ("ot", k), out_own[(4 * i + tt) * 128:(4 * i + tt + 1) * 128, :], ot[k], reads=[("ot", k)])
        B.barrier()

        def replay(e, stream):
            for ent in stream:
                if ent[0] == "wait":
                    e.wait_ge(ent[1], ent[2])
                else:
                    _, name, args, kwargs, sem, inc = ent
                    getattr(e, name)(*args, **kwargs).then_inc(sem, inc)

        with nc.Block() as block:
            @block.tensor
            def _(e):
                replay(e, B.streams["pe"])

            @block.scalar
            def _(e):
                replay(e, B.streams["act"])

            @block.vector
            def _(e):
                replay(e, B.streams["dve"])

            @block.gpsimd
            def _(e):
                replay(e, B.streams["pool"])

            @block.sync
            def _(e):
                replay(e, B.streams["sp"])
        print("instr counts", {k: len(v) for k, v in B.streams.items()}, "sems", len(B.dsem))
    return nc


def _t5_bucket_np(rel):
    try:
        import jax
        import jax.numpy as jnp
        with jax.default_device(jax.devices("cpu")[0]):
            rel_j = jnp.asarray(rel, dtype=jnp.int32)
            nb = 16
            max_exact = 8
            side = jnp.where(rel_j > 0, nb, 0)
            n = jnp.abs(rel_j)
            nf = jnp.maximum(n, 1).astype(jnp.float32)
            large = max_exact + (jnp.log(nf / max_exact) / math.log(128 / max_exact) * (nb - max_exact)).astype(jnp.int32)
            large = jnp.minimum(large, nb - 1)
            return np.asarray(side + jnp.where(n < max_exact, n, large)).astype(np.int64)
    except Exception:
        rel = np.asarray(rel, dtype=np.int64)
        side = np.where(rel > 0, 16, 0)
        n = np.abs(rel)
        nf = np.maximum(n, 1).astype(np.float32)
        large = 8 + (np.log(nf / np.float32(8)) / np.float32(math.log(16.0)) * np.float32(8)).astype(np.int32)
        large = np.minimum(large, 15)
        return side + np.where(n < 8, n, large)


_PROG_CACHE = {}


def kernel(x, norm_g, w_in, sgu_ln_g, sgu_ln_b, sgu_w, sgu_b, lambda_q1, lambda_k1,
           lambda_q2, lambda_k2, subln_g, rel_bias, w_out, final_g):
    x = np.asarray(x, dtype=np.float32)
    Bn, S, _ = x.shape
    assert Bn == 2 and S % 2048 == 0
    NSLOT = S // 2048
    NT = 16 * NSLOT
    bf = ml_dtypes.bfloat16

    u = np.arange(NR)
    bucket = _t5_bucket_np(511 - u)
    onehot = np.zeros((32, NR), np.float32)
    onehot[bucket, u] = 1.0
    kk = np.arange(128)[:, None]
    qq = np.arange(512)[None, :]
    masks = np.zeros((128, 4, 512), np.float32)
    for r in range(4):
        allowed = ((128 * r + kk) // 64) <= (qq // 64)
        masks[:, r, :] = np.where(allowed, 0.0, NEG)
    ss_ = np.arange(128)[:, None]
    tt_ = np.arange(128)[None, :]
    sgumask = ((ss_ // 64) <= (tt_ // 64)).astype(np.float32)
    ident = np.eye(128, dtype=np.float32)
    rev = np.ascontiguousarray(ident[::-1])
    lam_in = np.stack([np.asarray(a, np.float32).reshape(64) for a in (lambda_q1, lambda_k1, lambda_q2, lambda_k2)])

    common = {
        "w_in": np.ascontiguousarray(np.asarray(w_in, np.float32).reshape(D, DIN)),
        "w_out": np.ascontiguousarray(np.asarray(w_out, np.float32).reshape(D, D)),
        "norm_g": np.asarray(norm_g, np.float32).reshape(D),
        "final_g": np.asarray(final_g, np.float32).reshape(D),
        "sgu_ln_g": np.asarray(sgu_ln_g, np.float32).reshape(1024),
        "sgu_ln_b": np.asarray(sgu_ln_b, np.float32).reshape(1024),
        "sgu_w": np.ascontiguousarray(np.asarray(sgu_w, np.float32).reshape(8, 128, 128)),
        "sgu_b": np.ascontiguousarray(np.asarray(sgu_b, np.float32).reshape(8, 128)),
        "lam_in": lam_in,
        "subln_g": np.asarray(subln_g, np.float32).reshape(128),
        "rel_bias": np.ascontiguousarray(np.asarray(rel_bias, np.float32).reshape(32, 8)),
        "c_ident": ident.astype(bf),
        "c_identf": ident,
        "c_rev": rev,
        "c_onehot": onehot,
        "c_masks": np.ascontiguousarray(masks.reshape(128, 4 * 512)),
        "c_sgumask": sgumask,
    }
    in_maps = []
    for c in range(8):
        b, j = divmod(c, 4)
        npad = 12 - 4 * j
        xvirt = np.zeros((NT * 128, D), np.float32)
        xvirt[npad * 128:] = x[b, :(NT - npad) * 128]
        kvalid = np.zeros((128, NT), np.float32)
        kvalid[:, npad:] = 1.0
        m = dict(common)
        m["xv"] = xvirt
        m["c_kvalid"] = kvalid.astype(bf)
        in_maps.append(m)

    if NSLOT not in _PROG_CACHE:
        _PROG_CACHE[NSLOT] = build_program(NSLOT)
    nc = _PROG_CACHE[NSLOT]
    res = run_bass_kernel_spmd(nc, in_maps, core_ids=list(range(8)))
    out = np.empty((Bn, S, D), np.float32)
    for c in range(8):
        b, j = divmod(c, 4)
        o = np.asarray(res.results[c]["out_own"], np.float32)
        for i in range(NSLOT):
            q0 = (4 * i + j) * 512
            out[b, q0:q0 + 512] = o[i * 512:(i + 1) * 512]
    return out
```

```python
import math
from contextlib import ExitStack

import numpy as np
import ml_dtypes

import concourse.bass as bass
import concourse.mybir as mybir
from concourse.bass_utils import run_bass_kernel_spmd

F32 = mybir.dt.float32
BF16 = mybir.dt.bfloat16
AF = mybir.ActivationFunctionType
ALU = mybir.AluOpType

D = 2048
DIN = 7168
NH = 8
NR = 1151
EPS = 1e-6
NEG = -30000.0
ENG = ("pe", "act", "dve", "pool", "sp")
NDS = 90


class _Rec:
    def __getattr__(self, name):
        return lambda *a, **k: (name, a, k)


R = _Rec()


class Builder:
    def __init__(self, nc, stack):
        self.nc = nc
        self.streams = {e: [] for e in ENG}
        self.esem = {e: stack.enter_context(nc.semaphore("s_" + e)) for e in ENG}
        self.cnt = {e: 0 for e in ENG}
        self.pool = [stack.enter_context(nc.semaphore("d%d" % i)) for i in range(NDS)]
        self.dsem = {}
        self.waited = {e: {} for e in ENG}
        self.res = {}

    def _semof(self, key):
        if isinstance(key, str) and key in self.esem:
            return self.esem[key]
        return self.dsem[key][0]

    def _wait(self, eng, ev):
        key, val = ev
        if key == "pe" and eng == "pe":
            return
        w = self.waited[eng]
        if w.get(key, 0) >= val:
            return
        w[key] = val
        sem = self._semof(key)
        self.streams[eng].append(("wait", sem, val))

    def _deps(self, eng, reads, writes):
        for r in reads:
            st = self.res.get(r)
            if st and st[0] is not None:
                self._wait(eng, st[0])
        for w in writes:
            st = self.res.get(w)
            if st:
                if st[0] is not None:
                    self._wait(eng, st[0])
                for k, v in st[1].items():
                    self._wait(eng, (k, v))

    def _update(self, ev, reads, writes):
        for r in reads:
            st = self.res.setdefault(r, [None, {}])
            if st[1].get(ev[0], 0) < ev[1]:
                st[1][ev[0]] = ev[1]
        for w in writes:
            self.res[w] = [ev, {}]

    def op(self, eng, fn, reads=(), writes=()):
        self._deps(eng, reads, writes)
        self.cnt[eng] += 1
        ev = (eng, self.cnt[eng])
        sem = self.esem[eng]
        name, args, kwargs = fn
        self.streams[eng].append(("op", name, args, kwargs, sem, 1))
        self._update(ev, reads, writes)
        return ev

    def dma(self, q, key, out, in_, reads=(), writes=()):
        self._deps(q, reads, writes)
        if key not in self.dsem:
            self.dsem[key] = [self.pool.pop(), 0]
        d = self.dsem[key]
        d[1] += 16
        ev = (key, d[1])
        sem = d[0]
        self.streams[q].append(("op", "dma_start", (), dict(out=out, in_=in_), sem, 16))
        self._update(ev, reads, writes)
        return ev

    def barrier(self):
        evs = [(e, self.cnt[e]) for e in ENG if self.cnt[e] > 0]
        evs += [(k, d[1]) for k, d in self.dsem.items() if d[1] > 0]
        for eng in ENG:
            for ev in evs:
                self._wait(eng, ev)
        self.res = {}


class Mem:
    def __init__(self, big, cap):
        self.big = big
        self.cap = cap
        self.ptr = 0

    def alloc(self, free_shape, dtype):
        esz = 4 if dtype == F32 else 2
        n = 1
        for s in free_shape:
            n *= s
        nb = n * esz
        off = (self.ptr + 63) // 64 * 64
        assert off + nb <= self.cap, ("SBUF overflow", off, nb, self.cap)
        self.ptr = off + nb
        v = self.big[:, off // 2:(off + nb) // 2]
        if dtype == F32:
            v = v.bitcast(F32)
        if len(free_shape) == 2:
            v = v.rearrange("p (a b) -> p a b", a=free_shape[0])
        elif len(free_shape) == 3:
            v = v.rearrange("p (a b c) -> p a b c", a=free_shape[0], b=free_shape[1])
        return v


def build_program(NSLOT):
    NT = 16 * NSLOT
    NG = 4 * NSLOT
    NOWN = NSLOT * 512

    nc = bass.Bass("TRN2", target_bir_lowering=False)

    def din(name, shape, dt=F32):
        return nc.dram_tensor(name, list(shape), dt, kind="ExternalInput").ap()

    def dscr(name, shape, dt):
        return nc.dram_tensor(name, list(shape), dt, kind="Internal").ap()

    xv = din("xv", [NT * 128, D])
    w_in = din("w_in", [D, DIN])
    w_out = din("w_out", [D, D])
    norm_g = din("norm_g", [D])
    final_g = din("final_g", [D])
    ln_g = din("sgu_ln_g", [1024])
    ln_b = din("sgu_ln_b", [1024])
    sgu_w = din("sgu_w", [8, 128, 128])
    sgu_b = din("sgu_b", [8, 128])
    lam_in = din("lam_in", [4, 64])
    subln_g = din("subln_g", [128])
    rel_bias = din("rel_bias", [32, 8])
    c_ident = din("c_ident", [128, 128], BF16)
    c_identf = din("c_identf", [128, 128])
    c_rev = din("c_rev", [128, 128])
    c_onehot = din("c_onehot", [32, NR])
    c_masks = din("c_masks", [128, 4 * 512])
    c_sgumask = din("c_sgumask", [128, 128])
    c_kvalid = din("c_kvalid", [128, NT], BF16)
    out_own = nc.dram_tensor("out_own", [NOWN, D], F32, kind="ExternalOutput").ap()

    KTd = dscr("KTd", [NH, 128, NT * 128], BF16)
    Vd = dscr("Vd", [NH, 128, NT, 128], BF16)
    hTd = dscr("hTd", [NSLOT, 128, 16, 512], BF16)
    QTd = dscr("QTd", [NH, 2, 128, NOWN], BF16)
    GTd = dscr("GTd", [NH, 128, NOWN], F32)
    YTd = dscr("YTd", [16, 128, NOWN], BF16)
    Gd = dscr("Gd", [NH, NR], F32)
    BMd = dscr("BMd", [NH, 128, 5 * 512], F32)

    def bcast(ap1d, n, offset=0):
        return bass.AP(ap1d.tensor, offset, [[0, 128], [1, n]])

    CAP = 210944
    with ExitStack() as stack:
        big = stack.enter_context(nc.sbuf_tensor("big", [128, CAP // 2], BF16))
        ps = stack.enter_context(nc.psum_tensor("ps", [128, 8, 512], F32))
        ps2 = ps.rearrange("p b n -> p (b n)")
        B = Builder(nc, stack)
        mem = Mem(big, CAP)

        def psb16(bank, nbanks=1):
            return ps2[:, bank * 512:(bank + nbanks) * 512].bitcast(BF16)

        ident = mem.alloc([128], BF16)
        gbc = mem.alloc([D], F32)
        epsb = mem.alloc([1], F32)
        chb = mem.alloc([8], F32)
        lamb = mem.alloc([1], F32)
        sgl = mem.alloc([1], F32)
        kval = mem.alloc([NT], BF16)
        small = mem.alloc([64], F32)
        pmark = mem.ptr

        rb = mem.alloc([8], F32)
        oh = mem.alloc([NR], F32)
        rev = mem.alloc([128], F32)
        msk = mem.alloc([4, 512], F32)
        lv = mem.alloc([4, 64], F32)
        lj = mem.alloc([64], F32)
        gsb = mem.alloc([NR], F32)
        Xh = [mem.alloc([512], F32) for _ in range(2)]
        bmst = [mem.alloc([5, 512], F32) for _ in range(2)]

        B.dma("sp", "pl", ident, c_ident, writes=["ident"])
        B.dma("sp", "pl", gbc, bcast(norm_g, D), writes=["gbc"])
        B.dma("sp", "pl", chb, bcast(rel_bias, 8, 15 * 8), writes=["chb"])
        B.dma("sp", "pl", sgl, bass.AP(subln_g.tensor, 0, [[1, 128], [1, 1]]), writes=["sgl"])
        B.dma("sp", "pl", kval, c_kvalid, writes=["kval"])
        B.dma("sp", "pl", rb[0:32, :], rel_bias, writes=["rb"])
        B.dma("sp", "pl", oh[0:32, :], c_onehot, writes=["oh"])
        B.dma("sp", "pl", rev, c_rev, writes=["rev"])
        B.dma("sp", "pl", msk.rearrange("p a b -> p (a b)"), c_masks, writes=["msk"])
        for k in range(4):
            B.dma("sp", "pl", lv[:, k, :], bcast(lam_in, 64, k * 64), writes=[("lv", k)])
        B.barrier()

        B.op("dve", R.memset(epsb, EPS), writes=["epsb"])
        B.op("dve", R.tensor_scalar(out=sgl, in0=sgl, scalar1=0.4, scalar2=None, op0=ALU.mult),
             reads=["sgl"], writes=["sgl"])
        B.op("dve", R.scalar_tensor_tensor(out=lj, in0=lv[:, 0, :], scalar=1.0, in1=lv[:, 1, :],
                                                     op0=ALU.mult, op1=ALU.mult, accum_out=small[:, 0:1]),
             writes=["lj", "s0"])
        B.op("dve", R.scalar_tensor_tensor(out=lj, in0=lv[:, 2, :], scalar=1.0, in1=lv[:, 3, :],
                                                     op0=ALU.mult, op1=ALU.mult, accum_out=small[:, 1:2]),
             reads=[], writes=["lj", "s1"])
        B.op("act", R.activation(out=small[:, 2:4], in_=small[:, 0:2], func=AF.Exp),
             reads=["s0", "s1"], writes=["s23"])
        B.op("dve", R.tensor_tensor(out=small[:, 4:5], in0=small[:, 2:3], in1=small[:, 3:4],
                                              op=ALU.subtract), reads=["s23"], writes=["s4"])
        B.op("dve", R.tensor_scalar(out=lamb, in0=small[:, 4:5], scalar1=0.2, scalar2=None,
                                              op0=ALU.add), reads=["s4"], writes=["lamb"])
        for ci, (c0, n) in enumerate([(0, 512), (512, 512), (1024, NR - 1024)]):
            B.op("pe", R.matmul(ps[0:8, ci, 0:n], lhsT=rb[0:32, 0:8],
                                                              rhs=oh[0:32, c0:c0 + n], start=True, stop=True),
                 writes=[("ps", ci)])
            B.op("act", R.activation(out=gsb[0:8, c0:c0 + n], in_=ps[0:8, ci, 0:n],
                                                                  func=AF.Copy),
                 reads=[("ps", ci)], writes=[("gsb", ci)])
        B.dma("sp", "gd", Gd, gsb[0:8, :], reads=[("gsb", 0), ("gsb", 1), ("gsb", 2)], writes=["Gd"])
        cnt = 0
        for h in range(NH):
            bs = h % 2
            for r in range(-1, 4):
                xs = cnt % 2
                bk = 4 + cnt % 4
                cnt += 1
                src = bass.AP(Gd.tensor, h * NR + 128 * (3 - r), [[1, 128], [1, 512]])
                B.dma("sp", ("xh", xs), Xh[xs], src, reads=["Gd"], writes=[("xh", xs)])
                B.op("pe", R.matmul(ps[:, bk, :], lhsT=rev, rhs=Xh[xs], start=True, stop=True),
                     reads=[("xh", xs)], writes=[("ps", bk)])
                if r >= 0:
                    B.op("dve", R.tensor_tensor(out=bmst[bs][:, r + 1, :], in0=ps[:, bk, :],
                                                                            in1=msk[:, r, :], op=ALU.add),
                         reads=[("ps", bk)], writes=[("bmst", bs, r)])
                else:
                    B.op("dve", R.tensor_copy(out=bmst[bs][:, 0, :], in_=ps[:, bk, :]),
                         reads=[("ps", bk)], writes=[("bmst", bs, r)])
            B.dma("pool", ("bmst", bs), BMd[h], bmst[bs].rearrange("p a b -> p (a b)"),
                  reads=[("bmst", bs, r) for r in range(-1, 4)])
        B.barrier()

        mem.ptr = pmark
        Wkv = mem.alloc([16, 2048], BF16)
        xt = [mem.alloc([D], F32) for _ in range(3)]
        sqj = mem.alloc([D], BF16)
        hb = [mem.alloc([D], BF16) for _ in range(2)]
        hT = [mem.alloc([16, 512], BF16) for _ in range(2)]
        kst = [mem.alloc([8, 512], BF16) for _ in range(2)]
        vst = [mem.alloc([4, 1024], BF16) for _ in range(2)]

        for dc in range(16):
            B.dma("pool", "w", Wkv[:, dc, :], w_in[dc * 128:(dc + 1) * 128, 4096:6144], writes=["W"])

        def hT_keys(gs):
            return [("hT", gs, t, hf) for t in range(4) for hf in range(2)]

        def rms_stats(src, skey, col):
            B.op("act", R.activation(out=sqj, in_=src, func=AF.Square, accum_out=small[:, col:col + 1]),
                 reads=[skey], writes=["sqj", ("st", col)])
            B.op("act", R.activation(out=small[:, col + 1:col + 2], in_=small[:, col:col + 1], func=AF.Ln,
                                               scale=1.0 / D, bias=epsb),
                 reads=[("st", col)], writes=[("st", col + 1)])
            B.op("act", R.activation(out=small[:, col + 2:col + 3], in_=small[:, col + 1:col + 2],
                                               func=AF.Exp, scale=-0.5),
                 reads=[("st", col + 1)], writes=[("st", col + 2)])
            return ("st", col + 2), small[:, col + 2:col + 3]

        def frontend(T):
            g, tt = divmod(T, 4)
            xs, hs, ts, gs = T % 3, T % 2, T % 2, g % 2
            B.dma("sp", ("xt", xs), xt[xs], xv[T * 128:(T + 1) * 128, :], writes=[("xt", xs)])
            rkey, rstd = rms_stats(xt[xs], ("xt", xs), 8 + 4 * (T % 2))
            B.op("dve", R.scalar_tensor_tensor(out=hb[hs], in0=xt[xs], scalar=rstd, in1=gbc,
                                                         op0=ALU.mult, op1=ALU.mult),
                 reads=[("xt", xs), rkey], writes=[("hb", hs)])
            tpv = psb16(2 * ts, 2)
            for dc in range(16):
                B.op("pe", R.transpose(tpv[:, dc * 128:(dc + 1) * 128],
                                                        hb[hs][:, dc * 128:(dc + 1) * 128], ident),
                     reads=[("hb", hs)], writes=[("ps", 2 * ts + dc // 8)])
            B.op("act", R.activation(out=hT[gs][:, 0:8, tt * 128:(tt + 1) * 128],
                                               in_=tpv[:, 0:1024].rearrange("p (c n) -> p c n", c=8), func=AF.Copy),
                 reads=[("ps", 2 * ts)], writes=[("hT", gs, tt, 0)])
            B.op("dve", R.tensor_copy(out=hT[gs][:, 8:16, tt * 128:(tt + 1) * 128],
                                                in_=tpv[:, 1024:2048].rearrange("p (c n) -> p c n", c=8)),
                 reads=[("ps", 2 * ts + 1)], writes=[("hT", gs, tt, 1)])

        chain_no = [0]

        def backend_chains(g):
            gs, ks, vs = g % 2, g % 2, g % 2
            chains = []

            def kchain(fc):
                bk = 4 + chain_no[0] % 4
                chain_no[0] += 1
                for dc in range(16):
                    B.op("pe", R.matmul(ps[:, bk, :], lhsT=Wkv[:, dc, fc * 128:(fc + 1) * 128],
                                                         rhs=hT[gs][:, dc, :], start=(dc == 0), stop=(dc == 15)),
                         reads=hT_keys(gs) + ["W"], writes=[("ps", bk)])
                if fc % 2 == 0:
                    B.op("act", R.activation(out=kst[ks][:, fc, :], in_=ps[:, bk, :], func=AF.Copy),
                         reads=[("ps", bk)], writes=[("kst", ks, fc)])
                else:
                    B.op("dve", R.tensor_copy(out=kst[ks][:, fc, :], in_=ps[:, bk, :]),
                         reads=[("ps", bk)], writes=[("kst", ks, fc)])
                if fc == 7:
                    B.dma("pool", ("kst", ks), KTd[:, :, g * 512:(g + 1) * 512].rearrange("h p t -> p h t"), kst[ks],
                          reads=[("kst", ks, f) for f in range(8)])

            def vchain(tt, hf):
                bk = 4 + chain_no[0] % 4
                chain_no[0] += 1
                for dc in range(16):
                    B.op("pe", R.matmul(ps[:, bk, :], lhsT=hT[gs][:, dc, tt * 128:(tt + 1) * 128],
                                                         rhs=Wkv[:, dc, 1024 + hf * 512:1024 + (hf + 1) * 512],
                                                         start=(dc == 0), stop=(dc == 15)),
                         reads=[("hT", gs, tt, 0), ("hT", gs, tt, 1), "W"], writes=[("ps", bk)])
                if hf == 0:
                    B.op("act", R.activation(out=vst[vs][:, tt, 0:512], in_=ps[:, bk, :], func=AF.Copy),
                         reads=[("ps", bk)], writes=[("vst", vs, tt, hf)])
                else:
                    B.op("dve", R.tensor_copy(out=vst[vs][:, tt, 512:1024], in_=ps[:, bk, :]),
                         reads=[("ps", bk)], writes=[("vst", vs, tt, hf)])
                if tt == 3 and hf == 1:
                    for t4 in range(4):
                        B.dma("pool", ("vst", vs),
                              Vd[:, :, 4 * g + t4, :].rearrange("h p d -> p h d"),
                              vst[vs][:, t4, :].rearrange("p (h d) -> p h d", h=8),
                              reads=[("vst", vs, t, f) for t in range(4) for f in range(2)])
                    if g % 4 == 3:
                        B.dma("pool", ("hTd", gs), hTd[g // 4], hT[gs], reads=hT_keys(gs))

            for fc in range(8):
                chains.append(lambda fc=fc: kchain(fc))
            for tt in range(4):
                for hf in range(2):
                    chains.append(lambda tt=tt, hf=hf: vchain(tt, hf))
            return chains

        for T in range(4):
            frontend(T)
        for g in range(NG):
            chains = backend_chains(g)
            for ci, ch in enumerate(chains):
                ch()
                if ci % 4 == 3 and g + 1 < NG:
                    frontend(4 * (g + 1) + ci // 4)
        B.barrier()

        mem.ptr = pmark
        Wqg = mem.alloc([16, 2048], BF16)
        hT = [mem.alloc([16, 512], BF16) for _ in range(2)]
        qst = [mem.alloc([2, 8, 512], BF16) for _ in range(2)]
        gst = [mem.alloc([512], F32) for _ in range(2)]
        th = [mem.alloc([512], F32) for _ in range(2)]
        for dc in range(16):
            B.dma("pool", "w", Wqg[:, dc, 0:1024], w_in[dc * 128:(dc + 1) * 128, 3072:4096], writes=["W"])
            B.dma("pool", "w", Wqg[:, dc, 1024:2048], w_in[dc * 128:(dc + 1) * 128, 6144:7168], writes=["W"])
        for s in range(2):
            B.op("dve", R.memset(qst[s].rearrange("p a b c -> p (a b c)"), 0.0),
                 writes=[("qst", s, m, f) for m in range(2) for f in range(8)])
        cno = 0
        tno = 0
        for i in range(NSLOT):
            hs = i % 2
            qs = i % 2
            B.dma("sp", ("hTl", hs), hT[hs], hTd[i], writes=[("hT", hs)])
            for fc in range(8):
                bk = cno % 4
                cno += 1
                for dc in range(16):
                    B.op("pe", R.matmul(ps[:, bk, :], lhsT=Wqg[:, dc, fc * 128:(fc + 1) * 128],
                                                                      rhs=hT[hs][:, dc, :], start=(dc == 0), stop=(dc == 15)),
                         reads=[("hT", hs), "W"], writes=[("ps", bk)])
                B.op("act", R.activation(out=qst[qs][0:64, 0, fc, :], in_=ps[0:64, bk, :], func=AF.Copy),
                     reads=[("ps", bk)], writes=[("qst", qs, 0, fc)])
                B.op("dve", R.tensor_copy(out=qst[qs][64:128, 1, fc, :], in_=ps[64:128, bk, :]),
                     reads=[("ps", bk)], writes=[("qst", qs, 1, fc)])
            for m2 in range(2):
                B.dma("pool", ("qst", qs), QTd[:, m2, :, i * 512:(i + 1) * 512].rearrange("h p t -> p h t"),
                      qst[qs][:, m2, :, :], reads=[("qst", qs, m, f) for m in range(2) for f in range(8)])
            for hh in range(8):
                gsl = hh % 2
                bk = 4 + cno % 4
                cno += 1
                tsl = tno % 2
                tno += 1
                for dc in range(16):
                    B.op("pe", R.matmul(ps[:, bk, :], lhsT=Wqg[:, dc, 1024 + hh * 128:1024 + (hh + 1) * 128],
                                        rhs=hT[hs][:, dc, :], start=(dc == 0), stop=(dc == 15)),
                         reads=[("hT", hs), "W"], writes=[("ps", bk)])
                B.op("act", R.activation(out=th[tsl], in_=ps[:, bk, :], func=AF.Tanh, scale=0.5),
                     reads=[("ps", bk)], writes=[("th", tsl)])
                B.op("dve", R.scalar_tensor_tensor(out=gst[gsl], in0=th[tsl], scalar=1.0, in1=ps[:, bk, :],
                                                   op0=ALU.add, op1=ALU.mult),
                     reads=[("th", tsl), ("ps", bk)], writes=[("gst", gsl)])
                B.dma("pool", ("gst", gsl), GTd[hh, :, i * 512:(i + 1) * 512], gst[gsl], reads=[("gst", gsl)])
        B.barrier()

        mem.ptr = pmark
        Wsg = mem.alloc([16, 3072], BF16)
        hT = [mem.alloc([16, 512], BF16) for _ in range(2)]
        wnat = mem.alloc([8, 128], F32)
        identf = mem.alloc([128], F32)
        smk = mem.alloc([128], F32)
        bnat = mem.alloc([128], F32)
        wT = mem.alloc([8, 128], BF16)
        bT = mem.alloc([8], F32)
        lngb = mem.alloc([1024], F32)
        lnbb = mem.alloc([1024], F32)
        gu = mem.alloc([1024], F32)
        gv = mem.alloc([1024], F32)
        t1 = mem.alloc([1024], F32)
        vn = mem.alloc([1024], BF16)
        tha = mem.alloc([1024], F32)
        sa2 = mem.alloc([1024], F32)
        aa = mem.alloc([1024], F32)
        aout = mem.alloc([1024], BF16)
        bst = mem.alloc([2, 6], F32)
        yst = [mem.alloc([8, 512], BF16) for _ in range(2)]
        for dc in range(16):
            B.dma("pool", "w", Wsg[:, dc, :], w_in[dc * 128:(dc + 1) * 128, 0:3072], writes=["W"])
        B.dma("sp", "pl", wnat, sgu_w.rearrange("g t s -> t g s"), writes=["wnat"])
        B.dma("sp", "pl", identf, c_identf, writes=["identf"])
        B.dma("sp", "pl", smk, c_sgumask, writes=["smk"])
        B.dma("sp", "pl", bnat[0:8, :], sgu_b, writes=["bnat"])
        B.dma("sp", "pl", lngb, bcast(ln_g, 1024), writes=["lngb"])
        B.dma("sp", "pl", lnbb, bcast(ln_b, 1024), writes=["lnbb"])
        B.barrier()
        for g8 in range(8):
            bk = g8 % 4
            B.op("pe", R.transpose(ps[:, bk, 0:128], wnat[:, g8, :], identf),
                 reads=["wnat", "identf"], writes=[("ps", bk)])
            B.op("dve", R.tensor_tensor(out=wT[:, g8, :], in0=ps[:, bk, 0:128], in1=smk, op=ALU.mult),
                 reads=[("ps", bk), "smk"], writes=[("wT", g8)])
        B.op("pe", R.transpose(ps[:, 4, 0:8], bnat[0:8, :], identf[0:8, 0:8]),
             reads=["bnat", "identf"], writes=[("ps", 4)])
        B.op("dve", R.tensor_copy(out=bT, in_=ps[:, 4, 0:8]), reads=[("ps", 4)], writes=["bT"])
        B.barrier()

        for i in range(NSLOT):
            hs = i % 2
            ys = i % 2
            B.dma("sp", ("hTl", hs), hT[hs], hTd[i], writes=[("hT", hs)])
            for tt in range(4):
                for cb in range(6):
                    for dc in range(16):
                        B.op("pe", R.matmul(
                            ps[:, cb, :], lhsT=hT[hs][:, dc, tt * 128:(tt + 1) * 128],
                            rhs=Wsg[:, dc, cb * 512:(cb + 1) * 512], start=(dc == 0), stop=(dc == 15)),
                            reads=[("hT", hs), "W"], writes=[("ps", cb)])
                B.op("act", R.activation(out=gu.rearrange("p (a b) -> p a b", a=2), in_=ps[:, 0:2, :],
                                                   func=AF.Gelu_apprx_tanh),
                     reads=[("ps", 0), ("ps", 1)], writes=["gu"])
                B.op("act", R.activation(out=gv.rearrange("p (a b) -> p a b", a=2), in_=ps[:, 2:4, :],
                                                   func=AF.Gelu_apprx_tanh),
                     reads=[("ps", 2), ("ps", 3)], writes=["gv"])
                B.op("act", R.activation(out=tha.rearrange("p (a b) -> p a b", a=2), in_=ps[:, 4:6, :],
                                                   func=AF.Tanh, scale=0.5),
                     reads=[("ps", 4), ("ps", 5)], writes=["tha"])
                B.op("dve", R.bn_stats(out=bst[:, 0, :], in_=gv[:, 0:512]), reads=["gv"], writes=["bst0"])
                B.op("dve", R.bn_stats(out=bst[:, 1, :], in_=gv[:, 512:1024]), reads=["gv"], writes=["bst1"])
                B.op("dve", R.bn_aggr(out=small[:, 16:18], in_=bst.rearrange("p a b -> p (a b)")),
                     reads=["bst0", "bst1"], writes=["mv"])
                B.op("act", R.activation(out=small[:, 18:19], in_=small[:, 17:18], func=AF.Ln, bias=epsb, scale=1.0),
                     reads=["mv"], writes=["lnv"])
                B.op("act", R.activation(out=small[:, 19:20], in_=small[:, 18:19], func=AF.Exp, scale=-0.5),
                     reads=["lnv"], writes=["lrs"])
                B.op("dve", R.tensor_scalar(out=t1, in0=gv, scalar1=small[:, 16:17], scalar2=small[:, 19:20],
                                                      op0=ALU.subtract, op1=ALU.mult),
                     reads=["gv", "mv", "lrs"], writes=["t1"])
                B.op("dve", R.tensor_tensor(out=t1, in0=t1, in1=lngb, op=ALU.mult), reads=["t1"], writes=["t1"])
                B.op("dve", R.tensor_tensor(out=vn, in0=t1, in1=lnbb, op=ALU.add), reads=["t1"], writes=["vn"])
                for g8 in range(8):
                    B.op("pe", R.matmul(ps[:, 6 + g8 // 4, (g8 % 4) * 128:(g8 % 4 + 1) * 128],
                                                         lhsT=wT[:, g8, :], rhs=vn[:, g8 * 128:(g8 + 1) * 128],
                                                         start=True, stop=True),
                         reads=["vn"], writes=[("ps", 6 + g8 // 4)])
                for hf in range(2):
                    B.op("dve", R.scalar_tensor_tensor(
                        out=sa2[:, hf * 512:(hf + 1) * 512], in0=tha[:, hf * 512:(hf + 1) * 512], scalar=1.0,
                        in1=ps[:, 4 + hf, :], op0=ALU.add, op1=ALU.mult),
                        reads=["tha", ("ps", 4 + hf)], writes=[("sa2", hf)])
                for g8 in range(8):
                    B.op("dve", R.scalar_tensor_tensor(
                        out=aa[:, g8 * 128:(g8 + 1) * 128], in0=ps[:, 6 + g8 // 4, (g8 % 4) * 128:(g8 % 4 + 1) * 128],
                        scalar=bT[:, g8:g8 + 1], in1=gu[:, g8 * 128:(g8 + 1) * 128], op0=ALU.add, op1=ALU.mult),
                        reads=[("ps", 6 + g8 // 4), "gu"], writes=[("aa", g8)])
                B.op("dve", R.scalar_tensor_tensor(out=aout, in0=aa, scalar=0.5, in1=sa2, op0=ALU.mult, op1=ALU.mult),
                     reads=[("aa", g8) for g8 in range(8)] + [("sa2", 0), ("sa2", 1)], writes=["aout"])
                tpb = psb16(0, 1)
                for ec in range(8):
                    B.op("pe", R.transpose(tpb[:, ec * 128:(ec + 1) * 128], aout[:, ec * 128:(ec + 1) * 128], ident),
                         reads=["aout"], writes=[("ps", 0)])
                B.op("act", R.activation(out=yst[ys][:, :, tt * 128:(tt + 1) * 128],
                                                          in_=tpb.rearrange("p (c n) -> p c n", c=8), func=AF.Copy),
                     reads=[("ps", 0)], writes=[("yst", ys, tt)])
            B.dma("pool", ("yst", ys), YTd[0:8, :, i * 512:(i + 1) * 512].rearrange("c p t -> p c t"), yst[ys],
                  reads=[("yst", ys, t) for t in range(4)])
        B.barrier()

        mem.ptr = pmark
        kT = [mem.alloc([NT * 128], BF16) for _ in range(2)]
        vA = [mem.alloc([NT, 128], BF16) for _ in range(2)]
        bm = mem.alloc([5, 512], F32)
        qT = [mem.alloc([2, 512], BF16) for _ in range(2)]
        gtT = [mem.alloc([512], F32) for _ in range(2)]
        pt = [mem.alloc([2, 512], BF16) for _ in range(3)]
        tmpf = mem.alloc([2, 512], F32)
        rsd = [mem.alloc([2, 512], F32) for _ in range(2)]
        sh = mem.alloc([512], F32)
        zb = mem.alloc([2, 128], BF16)
        sel = mem.alloc([2, 128], F32)
        onesf = mem.alloc([128], F32)
        kvf = mem.alloc([NT], F32)
        nlam = mem.alloc([1], F32)
        sbs = mem.alloc([2, 512], F32)
        ea = mem.alloc([512], F32)
        eb = mem.alloc([512], F32)
        eo = mem.alloc([512], F32)
        ep = mem.alloc([512], F32)
        ystc = [mem.alloc([512], BF16) for _ in range(2)]

        B.op("dve", R.memset(onesf, 1.0), writes=["onesf"])
        B.op("dve", R.memset(zb.rearrange("p a b -> p (a b)"), 0.0), writes=["zb"])
        B.op("dve", R.memset(zb[:, 0, 0:64], 1.0), reads=["zb"], writes=["zb"])
        B.op("dve", R.memset(zb[:, 1, 64:128], 1.0), reads=["zb"], writes=["zb"])
        B.op("dve", R.memset(sel.rearrange("p a b -> p (a b)"), 0.0), writes=["sel"])
        B.op("dve", R.memset(sel[0:1, 0, :], 1.0), reads=["sel"], writes=["sel"])
        B.op("dve", R.memset(sel[64:65, 1, :], 1.0), reads=["sel"], writes=["sel"])
        B.op("dve", R.tensor_copy(out=kvf, in_=kval), reads=["kval"], writes=["kvf"])
        B.op("dve", R.tensor_scalar(out=nlam, in0=lamb, scalar1=-1.0, scalar2=None, op0=ALU.mult),
             reads=["lamb"], writes=["nlam"])

        def load_head(h):
            hs = h % 2
            B.dma("sp", ("kT", hs), kT[hs], KTd[h], writes=[("kT", hs)])
            B.dma("sp", ("vA", hs), vA[hs], Vd[h], writes=[("vA", hs)])

        pcount = [0]
        slotc = [0]
        pending = []

        def flat(t):
            return t.rearrange("p a b -> p (a b)")

        def epilogue_parts(h, i, sl, gs, ys):
            def part0():
                B.op("dve", R.tensor_copy(out=ea, in_=ps[:, 4, :]), reads=[("ps", 4)], writes=["ea"])
                B.op("dve", R.tensor_copy(out=eb, in_=ps[:, 5, :]), reads=[("ps", 5)], writes=["eb"])
                B.op("dve", R.tensor_copy(out=sh, in_=ps[:, 6, :]), reads=[("ps", 6)], writes=["sh"])
                for m in range(2):
                    B.op("pe", R.matmul(ps[:, 7, :], lhsT=onesf, rhs=rsd[sl][:, m, :], start=True, stop=False),
                         reads=[("rsd", sl), "onesf"], writes=[("ps", 7)])
                    B.op("pe", R.matmul(ps[:, 7, :], lhsT=sel[:, m, :], rhs=sh, start=False, stop=True),
                         reads=["sh", "sel"], writes=[("ps", 7)])
                    B.op("dve", R.tensor_scalar(out=sbs[:, m, :], in0=ps[:, 7, :], scalar1=2.0 ** -14, scalar2=None,
                                                op0=ALU.mult),
                         reads=[("ps", 7)], writes=[("sbs", m)])

            def part1():
                B.op("dve", R.tensor_tensor(out=ea, in0=ea, in1=sbs[:, 1, :], op=ALU.mult), reads=["ea", ("sbs", 1)], writes=["ea"])
                B.op("dve", R.tensor_tensor(out=eb, in0=eb, in1=sbs[:, 0, :], op=ALU.mult), reads=["eb", ("sbs", 0)], writes=["eb"])
                B.op("dve", R.scalar_tensor_tensor(out=eo, in0=eb, scalar=nlam, in1=ea, op0=ALU.mult, op1=ALU.add),
                     reads=["ea", "eb", "nlam"], writes=["eo"])
                B.op("dve", R.tensor_tensor(out=eb, in0=eo, in1=eo, op=ALU.mult), reads=["eo"], writes=["eb"])
                B.op("pe", R.matmul(ps[:, 7, :], lhsT=onesf, rhs=eb, start=True, stop=True),
                     reads=["eb", "onesf"], writes=[("ps", 7)])
                B.op("dve", R.tensor_tensor(out=ep, in0=sbs[:, 0, :], in1=sbs[:, 1, :], op=ALU.mult), reads=[("sbs", 0), ("sbs", 1)], writes=["ep"])
                B.op("dve", R.scalar_tensor_tensor(out=ep, in0=ep, scalar=EPS, in1=ep, op0=ALU.mult, op1=ALU.mult),
                     reads=["ep"], writes=["ep"])

            def part2():
                B.op("dve", R.scalar_tensor_tensor(out=ea, in0=ps[:, 7, :], scalar=1.0 / 128, in1=ep, op0=ALU.mult, op1=ALU.add),
                     reads=[("ps", 7), "ep"], writes=["ea"])
                B.op("act", R.activation(out=ea, in_=ea, func=AF.Ln), reads=["ea"], writes=["ea"])
                B.op("act", R.activation(out=ea, in_=ea, func=AF.Exp, scale=-0.5), reads=["ea"], writes=["ea"])

            def part3():
                B.op("dve", R.scalar_tensor_tensor(out=eo, in0=eo, scalar=sgl, in1=ea, op0=ALU.mult, op1=ALU.mult),
                     reads=["eo", "ea", "sgl"], writes=["eo"])
                B.op("dve", R.tensor_tensor(out=ystc[ys], in0=eo, in1=gtT[gs], op=ALU.mult),
                     reads=["eo", ("gtT", gs)], writes=[("ystc", ys)])
                B.dma("pool", ("ystc", ys), YTd[8 + h, :, i * 512:(i + 1) * 512], ystc[ys], reads=[("ystc", ys)])
            return [part0, part1, part2, part3]

        load_head(0)
        for h in range(NH):
            hs = h % 2
            if h + 1 < NH:
                load_head(h + 1)
            B.dma("sp", "bm", flat(bm), BMd[h], writes=["bm"])
            for i in range(NSLOT):
                sc = slotc[0]
                slotc[0] += 1
                qs = gs = sl = ys = sc % 2
                B.dma("sp", ("qT", qs), qT[qs], QTd[h, :, :, i * 512:(i + 1) * 512].rearrange("m p t -> p m t"),
                      writes=[("qT", qs)])
                B.dma("sp", ("gtT", gs), gtT[gs], GTd[h, :, i * 512:(i + 1) * 512], writes=[("gtT", gs)])
                n = 16 * (i + 1)

                def emit_S(v, pu):
                    b = pu % 2
                    for m in range(2):
                        B.op("pe", R.matmul(ps[:, 2 * b + m, :], lhsT=kT[hs][:, v * 128:(v + 1) * 128], rhs=qT[qs][:, m, :],
                                            start=True, stop=True),
                             reads=[("kT", hs), ("qT", qs)], writes=[("ps", 2 * b + m)])

                emit_S(0, pcount[0])
                pool_started = False
                for v in range(n):
                    pu = pcount[0]
                    pcount[0] += 1
                    b = pu % 2
                    s3 = pu % 3
                    if v + 1 < n:
                        emit_S(v + 1, pu + 1)
                    r = v - (n - 4)
                    if r < -1:
                        B.op("act", R.activation(out=pt[s3], in_=ps[:, 2 * b:2 * b + 2, :], func=AF.Exp, scale=0.125,
                                                 bias=chb[:, h:h + 1]),
                             reads=[("ps", 2 * b), ("ps", 2 * b + 1)], writes=[("pt", s3)])
                    else:
                        for m in range(2):
                            B.op("dve", R.scalar_tensor_tensor(out=tmpf[:, m, :], in0=ps[:, 2 * b + m, :], scalar=0.125,
                                                               in1=bm[:, r + 1, :], op0=ALU.mult, op1=ALU.add),
                                 reads=[("ps", 2 * b + m), "bm"], writes=[("tmpf", m)])
                        B.op("act", R.activation(out=pt[s3], in_=tmpf, func=AF.Exp),
                             reads=[("tmpf", 0), ("tmpf", 1)], writes=[("pt", s3)])
                    for m in range(2):
                        B.op("pe", R.matmul(ps[:, 4 + m, :], lhsT=vA[hs][:, v, :], rhs=pt[s3][:, m, :],
                                            start=(v == 0), stop=(v == n - 1)),
                             reads=[("pt", s3), ("vA", hs)], writes=[("ps", 4 + m)])
                    if v == 0:
                        B.op("dve", R.tensor_scalar(out=flat(rsd[sl]), in0=flat(pt[s3]), scalar1=kvf[:, 0:1], scalar2=None,
                                                    op0=ALU.mult),
                             reads=[("pt", s3), "kvf"], writes=[("rsd", sl)])
                    elif v < 12:
                        B.op("dve", R.scalar_tensor_tensor(out=flat(rsd[sl]), in0=flat(pt[s3]), scalar=kvf[:, v:v + 1],
                                                           in1=flat(rsd[sl]), op0=ALU.mult, op1=ALU.add),
                             reads=[("pt", s3), "kvf", ("rsd", sl)], writes=[("rsd", sl)])
                    elif v % 2 == 1:
                        for m in range(2):
                            B.op("pe", R.matmul(ps[:, 6, :], lhsT=zb[:, m, :], rhs=pt[s3][:, m, :],
                                                start=(not pool_started and m == 0), stop=(v == n - 1 and m == 1)),
                                 reads=[("pt", s3), "zb"], writes=[("ps", 6)])
                        pool_started = True
                    else:
                        B.op("dve", R.tensor_tensor(out=flat(rsd[sl]), in0=flat(rsd[sl]), in1=flat(pt[s3]), op=ALU.add),
                             reads=[("pt", s3), ("rsd", sl)], writes=[("rsd", sl)])
                    if pending and v in (2, 5, 8):
                        pending.pop(0)()
                while pending:
                    pending.pop(0)()
                parts = epilogue_parts(h, i, sl, gs, ys)
                parts[0]()
                pending.extend(parts[1:])
        while pending:
            pending.pop(0)()
        B.barrier()

        mem.ptr = pmark
        Wo = mem.alloc([16, 2048], BF16)
        fgb = mem.alloc([D], F32)
        yT = [mem.alloc([16, 512], BF16) for _ in range(2)]
        xo = [mem.alloc([D], F32) for _ in range(2)]
        rr = [mem.alloc([D], F32) for _ in range(2)]
        ot = [mem.alloc([D], F32) for _ in range(2)]
        sqj = mem.alloc([D], BF16)
        for dc in range(16):
            B.dma("pool", "w", Wo[:, dc, :], w_out[dc * 128:(dc + 1) * 128, :], writes=["W"])
        B.dma("sp", "pl", fgb, bcast(final_g, D), writes=["fgb"])
        tcount = 0
        for i in range(NSLOT):
            ysl = i % 2
            B.dma("sp", ("yT", ysl), yT[ysl], YTd[:, :, i * 512:(i + 1) * 512].rearrange("c p t -> p c t"),
                  writes=[("yT", ysl)])
            for tt in range(4):
                k = tcount % 2
                tcount += 1
                Tv = 16 * i + 12 + tt
                B.dma("sp", ("xo", k), xo[k], xv[Tv * 128:(Tv + 1) * 128, :], writes=[("xo", k)])
                for nb in range(4):
                    bk = 4 * k + nb
                    for ec in range(16):
                        B.op("pe", R.matmul(
                            ps[:, bk, :], lhsT=yT[ysl][:, ec, tt * 128:(tt + 1) * 128],
                            rhs=Wo[:, ec, nb * 512:(nb + 1) * 512], start=(ec == 0), stop=(ec == 15)),
                            reads=[("yT", ysl), "W"], writes=[("ps", bk)])
                    B.op("dve", R.tensor_tensor(out=rr[k][:, nb * 512:(nb + 1) * 512], in0=ps[:, bk, :],
                                                                      in1=xo[k][:, nb * 512:(nb + 1) * 512], op=ALU.add),
                         reads=[("ps", bk), ("xo", k)], writes=[("rr", k, nb)])
                col = 8 + 4 * k
                B.op("act", R.activation(out=sqj, in_=rr[k], func=AF.Square, accum_out=small[:, col:col + 1]),
                     reads=[("rr", k, nb) for nb in range(4)], writes=["sqj", ("st", col)])
                B.op("act", R.activation(out=small[:, col + 1:col + 2], in_=small[:, col:col + 1], func=AF.Ln,
                                                   scale=1.0 / D, bias=epsb),
                     reads=[("st", col)], writes=[("st", col + 1)])
                B.op("act", R.activation(out=small[:, col + 2:col + 3], in_=small[:, col + 1:col + 2], func=AF.Exp,
                                                   scale=-0.5),
                     reads=[("st", col + 1)], writes=[("st", col + 2)])
                B.op("dve", R.scalar_tensor_tensor(out=ot[k], in0=rr[k], scalar=small[:, col + 2:col + 3], in1=fgb,
                                                             op0=ALU.mult, op1=ALU.mult),
                     reads=[("rr", k, nb) for nb in range(4)] + [("st", col + 2), "fgb"], writes=[("ot", k)])
                B.dma("pool", ("ot", k), out_own[(4 * i + tt) * 128:(4 * i + tt + 1) * 128, :], ot[k], reads=[("ot", k)])
        B.barrier()

        def replay(e, stream):
            for ent in stream:
                if ent[0] == "wait":
                    e.wait_ge(ent[1], ent[2])
                else:
                    _, name, args, kwargs, sem, inc = ent
                    getattr(e, name)(*args, **kwargs).then_inc(sem, inc)

        with nc.Block() as block:
            @block.tensor
            def _(e):
                replay(e, B.streams["pe"])

            @block.scalar
            def _(e):
                replay(e, B.streams["act"])

            @block.vector
            def _(e):
                replay(e, B.streams["dve"])

            @block.gpsimd
            def _(e):
                replay(e, B.streams["pool"])

            @block.sync
            def _(e):
                replay(e, B.streams["sp"])
        print("instr counts", {k: len(v) for k, v in B.streams.items()}, "sems", len(B.dsem))
    return nc


def _t5_bucket_np(rel):
    try:
        import jax
        import jax.numpy as jnp
        with jax.default_device(jax.devices("cpu")[0]):
            rel_j = jnp.asarray(rel, dtype=jnp.int32)
            nb = 16
            max_exact = 8
            side = jnp.where(rel_j > 0, nb, 0)
            n = jnp.abs(rel_j)
            nf = jnp.maximum(n, 1).astype(jnp.float32)
            large = max_exact + (jnp.log(nf / max_exact) / math.log(128 / max_exact) * (nb - max_exact)).astype(jnp.int32)
            large = jnp.minimum(large, nb - 1)
            return np.asarray(side + jnp.where(n < max_exact, n, large)).astype(np.int64)
    except Exception:
        rel = np.asarray(rel, dtype=np.int64)
        side = np.where(rel > 0, 16, 0)
        n = np.abs(rel)
        nf = np.maximum(n, 1).astype(np.float32)
        large = 8 + (np.log(nf / np.float32(8)) / np.float32(math.log(16.0)) * np.float32(8)).astype(np.int32)
        large = np.minimum(large, 15)
        return side + np.where(n < 8, n, large)


_PROG_CACHE = {}


def kernel(x, norm_g, w_in, sgu_ln_g, sgu_ln_b, sgu_w, sgu_b, lambda_q1, lambda_k1,
           lambda_q2, lambda_k2, subln_g, rel_bias, w_out, final_g):
    x = np.asarray(x, dtype=np.float32)
    Bn, S, _ = x.shape
    assert Bn == 2 and S % 2048 == 0
    NSLOT = S // 2048
    NT = 16 * NSLOT
    bf = ml_dtypes.bfloat16

    u = np.arange(NR)
    bucket = _t5_bucket_np(511 - u)
    onehot = np.zeros((32, NR), np.float32)
    onehot[bucket, u] = 1.0
    kk = np.arange(128)[:, None]
    qq = np.arange(512)[None, :]
    masks = np.zeros((128, 4, 512), np.float32)
    for r in range(4):
        allowed = ((128 * r + kk) // 64) <= (qq // 64)
        masks[:, r, :] = np.where(allowed, 0.0, NEG)
    ss_ = np.arange(128)[:, None]
    tt_ = np.arange(128)[None, :]
    sgumask = ((ss_ // 64) <= (tt_ // 64)).astype(np.float32)
    ident = np.eye(128, dtype=np.float32)
    rev = np.ascontiguousarray(ident[::-1])
    lam_in = np.stack([np.asarray(a, np.float32).reshape(64) for a in (lambda_q1, lambda_k1, lambda_q2, lambda_k2)])

    common = {
        "w_in": np.ascontiguousarray(np.asarray(w_in, np.float32).reshape(D, DIN)),
        "w_out": np.ascontiguousarray(np.asarray(w_out, np.float32).reshape(D, D)),
        "norm_g": np.asarray(norm_g, np.float32).reshape(D),
        "final_g": np.asarray(final_g, np.float32).reshape(D),
        "sgu_ln_g": np.asarray(sgu_ln_g, np.float32).reshape(1024),
        "sgu_ln_b": np.asarray(sgu_ln_b, np.float32).reshape(1024),
        "sgu_w": np.ascontiguousarray(np.asarray(sgu_w, np.float32).reshape(8, 128, 128)),
        "sgu_b": np.ascontiguousarray(np.asarray(sgu_b, np.float32).reshape(8, 128)),
        "lam_in": lam_in,
        "subln_g": np.asarray(subln_g, np.float32).reshape(128),
        "rel_bias": np.ascontiguousarray(np.asarray(rel_bias, np.float32).reshape(32, 8)),
        "c_ident": ident.astype(bf),
        "c_identf": ident,
        "c_rev": rev,
        "c_onehot": onehot,
        "c_masks": np.ascontiguousarray(masks.reshape(128, 4 * 512)),
        "c_sgumask": sgumask,
    }
    in_maps = []
    for c in range(8):
        b, j = divmod(c, 4)
        npad = 12 - 4 * j
        xvirt = np.zeros((NT * 128, D), np.float32)
        xvirt[npad * 128:] = x[b, :(NT - npad) * 128]
        kvalid = np.zeros((128, NT), np.float32)
        kvalid[:, npad:] = 1.0
        m = dict(common)
        m["xv"] = xvirt
        m["c_kvalid"] = kvalid.astype(bf)
        in_maps.append(m)

    if NSLOT not in _PROG_CACHE:
        _PROG_CACHE[NSLOT] = build_program(NSLOT)
    nc = _PROG_CACHE[NSLOT]
    res = run_bass_kernel_spmd(nc, in_maps, core_ids=list(range(8)))
    out = np.empty((Bn, S, D), np.float32)
    for c in range(8):
        b, j = divmod(c, 4)
        o = np.asarray(res.results[c]["out_own"], np.float32)
        for i in range(NSLOT):
            q0 = (4 * i + j) * 512
            out[b, q0:q0 + 512] = o[i * 512:(i + 1) * 512]
    return out
```

```python
import math
from contextlib import ExitStack

import numpy as np
import ml_dtypes

import concourse.bass as bass
import concourse.mybir as mybir
from concourse.bass_utils import run_bass_kernel_spmd

F32 = mybir.dt.float32
BF16 = mybir.dt.bfloat16
AF = mybir.ActivationFunctionType
ALU = mybir.AluOpType

D = 2048
DIN = 7168
NH = 8
NR = 1151
EPS = 1e-6
NEG = -30000.0
ENG = ("pe", "act", "dve", "pool", "sp")
NDS = 90


class _Rec:
    def __getattr__(self, name):
        return lambda *a, **k: (name, a, k)


R = _Rec()


class Builder:
    def __init__(self, nc, stack):
        self.nc = nc
        self.streams = {e: [] for e in ENG}
        self.esem = {e: stack.enter_context(nc.semaphore("s_" + e)) for e in ENG}
        self.cnt = {e: 0 for e in ENG}
        self.pool = [stack.enter_context(nc.semaphore("d%d" % i)) for i in range(NDS)]
        self.dsem = {}
        self.waited = {e: {} for e in ENG}
        self.res = {}

    def _semof(self, key):
        if isinstance(key, str) and key in self.esem:
            return self.esem[key]
        return self.dsem[key][0]

    def _wait(self, eng, ev):
        key, val = ev
        if key == "pe" and eng == "pe":
            return
        w = self.waited[eng]
        if w.get(key, 0) >= val:
            return
        w[key] = val
        sem = self._semof(key)
        self.streams[eng].append(("wait", sem, val))

    def _deps(self, eng, reads, writes):
        for r in reads:
            st = self.res.get(r)
            if st and st[0] is not None:
                self._wait(eng, st[0])
        for w in writes:
            st = self.res.get(w)
            if st:
                if st[0] is not None:
                    self._wait(eng, st[0])
                for k, v in st[1].items():
                    self._wait(eng, (k, v))

    def _update(self, ev, reads, writes):
        for r in reads:
            st = self.res.setdefault(r, [None, {}])
            if st[1].get(ev[0], 0) < ev[1]:
                st[1][ev[0]] = ev[1]
        for w in writes:
            self.res[w] = [ev, {}]

    def op(self, eng, fn, reads=(), writes=()):
        self._deps(eng, reads, writes)
        self.cnt[eng] += 1
        ev = (eng, self.cnt[eng])
        sem = self.esem[eng]
        name, args, kwargs = fn
        self.streams[eng].append(("op", name, args, kwargs, sem, 1))
        self._update(ev, reads, writes)
        return ev

    def dma(self, q, key, out, in_, reads=(), writes=()):
        self._deps(q, reads, writes)
        if key not in self.dsem:
            self.dsem[key] = [self.pool.pop(), 0]
        d = self.dsem[key]
        d[1] += 16
        ev = (key, d[1])
        sem = d[0]
        self.streams[q].append(("op", "dma_start", (), dict(out=out, in_=in_), sem, 16))
        self._update(ev, reads, writes)
        return ev

    def barrier(self):
        evs = [(e, self.cnt[e]) for e in ENG if self.cnt[e] > 0]
        evs += [(k, d[1]) for k, d in self.dsem.items() if d[1] > 0]
        for eng in ENG:
            for ev in evs:
                self._wait(eng, ev)
        self.res = {}


class Mem:
    def __init__(self, big, cap):
        self.big = big
        self.cap = cap
        self.ptr = 0

    def alloc(self, free_shape, dtype):
        esz = 4 if dtype == F32 else 2
        n = 1
        for s in free_shape:
            n *= s
        nb = n * esz
        off = (self.ptr + 63) // 64 * 64
        assert off + nb <= self.cap, ("SBUF overflow", off, nb, self.cap)
        self.ptr = off + nb
        v = self.big[:, off // 2:(off + nb) // 2]
        if dtype == F32:
            v = v.bitcast(F32)
        if len(free_shape) == 2:
            v = v.rearrange("p (a b) -> p a b", a=free_shape[0])
        elif len(free_shape) == 3:
            v = v.rearrange("p (a b c) -> p a b c", a=free_shape[0], b=free_shape[1])
        return v


def build_program(NSLOT):
    NT = 16 * NSLOT
    NG = 4 * NSLOT
    NOWN = NSLOT * 512

    nc = bass.Bass("TRN2", target_bir_lowering=False)

    def din(name, shape, dt=F32):
        return nc.dram_tensor(name, list(shape), dt, kind="ExternalInput").ap()

    def dscr(name, shape, dt):
        return nc.dram_tensor(name, list(shape), dt, kind="Internal").ap()

    xv = din("xv", [NT * 128, D])
    w_in = din("w_in", [D, DIN])
    w_out = din("w_out", [D, D])
    norm_g = din("norm_g", [D])
    final_g = din("final_g", [D])
    ln_g = din("sgu_ln_g", [1024])
    ln_b = din("sgu_ln_b", [1024])
    sgu_w = din("sgu_w", [8, 128, 128])
    sgu_b = din("sgu_b", [8, 128])
    lam_in = din("lam_in", [4, 64])
    subln_g = din("subln_g", [128])
    rel_bias = din("rel_bias", [32, 8])
    c_ident = din("c_ident", [128, 128], BF16)
    c_identf = din("c_identf", [128, 128])
    c_rev = din("c_rev", [128, 128])
    c_onehot = din("c_onehot", [32, NR])
    c_masks = din("c_masks", [128, 4 * 512])
    c_sgumask = din("c_sgumask", [128, 128])
    c_kvalid = din("c_kvalid", [128, NT], BF16)
    out_own = nc.dram_tensor("out_own", [NOWN, D], F32, kind="ExternalOutput").ap()

    KTd = dscr("KTd", [NH, 128, NT * 128], BF16)
    Vd = dscr("Vd", [NH, 128, NT, 128], BF16)
    hTd = dscr("hTd", [NSLOT, 128, 16, 512], BF16)
    QTd = dscr("QTd", [NH, 2, 128, NOWN], BF16)
    GTd = dscr("GTd", [NH, 128, NOWN], F32)
    YTd = dscr("YTd", [16, 128, NOWN], BF16)
    Gd = dscr("Gd", [NH, NR], F32)
    BMd = dscr("BMd", [NH, 128, 5 * 512], F32)

    def bcast(ap1d, n, offset=0):
        return bass.AP(ap1d.tensor, offset, [[0, 128], [1, n]])

    CAP = 210944
    with ExitStack() as stack:
        big = stack.enter_context(nc.sbuf_tensor("big", [128, CAP // 2], BF16))
        ps = stack.enter_context(nc.psum_tensor("ps", [128, 8, 512], F32))
        ps2 = ps.rearrange("p b n -> p (b n)")
        B = Builder(nc, stack)
        mem = Mem(big, CAP)

        def psb16(bank, nbanks=1):
            return ps2[:, bank * 512:(bank + nbanks) * 512].bitcast(BF16)

        ident = mem.alloc([128], BF16)
        gbc = mem.alloc([D], F32)
        epsb = mem.alloc([1], F32)
        chb = mem.alloc([8], F32)
        lamb = mem.alloc([1], F32)
        sgl = mem.alloc([1], F32)
        kval = mem.alloc([NT], BF16)
        small = mem.alloc([64], F32)
        pmark = mem.ptr

        rb = mem.alloc([8], F32)
        oh = mem.alloc([NR], F32)
        rev = mem.alloc([128], F32)
        msk = mem.alloc([4, 512], F32)
        lv = mem.alloc([4, 64], F32)
        lj = mem.alloc([64], F32)
        gsb = mem.alloc([NR], F32)
        Xh = [mem.alloc([512], F32) for _ in range(2)]
        bmst = [mem.alloc([5, 512], F32) for _ in range(2)]

        B.dma("sp", "pl", ident, c_ident, writes=["ident"])
        B.dma("sp", "pl", gbc, bcast(norm_g, D), writes=["gbc"])
        B.dma("sp", "pl", chb, bcast(rel_bias, 8, 15 * 8), writes=["chb"])
        B.dma("sp", "pl", sgl, bass.AP(subln_g.tensor, 0, [[1, 128], [1, 1]]), writes=["sgl"])
        B.dma("sp", "pl", kval, c_kvalid, writes=["kval"])
        B.dma("sp", "pl", rb[0:32, :], rel_bias, writes=["rb"])
        B.dma("sp", "pl", oh[0:32, :], c_onehot, writes=["oh"])
        B.dma("sp", "pl", rev, c_rev, writes=["rev"])
        B.dma("sp", "pl", msk.rearrange("p a b -> p (a b)"), c_masks, writes=["msk"])
        for k in range(4):
            B.dma("sp", "pl", lv[:, k, :], bcast(lam_in, 64, k * 64), writes=[("lv", k)])
        B.barrier()

        B.op("dve", R.memset(epsb, EPS), writes=["epsb"])
        B.op("dve", R.tensor_scalar(out=sgl, in0=sgl, scalar1=0.4, scalar2=None, op0=ALU.mult),
             reads=["sgl"], writes=["sgl"])
        B.op("dve", R.scalar_tensor_tensor(out=lj, in0=lv[:, 0, :], scalar=1.0, in1=lv[:, 1, :],
                                                     op0=ALU.mult, op1=ALU.mult, accum_out=small[:, 0:1]),
             writes=["lj", "s0"])
        B.op("dve", R.scalar_tensor_tensor(out=lj, in0=lv[:, 2, :], scalar=1.0, in1=lv[:, 3, :],
                                                     op0=ALU.mult, op1=ALU.mult, accum_out=small[:, 1:2]),
             reads=[], writes=["lj", "s1"])
        B.op("act", R.activation(out=small[:, 2:4], in_=small[:, 0:2], func=AF.Exp),
             reads=["s0", "s1"], writes=["s23"])
        B.op("dve", R.tensor_tensor(out=small[:, 4:5], in0=small[:, 2:3], in1=small[:, 3:4],
                                              op=ALU.subtract), reads=["s23"], writes=["s4"])
        B.op("dve", R.tensor_scalar(out=lamb, in0=small[:, 4:5], scalar1=0.2, scalar2=None,
                                              op0=ALU.add), reads=["s4"], writes=["lamb"])
        for ci, (c0, n) in enumerate([(0, 512), (512, 512), (1024, NR - 1024)]):
            B.op("pe", R.matmul(ps[0:8, ci, 0:n], lhsT=rb[0:32, 0:8],
                                                              rhs=oh[0:32, c0:c0 + n], start=True, stop=True),
                 writes=[("ps", ci)])
            B.op("act", R.activation(out=gsb[0:8, c0:c0 + n], in_=ps[0:8, ci, 0:n],
                                                                  func=AF.Copy),
                 reads=[("ps", ci)], writes=[("gsb", ci)])
        B.dma("sp", "gd", Gd, gsb[0:8, :], reads=[("gsb", 0), ("gsb", 1), ("gsb", 2)], writes=["Gd"])
        cnt = 0
        for h in range(NH):
            bs = h % 2
            for r in range(-1, 4):
                xs = cnt % 2
                bk = 4 + cnt % 4
                cnt += 1
                src = bass.AP(Gd.tensor, h * NR + 128 * (3 - r), [[1, 128], [1, 512]])
                B.dma("sp", ("xh", xs), Xh[xs], src, reads=["Gd"], writes=[("xh", xs)])
                B.op("pe", R.matmul(ps[:, bk, :], lhsT=rev, rhs=Xh[xs], start=True, stop=True),
                     reads=[("xh", xs)], writes=[("ps", bk)])
                if r >= 0:
                    B.op("dve", R.tensor_tensor(out=bmst[bs][:, r + 1, :], in0=ps[:, bk, :],
                                                                            in1=msk[:, r, :], op=ALU.add),
                         reads=[("ps", bk)], writes=[("bmst", bs, r)])
                else:
                    B.op("dve", R.tensor_copy(out=bmst[bs][:, 0, :], in_=ps[:, bk, :]),
                         reads=[("ps", bk)], writes=[("bmst", bs, r)])
            B.dma("pool", ("bmst", bs), BMd[h], bmst[bs].rearrange("p a b -> p (a b)"),
                  reads=[("bmst", bs, r) for r in range(-1, 4)])
        B.barrier()

        mem.ptr = pmark
        Wkv = mem.alloc([16, 2048], BF16)
        xt = [mem.alloc([D], F32) for _ in range(3)]
        sqj = mem.alloc([D], BF16)
        hb = [mem.alloc([D], BF16) for _ in range(2)]
        hT = [mem.alloc([16, 512], BF16) for _ in range(2)]
        kst = [mem.alloc([8, 512], BF16) for _ in range(2)]
        vst = [mem.alloc([4, 1024], BF16) for _ in range(2)]

        for dc in range(16):
            B.dma("pool", "w", Wkv[:, dc, :], w_in[dc * 128:(dc + 1) * 128, 4096:6144], writes=["W"])

        def hT_keys(gs):
            return [("hT", gs, t, hf) for t in range(4) for hf in range(2)]

        def rms_stats(src, skey, col):
            B.op("act", R.activation(out=sqj, in_=src, func=AF.Square, accum_out=small[:, col:col + 1]),
                 reads=[skey], writes=["sqj", ("st", col)])
            B.op("act", R.activation(out=small[:, col + 1:col + 2], in_=small[:, col:col + 1], func=AF.Ln,
                                               scale=1.0 / D, bias=epsb),
                 reads=[("st", col)], writes=[("st", col + 1)])
            B.op("act", R.activation(out=small[:, col + 2:col + 3], in_=small[:, col + 1:col + 2],
                                               func=AF.Exp, scale=-0.5),
                 reads=[("st", col + 1)], writes=[("st", col + 2)])
            return ("st", col + 2), small[:, col + 2:col + 3]

        def frontend(T):
            g, tt = divmod(T, 4)
            xs, hs, ts, gs = T % 3, T % 2, T % 2, g % 2
            B.dma("sp", ("xt", xs), xt[xs], xv[T * 128:(T + 1) * 128, :], writes=[("xt", xs)])
            rkey, rstd = rms_stats(xt[xs], ("xt", xs), 8 + 4 * (T % 2))
            B.op("dve", R.scalar_tensor_tensor(out=hb[hs], in0=xt[xs], scalar=rstd, in1=gbc,
                                                         op0=ALU.mult, op1=ALU.mult),
                 reads=[("xt", xs), rkey], writes=[("hb", hs)])
            tpv = psb16(2 * ts, 2)
            for dc in range(16):
                B.op("pe", R.transpose(tpv[:, dc * 128:(dc + 1) * 128],
                                                        hb[hs][:, dc * 128:(dc + 1) * 128], ident),
                     reads=[("hb", hs)], writes=[("ps", 2 * ts + dc // 8)])
            B.op("act", R.activation(out=hT[gs][:, 0:8, tt * 128:(tt + 1) * 128],
                                               in_=tpv[:, 0:1024].rearrange("p (c n) -> p c n", c=8), func=AF.Copy),
                 reads=[("ps", 2 * ts)], writes=[("hT", gs, tt, 0)])
            B.op("dve", R.tensor_copy(out=hT[gs][:, 8:16, tt * 128:(tt + 1) * 128],
                                                in_=tpv[:, 1024:2048].rearrange("p (c n) -> p c n", c=8)),
                 reads=[("ps", 2 * ts + 1)], writes=[("hT", gs, tt, 1)])

        chain_no = [0]

        def backend_chains(g):
            gs, ks, vs = g % 2, g % 2, g % 2
            chains = []

            def kchain(fc):
                bk = 4 + chain_no[0] % 4
                chain_no[0] += 1
                for dc in range(16):
                    B.op("pe", R.matmul(ps[:, bk, :], lhsT=Wkv[:, dc, fc * 128:(fc + 1) * 128],
                                                         rhs=hT[gs][:, dc, :], start=(dc == 0), stop=(dc == 15)),
                         reads=hT_keys(gs) + ["W"], writes=[("ps", bk)])
                if fc % 2 == 0:
                    B.op("act", R.activation(out=kst[ks][:, fc, :], in_=ps[:, bk, :], func=AF.Copy),
                         reads=[("ps", bk)], writes=[("kst", ks, fc)])
                else:
                    B.op("dve", R.tensor_copy(out=kst[ks][:, fc, :], in_=ps[:, bk, :]),
                         reads=[("ps", bk)], writes=[("kst", ks, fc)])
                if fc == 7:
                    B.dma("pool", ("kst", ks), KTd[:, :, g * 512:(g + 1) * 512].rearrange("h p t -> p h t"), kst[ks],
                          reads=[("kst", ks, f) for f in range(8)])

            def vchain(tt, hf):
                bk = 4 + chain_no[0] % 4
                chain_no[0] += 1
                for dc in range(16):
                    B.op("pe", R.matmul(ps[:, bk, :], lhsT=hT[gs][:, dc, tt * 128:(tt + 1) * 128],
                                                         rhs=Wkv[:, dc, 1024 + hf * 512:1024 + (hf + 1) * 512],
                                                         start=(dc == 0), stop=(dc == 15)),
                         reads=[("hT", gs, tt, 0), ("hT", gs, tt, 1), "W"], writes=[("ps", bk)])
                if hf == 0:
                    B.op("act", R.activation(out=vst[vs][:, tt, 0:512], in_=ps[:, bk, :], func=AF.Copy),
                         reads=[("ps", bk)], writes=[("vst", vs, tt, hf)])
                else:
                    B.op("dve", R.tensor_copy(out=vst[vs][:, tt, 512:1024], in_=ps[:, bk, :]),
                         reads=[("ps", bk)], writes=[("vst", vs, tt, hf)])
                if tt == 3 and hf == 1:
                    for t4 in range(4):
                        B.dma("pool", ("vst", vs),
                              Vd[:, :, 4 * g + t4, :].rearrange("h p d -> p h d"),
                              vst[vs][:, t4, :].rearrange("p (h d) -> p h d", h=8),
                              reads=[("vst", vs, t, f) for t in range(4) for f in range(2)])
                    if g % 4 == 3:
                        B.dma("pool", ("hTd", gs), hTd[g // 4], hT[gs], reads=hT_keys(gs))

            for fc in range(8):
                chains.append(lambda fc=fc: kchain(fc))
            for tt in range(4):
                for hf in range(2):
                    chains.append(lambda tt=tt, hf=hf: vchain(tt, hf))
            return chains

        for T in range(4):
            frontend(T)
        for g in range(NG):
            chains = backend_chains(g)
            for ci, ch in enumerate(chains):
                ch()
                if ci % 4 == 3 and g + 1 < NG:
                    frontend(4 * (g + 1) + ci // 4)
        B.barrier()

        mem.ptr = pmark
        Wqg = mem.alloc([16, 2048], BF16)
        hT = [mem.alloc([16, 512], BF16) for _ in range(2)]
        qst = [mem.alloc([2, 8, 512], BF16) for _ in range(2)]
        gst = [mem.alloc([512], F32) for _ in range(2)]
        th = [mem.alloc([512], F32) for _ in range(2)]
        for dc in range(16):
            B.dma("pool", "w", Wqg[:, dc, 0:1024], w_in[dc * 128:(dc + 1) * 128, 3072:4096], writes=["W"])
            B.dma("pool", "w", Wqg[:, dc, 1024:2048], w_in[dc * 128:(dc + 1) * 128, 6144:7168], writes=["W"])
        for s in range(2):
            B.op("dve", R.memset(qst[s].rearrange("p a b c -> p (a b c)"), 0.0),
                 writes=[("qst", s, m, f) for m in range(2) for f in range(8)])
        cno = 0
        tno = 0
        for i in range(NSLOT):
            hs = i % 2
            qs = i % 2
            B.dma("sp", ("hTl", hs), hT[hs], hTd[i], writes=[("hT", hs)])
            for fc in range(8):
                bk = cno % 4
                cno += 1
                for dc in range(16):
                    B.op("pe", R.matmul(ps[:, bk, :], lhsT=Wqg[:, dc, fc * 128:(fc + 1) * 128],
                                                                      rhs=hT[hs][:, dc, :], start=(dc == 0), stop=(dc == 15)),
                         reads=[("hT", hs), "W"], writes=[("ps", bk)])
                B.op("act", R.activation(out=qst[qs][0:64, 0, fc, :], in_=ps[0:64, bk, :], func=AF.Copy),
                     reads=[("ps", bk)], writes=[("qst", qs, 0, fc)])
                B.op("dve", R.tensor_copy(out=qst[qs][64:128, 1, fc, :], in_=ps[64:128, bk, :]),
                     reads=[("ps", bk)], writes=[("qst", qs, 1, fc)])
            for m2 in range(2):
                B.dma("pool", ("qst", qs), QTd[:, m2, :, i * 512:(i + 1) * 512].rearrange("h p t -> p h t"),
                      qst[qs][:, m2, :, :], reads=[("qst", qs, m, f) for m in range(2) for f in range(8)])
            for hh in range(8):
                gsl = hh % 2
                bk = 4 + cno % 4
                cno += 1
                tsl = tno % 2
                tno += 1
                for dc in range(16):
                    B.op("pe", R.matmul(ps[:, bk, :], lhsT=Wqg[:, dc, 1024 + hh * 128:1024 + (hh + 1) * 128],
                                        rhs=hT[hs][:, dc, :], start=(dc == 0), stop=(dc == 15)),
                         reads=[("hT", hs), "W"], writes=[("ps", bk)])
                B.op("act", R.activation(out=th[tsl], in_=ps[:, bk, :], func=AF.Tanh, scale=0.5),
                     reads=[("ps", bk)], writes=[("th", tsl)])
                B.op("dve", R.scalar_tensor_tensor(out=gst[gsl], in0=th[tsl], scalar=1.0, in1=ps[:, bk, :],
                                                   op0=ALU.add, op1=ALU.mult),
                     reads=[("th", tsl), ("ps", bk)], writes=[("gst", gsl)])
                B.dma("pool", ("gst", gsl), GTd[hh, :, i * 512:(i + 1) * 512], gst[gsl], reads=[("gst", gsl)])
        B.barrier()

        mem.ptr = pmark
        Wsg = mem.alloc([16, 3072], BF16)
        hT = [mem.alloc([16, 512], BF16) for _ in range(2)]
        wnat = mem.alloc([8, 128], F32)
        identf = mem.alloc([128], F32)
        smk = mem.alloc([128], F32)
        bnat = mem.alloc([128], F32)
        wT = mem.alloc([8, 128], BF16)
        bT = mem.alloc([8], F32)
        lngb = mem.alloc([1024], F32)
        lnbb = mem.alloc([1024], F32)
        gu = mem.alloc([1024], F32)
        gv = mem.alloc([1024], F32)
        t1 = mem.alloc([1024], F32)
        vn = mem.alloc([1024], BF16)
        tha = mem.alloc([1024], F32)
        sa2 = mem.alloc([1024], F32)
        aa = mem.alloc([1024], F32)
        aout = mem.alloc([1024], BF16)
        bst = mem.alloc([2, 6], F32)
        yst = [mem.alloc([8, 512], BF16) for _ in range(2)]
        for dc in range(16):
            B.dma("pool", "w", Wsg[:, dc, :], w_in[dc * 128:(dc + 1) * 128, 0:3072], writes=["W"])
        B.dma("sp", "pl", wnat, sgu_w.rearrange("g t s -> t g s"), writes=["wnat"])
        B.dma("sp", "pl", identf, c_identf, writes=["identf"])
        B.dma("sp", "pl", smk, c_sgumask, writes=["smk"])
        B.dma("sp", "pl", bnat[0:8, :], sgu_b, writes=["bnat"])
        B.dma("sp", "pl", lngb, bcast(ln_g, 1024), writes=["lngb"])
        B.dma("sp", "pl", lnbb, bcast(ln_b, 1024), writes=["lnbb"])
        B.barrier()
        for g8 in range(8):
            bk = g8 % 4
            B.op("pe", R.transpose(ps[:, bk, 0:128], wnat[:, g8, :], identf),
                 reads=["wnat", "identf"], writes=[("ps", bk)])
            B.op("dve", R.tensor_tensor(out=wT[:, g8, :], in0=ps[:, bk, 0:128], in1=smk, op=ALU.mult),
                 reads=[("ps", bk), "smk"], writes=[("wT", g8)])
        B.op("pe", R.transpose(ps[:, 4, 0:8], bnat[0:8, :], identf[0:8, 0:8]),
             reads=["bnat", "identf"], writes=[("ps", 4)])
        B.op("dve", R.tensor_copy(out=bT, in_=ps[:, 4, 0:8]), reads=[("ps", 4)], writes=["bT"])
        B.barrier()

        for i in range(NSLOT):
            hs = i % 2
            ys = i % 2
            B.dma("sp", ("hTl", hs), hT[hs], hTd[i], writes=[("hT", hs)])
            for tt in range(4):
                for cb in range(6):
                    for dc in range(16):
                        B.op("pe", R.matmul(
                            ps[:, cb, :], lhsT=hT[hs][:, dc, tt * 128:(tt + 1) * 128],
                            rhs=Wsg[:, dc, cb * 512:(cb + 1) * 512], start=(dc == 0), stop=(dc == 15)),
                            reads=[("hT", hs), "W"], writes=[("ps", cb)])
                B.op("act", R.activation(out=gu.rearrange("p (a b) -> p a b", a=2), in_=ps[:, 0:2, :],
                                                   func=AF.Gelu_apprx_tanh),
                     reads=[("ps", 0), ("ps", 1)], writes=["gu"])
                B.op("act", R.activation(out=gv.rearrange("p (a b) -> p a b", a=2), in_=ps[:, 2:4, :],
                                                   func=AF.Gelu_apprx_tanh),
                     reads=[("ps", 2), ("ps", 3)], writes=["gv"])
                B.op("act", R.activation(out=tha.rearrange("p (a b) -> p a b", a=2), in_=ps[:, 4:6, :],
                                                   func=AF.Tanh, scale=0.5),
                     reads=[("ps", 4), ("ps", 5)], writes=["tha"])
                B.op("dve", R.bn_stats(out=bst[:, 0, :], in_=gv[:, 0:512]), reads=["gv"], writes=["bst0"])
                B.op("dve", R.bn_stats(out=bst[:, 1, :], in_=gv[:, 512:1024]), reads=["gv"], writes=["bst1"])
                B.op("dve", R.bn_aggr(out=small[:, 16:18], in_=bst.rearrange("p a b -> p (a b)")),
                     reads=["bst0", "bst1"], writes=["mv"])
                B.op("act", R.activation(out=small[:, 18:19], in_=small[:, 17:18], func=AF.Ln, bias=epsb, scale=1.0),
                     reads=["mv"], writes=["lnv"])
                B.op("act", R.activation(out=small[:, 19:20], in_=small[:, 18:19], func=AF.Exp, scale=-0.5),
                     reads=["lnv"], writes=["lrs"])
                B.op("dve", R.tensor_scalar(out=t1, in0=gv, scalar1=small[:, 16:17], scalar2=small[:, 19:20],
                                                      op0=ALU.subtract, op1=ALU.mult),
                     reads=["gv", "mv", "lrs"], writes=["t1"])
                B.op("dve", R.tensor_tensor(out=t1, in0=t1, in1=lngb, op=ALU.mult), reads=["t1"], writes=["t1"])
                B.op("dve", R.tensor_tensor(out=vn, in0=t1, in1=lnbb, op=ALU.add), reads=["t1"], writes=["vn"])
                for g8 in range(8):
                    B.op("pe", R.matmul(ps[:, 6 + g8 // 4, (g8 % 4) * 128:(g8 % 4 + 1) * 128],
                                                         lhsT=wT[:, g8, :], rhs=vn[:, g8 * 128:(g8 + 1) * 128],
                                                         start=True, stop=True),
                         reads=["vn"], writes=[("ps", 6 + g8 // 4)])
                for hf in range(2):
                    B.op("dve", R.scalar_tensor_tensor(
                        out=sa2[:, hf * 512:(hf + 1) * 512], in0=tha[:, hf * 512:(hf + 1) * 512], scalar=1.0,
                        in1=ps[:, 4 + hf, :], op0=ALU.add, op1=ALU.mult),
                        reads=["tha", ("ps", 4 + hf)], writes=[("sa2", hf)])
                for g8 in range(8):
                    B.op("dve", R.scalar_tensor_tensor(
                        out=aa[:, g8 * 128:(g8 + 1) * 128], in0=ps[:, 6 + g8 // 4, (g8 % 4) * 128:(g8 % 4 + 1) * 128],
                        scalar=bT[:, g8:g8 + 1], in1=gu[:, g8 * 128:(g8 + 1) * 128], op0=ALU.add, op1=ALU.mult),
                        reads=[("ps", 6 + g8 // 4), "gu"], writes=[("aa", g8)])
                B.op("dve", R.scalar_tensor_tensor(out=aout, in0=aa, scalar=0.5, in1=sa2, op0=ALU.mult, op1=ALU.mult),
                     reads=[("aa", g8) for g8 in range(8)] + [("sa2", 0), ("sa2", 1)], writes=["aout"])
                tpb = psb16(0, 1)
                for ec in range(8):
                    B.op("pe", R.transpose(tpb[:, ec * 128:(ec + 1) * 128], aout[:, ec * 128:(ec + 1) * 128], ident),
                         reads=["aout"], writes=[("ps", 0)])
                B.op("act", R.activation(out=yst[ys][:, :, tt * 128:(tt + 1) * 128],
                                                          in_=tpb.rearrange("p (c n) -> p c n", c=8), func=AF.Copy),
                     reads=[("ps", 0)], writes=[("yst", ys, tt)])
            B.dma("pool", ("yst", ys), YTd[0:8, :, i * 512:(i + 1) * 512].rearrange("c p t -> p c t"), yst[ys],
                  reads=[("yst", ys, t) for t in range(4)])
        B.barrier()

        mem.ptr = pmark
        kT = [mem.alloc([NT * 128], BF16) for _ in range(2)]
        vA = [mem.alloc([NT, 128], BF16) for _ in range(2)]
        bm = mem.alloc([5, 512], F32)
        qT = [mem.alloc([2, 512], BF16) for _ in range(2)]
        gtT = [mem.alloc([512], F32) for _ in range(2)]
        pt = [mem.alloc([2, 512], BF16) for _ in range(3)]
        tmpf = mem.alloc([2, 512], F32)
        rsd = [mem.alloc([2, 512], F32) for _ in range(2)]
        sh = mem.alloc([512], F32)
        zb = mem.alloc([2, 128], BF16)
        sel = mem.alloc([2, 128], F32)
        onesf = mem.alloc([128], F32)
        kvf = mem.alloc([NT], F32)
        nlam = mem.alloc([1], F32)
        sbs = mem.alloc([2, 512], F32)
        ea = mem.alloc([512], F32)
        eb = mem.alloc([512], F32)
        eo = mem.alloc([512], F32)
        ep = mem.alloc([512], F32)
        ystc = [mem.alloc([512], BF16) for _ in range(2)]

        B.op("dve", R.memset(onesf, 1.0), writes=["onesf"])
        B.op("dve", R.memset(zb.rearrange("p a b -> p (a b)"), 0.0), writes=["zb"])
        B.op("dve", R.memset(zb[:, 0, 0:64], 1.0), reads=["zb"], writes=["zb"])
        B.op("dve", R.memset(zb[:, 1, 64:128], 1.0), reads=["zb"], writes=["zb"])
        B.op("dve", R.memset(sel.rearrange("p a b -> p (a b)"), 0.0), writes=["sel"])
        B.op("dve", R.memset(sel[0:1, 0, :], 1.0), reads=["sel"], writes=["sel"])
        B.op("dve", R.memset(sel[64:65, 1, :], 1.0), reads=["sel"], writes=["sel"])
        B.op("dve", R.tensor_copy(out=kvf, in_=kval), reads=["kval"], writes=["kvf"])
        B.op("dve", R.tensor_scalar(out=nlam, in0=lamb, scalar1=-1.0, scalar2=None, op0=ALU.mult),
             reads=["lamb"], writes=["nlam"])

        def load_head(h):
            hs = h % 2
            B.dma("sp", ("kT", hs), kT[hs], KTd[h], writes=[("kT", hs)])
            B.dma("sp", ("vA", hs), vA[hs], Vd[h], writes=[("vA", hs)])

        pcount = [0]
        slotc = [0]
        pending = []

        def flat(t):
            return t.rearrange("p a b -> p (a b)")

        def epilogue_parts(h, i, sl, gs, ys):
            def copies():
                B.op("dve", R.tensor_copy(out=ea, in_=ps[:, 4, :]), reads=[("ps", 4)], writes=["ea"])
                B.op("dve", R.tensor_copy(out=eb, in_=ps[:, 5, :]), reads=[("ps", 5)], writes=["eb"])
                B.op("dve", R.tensor_copy(out=sh, in_=ps[:, 6, :]), reads=[("ps", 6)], writes=["sh"])

            def sums_mm(m):
                def f():
                    B.op("pe", R.matmul(ps[:, 7, :], lhsT=onesf, rhs=rsd[sl][:, m, :], start=True, stop=False),
                         reads=[("rsd", sl), "onesf"], writes=[("ps", 7)])
                    B.op("pe", R.matmul(ps[:, 7, :], lhsT=sel[:, m, :], rhs=sh, start=False, stop=True),
                         reads=["sh", "sel"], writes=[("ps", 7)])
                return f

            def sums_scale(m):
                def f():
                    B.op("dve", R.tensor_scalar(out=sbs[:, m, :], in0=ps[:, 7, :], scalar1=2.0 ** -14, scalar2=None,
                                                op0=ALU.mult),
                         reads=[("ps", 7)], writes=[("sbs", m)])
                return f

            def combine():
                B.op("dve", R.tensor_tensor(out=ea, in0=ea, in1=sbs[:, 1, :], op=ALU.mult), reads=["ea", ("sbs", 1)], writes=["ea"])
                B.op("dve", R.tensor_tensor(out=eb, in0=eb, in1=sbs[:, 0, :], op=ALU.mult), reads=["eb", ("sbs", 0)], writes=["eb"])
                B.op("dve", R.scalar_tensor_tensor(out=eo, in0=eb, scalar=nlam, in1=ea, op0=ALU.mult, op1=ALU.add),
                     reads=["ea", "eb", "nlam"], writes=["eo"])
                B.op("dve", R.tensor_tensor(out=eb, in0=eo, in1=eo, op=ALU.mult), reads=["eo"], writes=["eb"])
                B.op("dve", R.tensor_tensor(out=ep, in0=sbs[:, 0, :], in1=sbs[:, 1, :], op=ALU.mult),
                     reads=[("sbs", 0), ("sbs", 1)], writes=["ep"])
                B.op("dve", R.scalar_tensor_tensor(out=ep, in0=ep, scalar=EPS, in1=ep, op0=ALU.mult, op1=ALU.mult),
                     reads=["ep"], writes=["ep"])

            def ms_mm():
                B.op("pe", R.matmul(ps[:, 7, :], lhsT=onesf, rhs=eb, start=True, stop=True),
                     reads=["eb", "onesf"], writes=[("ps", 7)])

            def ms_add():
                B.op("dve", R.scalar_tensor_tensor(out=ea, in0=ps[:, 7, :], scalar=1.0 / 128, in1=ep, op0=ALU.mult, op1=ALU.add),
                     reads=[("ps", 7), "ep"], writes=["ea"])

            def rstd():
                B.op("act", R.activation(out=ea, in_=ea, func=AF.Ln), reads=["ea"], writes=["ea"])
                B.op("act", R.activation(out=ea, in_=ea, func=AF.Exp, scale=-0.5), reads=["ea"], writes=["ea"])

            def final():
                B.op("dve", R.scalar_tensor_tensor(out=eo, in0=eo, scalar=sgl, in1=ea, op0=ALU.mult, op1=ALU.mult),
                     reads=["eo", "ea", "sgl"], writes=["eo"])
                B.op("dve", R.tensor_tensor(out=ystc[ys], in0=eo, in1=gtT[gs], op=ALU.mult),
                     reads=["eo", ("gtT", gs)], writes=[("ystc", ys)])
                B.dma("pool", ("ystc", ys), YTd[8 + h, :, i * 512:(i + 1) * 512], ystc[ys], reads=[("ystc", ys)])
            return [(-1, copies), (0, sums_mm(0)), (1, sums_scale(0)), (2, sums_mm(1)), (3, sums_scale(1)), (3, combine),
                    (6, ms_mm), (7, ms_add), (9, rstd), (11, final)]

        zbv = mem.alloc([12, 2, 128], BF16)
        for v in range(12):
            B.op("dve", R.tensor_scalar(out=zbv[:, v, :, :], in0=zb, scalar1=kvf[:, v:v + 1], scalar2=None, op0=ALU.mult),
                 reads=["zb", "kvf"], writes=[("zbv", v)])
        tmpf2 = [tmpf, mem.alloc([2, 512], F32)]

        def run_pending(v):
            while pending and pending[0][0] <= v:
                pending.pop(0)[1]()

        load_head(0)
        for h in range(NH):
            hs = h % 2
            B.dma("sp", "bm", flat(bm), BMd[h], writes=["bm"])
            for i in range(NSLOT):
                sc = slotc[0]
                slotc[0] += 1
                qs = gs = sl = ys = sc % 2
                B.dma("sp", ("qT", qs), qT[qs], QTd[h, :, :, i * 512:(i + 1) * 512].rearrange("m p t -> p m t"),
                      writes=[("qT", qs)])
                B.dma("sp", ("gtT", gs), gtT[gs], GTd[h, :, i * 512:(i + 1) * 512], writes=[("gtT", gs)])
                if i == 0 and h + 1 < NH:
                    load_head(h + 1)
                n = 16 * (i + 1)

                def emit_S(v, pu):
                    b = pu % 2
                    for m in range(2):
                        B.op("pe", R.matmul(ps[:, 2 * b + m, :], lhsT=kT[hs][:, v * 128:(v + 1) * 128], rhs=qT[qs][:, m, :],
                                            start=True, stop=True),
                             reads=[("kT", hs), ("qT", qs)], writes=[("ps", 2 * b + m)])

                emit_S(0, pcount[0])
                emit_S(1, pcount[0] + 1)
                psum_started = False
                for v in range(n):
                    pu = pcount[0]
                    pcount[0] += 1
                    b = pu % 2
                    s3 = pu % 3
                    r = v - (n - 4)
                    if r < -1:
                        B.op("act", R.activation(out=pt[s3], in_=ps[:, 2 * b:2 * b + 2, :], func=AF.Exp, scale=0.125,
                                                 bias=chb[:, h:h + 1]),
                             reads=[("ps", 2 * b), ("ps", 2 * b + 1)], writes=[("pt", s3)])
                    else:
                        tf = tmpf2[(r + 1) % 2]
                        tk = (r + 1) % 2
                        for m in range(2):
                            B.op("dve", R.scalar_tensor_tensor(out=tf[:, m, :], in0=ps[:, 2 * b + m, :], scalar=0.125,
                                                               in1=bm[:, r + 1, :], op0=ALU.mult, op1=ALU.add),
                                 reads=[("ps", 2 * b + m), "bm"], writes=[("tmpf", tk, m)])
                        B.op("act", R.activation(out=pt[s3], in_=tf, func=AF.Exp),
                             reads=[("tmpf", tk, 0), ("tmpf", tk, 1)], writes=[("pt", s3)])
                    if v + 2 < n:
                        emit_S(v + 2, pu + 2)
                    for m in range(2):
                        B.op("pe", R.matmul(ps[:, 4 + m, :], lhsT=vA[hs][:, v, :], rhs=pt[s3][:, m, :],
                                            start=(v == 0), stop=(v == n - 1)),
                             reads=[("pt", s3), ("vA", hs)], writes=[("ps", 4 + m)])
                    if v % 2 == 1:
                        for m in range(2):
                            zt = zbv[:, v, m, :] if v < 12 else zb[:, m, :]
                            B.op("pe", R.matmul(ps[:, 6, :], lhsT=zt, rhs=pt[s3][:, m, :],
                                                start=(not psum_started and m == 0), stop=(v == n - 1 and m == 1)),
                                 reads=[("pt", s3), "zb"] + ([("zbv", v)] if v < 12 else []), writes=[("ps", 6)])
                        psum_started = True
                    elif v == 0:
                        B.op("dve", R.tensor_scalar(out=flat(rsd[sl]), in0=flat(pt[s3]), scalar1=kvf[:, 0:1], scalar2=None,
                                                    op0=ALU.mult),
                             reads=[("pt", s3), "kvf"], writes=[("rsd", sl)])
                    elif v < 12:
                        B.op("dve", R.scalar_tensor_tensor(out=flat(rsd[sl]), in0=flat(pt[s3]), scalar=kvf[:, v:v + 1],
                                                           in1=flat(rsd[sl]), op0=ALU.mult, op1=ALU.add),
                             reads=[("pt", s3), "kvf", ("rsd", sl)], writes=[("rsd", sl)])
                    else:
                        B.op("dve", R.tensor_tensor(out=flat(rsd[sl]), in0=flat(rsd[sl]), in1=flat(pt[s3]), op=ALU.add),
                             reads=[("pt", s3), ("rsd", sl)], writes=[("rsd", sl)])
                    run_pending(v)
                run_pending(10 ** 9)
                for trig, fn in epilogue_parts(h, i, sl, gs, ys):
                    if trig < 0:
                        fn()
                    else:
                        pending.append((trig, fn))
        run_pending(10 ** 9)
        B.barrier()

        mem.ptr = pmark
        Wo = mem.alloc([16, 2048], BF16)
        fgb = mem.alloc([D], F32)
        yT = [mem.alloc([16, 512], BF16) for _ in range(2)]
        xo = [mem.alloc([D], F32) for _ in range(2)]
        rr = [mem.alloc([D], F32) for _ in range(2)]
        ot = [mem.alloc([D], F32) for _ in range(2)]
        sqj = mem.alloc([D], BF16)
        for dc in range(16):
            B.dma("pool", "w", Wo[:, dc, :], w_out[dc * 128:(dc + 1) * 128, :], writes=["W"])
        B.dma("sp", "pl", fgb, bcast(final_g, D), writes=["fgb"])
        tcount = 0
        for i in range(NSLOT):
            ysl = i % 2
            B.dma("sp", ("yT", ysl), yT[ysl], YTd[:, :, i * 512:(i + 1) * 512].rearrange("c p t -> p c t"),
                  writes=[("yT", ysl)])
            for tt in range(4):
                k = tcount % 2
                tcount += 1
                Tv = 16 * i + 12 + tt
                B.dma("sp", ("xo", k), xo[k], xv[Tv * 128:(Tv + 1) * 128, :], writes=[("xo", k)])
                for nb in range(4):
                    bk = 4 * k + nb
                    for ec in range(16):
                        B.op("pe", R.matmul(
                            ps[:, bk, :], lhsT=yT[ysl][:, ec, tt * 128:(tt + 1) * 128],
                            rhs=Wo[:, ec, nb * 512:(nb + 1) * 512], start=(ec == 0), stop=(ec == 15)),
                            reads=[("yT", ysl), "W"], writes=[("ps", bk)])
                    B.op("dve", R.tensor_tensor(out=rr[k][:, nb * 512:(nb + 1) * 512], in0=ps[:, bk, :],
                                                                      in1=xo[k][:, nb * 512:(nb + 1) * 512], op=ALU.add),
                         reads=[("ps", bk), ("xo", k)], writes=[("rr", k, nb)])
                col = 8 + 4 * k
                B.op("act", R.activation(out=sqj, in_=rr[k], func=AF.Square, accum_out=small[:, col:col + 1]),
                     reads=[("rr", k, nb) for nb in range(4)], writes=["sqj", ("st", col)])
                B.op("act", R.activation(out=small[:, col + 1:col + 2], in_=small[:, col:col + 1], func=AF.Ln,
                                                   scale=1.0 / D, bias=epsb),
                     reads=[("st", col)], writes=[("st", col + 1)])
                B.op("act", R.activation(out=small[:, col + 2:col + 3], in_=small[:, col + 1:col + 2], func=AF.Exp,
                                                   scale=-0.5),
                     reads=[("st", col + 1)], writes=[("st", col + 2)])
                B.op("dve", R.scalar_tensor_tensor(out=ot[k], in0=rr[k], scalar=small[:, col + 2:col + 3], in1=fgb,
                                                             op0=ALU.mult, op1=ALU.mult),
                     reads=[("rr", k, nb) for nb in range(4)] + [("st", col + 2), "fgb"], writes=[("ot", k)])
                B.dma("pool", ("ot", k), out_own[(4 * i + tt) * 128:(4 * i + tt + 1) * 128, :], ot[k], reads=[("ot", k)])
        B.barrier()

        def replay(e, stream):
            for ent in stream:
                if ent[0] == "wait":
                    e.wait_ge(ent[1], ent[2])
                else:
                    _, name, args, kwargs, sem, inc = ent
                    getattr(e, name)(*args, **kwargs).then_inc(sem, inc)

        with nc.Block() as block:
            @block.tensor
            def _(e):
                replay(e, B.streams["pe"])

            @block.scalar
            def _(e):
                replay(e, B.streams["act"])

            @block.vector
            def _(e):
                replay(e, B.streams["dve"])

            @block.gpsimd
            def _(e):
                replay(e, B.streams["pool"])

            @block.sync
            def _(e):
                replay(e, B.streams["sp"])
        print("instr counts", {k: len(v) for k, v in B.streams.items()}, "sems", len(B.dsem))
    return nc


def _t5_bucket_np(rel):
    try:
        import jax
        import jax.numpy as jnp
        with jax.default_device(jax.devices("cpu")[0]):
            rel_j = jnp.asarray(rel, dtype=jnp.int32)
            nb = 16
            max_exact = 8
            side = jnp.where(rel_j > 0, nb, 0)
            n = jnp.abs(rel_j)
            nf = jnp.maximum(n, 1).astype(jnp.float32)
            large = max_exact + (jnp.log(nf / max_exact) / math.log(128 / max_exact) * (nb - max_exact)).astype(jnp.int32)
            large = jnp.minimum(large, nb - 1)
            return np.asarray(side + jnp.where(n < max_exact, n, large)).astype(np.int64)
    except Exception:
        rel = np.asarray(rel, dtype=np.int64)
        side = np.where(rel > 0, 16, 0)
        n = np.abs(rel)
        nf = np.maximum(n, 1).astype(np.float32)
        large = 8 + (np.log(nf / np.float32(8)) / np.float32(math.log(16.0)) * np.float32(8)).astype(np.int32)
        large = np.minimum(large, 15)
        return side + np.where(n < 8, n, large)


_PROG_CACHE = {}


def kernel(x, norm_g, w_in, sgu_ln_g, sgu_ln_b, sgu_w, sgu_b, lambda_q1, lambda_k1,
           lambda_q2, lambda_k2, subln_g, rel_bias, w_out, final_g):
    x = np.asarray(x, dtype=np.float32)
    Bn, S, _ = x.shape
    assert Bn == 2 and S % 2048 == 0
    NSLOT = S // 2048
    NT = 16 * NSLOT
    bf = ml_dtypes.bfloat16

    u = np.arange(NR)
    bucket = _t5_bucket_np(511 - u)
    onehot = np.zeros((32, NR), np.float32)
    onehot[bucket, u] = 1.0
    kk = np.arange(128)[:, None]
    qq = np.arange(512)[None, :]
    masks = np.zeros((128, 4, 512), np.float32)
    for r in range(4):
        allowed = ((128 * r + kk) // 64) <= (qq // 64)
        masks[:, r, :] = np.where(allowed, 0.0, NEG)
    ss_ = np.arange(128)[:, None]
    tt_ = np.arange(128)[None, :]
    sgumask = ((ss_ // 64) <= (tt_ // 64)).astype(np.float32)
    ident = np.eye(128, dtype=np.float32)
    rev = np.ascontiguousarray(ident[::-1])
    lam_in = np.stack([np.asarray(a, np.float32).reshape(64) for a in (lambda_q1, lambda_k1, lambda_q2, lambda_k2)])

    common = {
        "w_in": np.ascontiguousarray(np.asarray(w_in, np.float32).reshape(D, DIN)),
        "w_out": np.ascontiguousarray(np.asarray(w_out, np.float32).reshape(D, D)),
        "norm_g": np.asarray(norm_g, np.float32).reshape(D),
        "final_g": np.asarray(final_g, np.float32).reshape(D),
        "sgu_ln_g": np.asarray(sgu_ln_g, np.float32).reshape(1024),
        "sgu_ln_b": np.asarray(sgu_ln_b, np.float32).reshape(1024),
        "sgu_w": np.ascontiguousarray(np.asarray(sgu_w, np.float32).reshape(8, 128, 128)),
        "sgu_b": np.ascontiguousarray(np.asarray(sgu_b, np.float32).reshape(8, 128)),
        "lam_in": lam_in,
        "subln_g": np.asarray(subln_g, np.float32).reshape(128),
        "rel_bias": np.ascontiguousarray(np.asarray(rel_bias, np.float32).reshape(32, 8)),
        "c_ident": ident.astype(bf),
        "c_identf": ident,
        "c_rev": rev,
        "c_onehot": onehot,
        "c_masks": np.ascontiguousarray(masks.reshape(128, 4 * 512)),
        "c_sgumask": sgumask,
    }
    in_maps = []
    for c in range(8):
        b, j = divmod(c, 4)
        npad = 12 - 4 * j
        xvirt = np.zeros((NT * 128, D), np.float32)
        xvirt[npad * 128:] = x[b, :(NT - npad) * 128]
        kvalid = np.zeros((128, NT), np.float32)
        kvalid[:, npad:] = 1.0
        m = dict(common)
        m["xv"] = xvirt
        m["c_kvalid"] = kvalid.astype(bf)
        in_maps.append(m)

    if NSLOT not in _PROG_CACHE:
        _PROG_CACHE[NSLOT] = build_program(NSLOT)
    nc = _PROG_CACHE[NSLOT]
    res = run_bass_kernel_spmd(nc, in_maps, core_ids=list(range(8)))
    out = np.empty((Bn, S, D), np.float32)
    for c in range(8):
        b, j = divmod(c, 4)
        o = np.asarray(res.results[c]["out_own"], np.float32)
        for i in range(NSLOT):
            q0 = (4 * i + j) * 512
            out[b, q0:q0 + 512] = o[i * 512:(i + 1) * 512]
    return out
```

```python
import math
from contextlib import ExitStack

import numpy as np
import ml_dtypes

import concourse.bass as bass
import concourse.mybir as mybir
from concourse.bass_utils import run_bass_kernel_spmd

F32 = mybir.dt.float32
BF16 = mybir.dt.bfloat16
AF = mybir.ActivationFunctionType
ALU = mybir.AluOpType

D = 2048
DIN = 7168
NH = 8
NR = 1151
EPS = 1e-6
NEG = -30000.0
ENG = ("pe", "act", "dve", "pool", "sp")
NDS = 90


class _Rec:
    def __getattr__(self, name):
        return lambda *a, **k: (name, a, k)


R = _Rec()


class Builder:
    def __init__(self, nc, stack):
        self.nc = nc
        self.streams = {e: [] for e in ENG}
        self.esem = {e: stack.enter_context(nc.semaphore("s_" + e)) for e in ENG}
        self.cnt = {e: 0 for e in ENG}
        self.pool = [stack.enter_context(nc.semaphore("d%d" % i)) for i in range(NDS)]
        self.dsem = {}
        self.waited = {e: {} for e in ENG}
        self.res = {}

    def _semof(self, key):
        if isinstance(key, str) and key in self.esem:
            return self.esem[key]
        return self.dsem[key][0]

    def _wait(self, eng, ev):
        key, val = ev
        if key == "pe" and eng == "pe":
            return
        w = self.waited[eng]
        if w.get(key, 0) >= val:
            return
        w[key] = val
        sem = self._semof(key)
        self.streams[eng].append(("wait", sem, val))

    def _deps(self, eng, reads, writes):
        for r in reads:
            st = self.res.get(r)
            if st and st[0] is not None:
                self._wait(eng, st[0])
        for w in writes:
            st = self.res.get(w)
            if st:
                if st[0] is not None:
                    self._wait(eng, st[0])
                for k, v in st[1].items():
                    self._wait(eng, (k, v))

    def _update(self, ev, reads, writes):
        for r in reads:
            st = self.res.setdefault(r, [None, {}])
            if st[1].get(ev[0], 0) < ev[1]:
                st[1][ev[0]] = ev[1]
        for w in writes:
            self.res[w] = [ev, {}]

    def op(self, eng, fn, reads=(), writes=()):
        self._deps(eng, reads, writes)
        self.cnt[eng] += 1
        ev = (eng, self.cnt[eng])
        sem = self.esem[eng]
        name, args, kwargs = fn
        self.streams[eng].append(("op", name, args, kwargs, sem, 1))
        self._update(ev, reads, writes)
        return ev

    def dma(self, q, key, out, in_, reads=(), writes=()):
        self._deps(q, reads, writes)
        if key not in self.dsem:
            self.dsem[key] = [self.pool.pop(), 0]
        d = self.dsem[key]
        d[1] += 16
        ev = (key, d[1])
        sem = d[0]
        self.streams[q].append(("op", "dma_start", (), dict(out=out, in_=in_), sem, 16))
        self._update(ev, reads, writes)
        return ev

    def barrier(self):
        evs = [(e, self.cnt[e]) for e in ENG if self.cnt[e] > 0]
        evs += [(k, d[1]) for k, d in self.dsem.items() if d[1] > 0]
        for eng in ENG:
            for ev in evs:
                self._wait(eng, ev)
        self.res = {}


class Mem:
    def __init__(self, big, cap):
        self.big = big
        self.cap = cap
        self.ptr = 0

    def alloc(self, free_shape, dtype):
        esz = 4 if dtype == F32 else 2
        n = 1
        for s in free_shape:
            n *= s
        nb = n * esz
        off = (self.ptr + 63) // 64 * 64
        assert off + nb <= self.cap, ("SBUF overflow", off, nb, self.cap)
        self.ptr = off + nb
        v = self.big[:, off // 2:(off + nb) // 2]
        if dtype == F32:
            v = v.bitcast(F32)
        if len(free_shape) == 2:
            v = v.rearrange("p (a b) -> p a b", a=free_shape[0])
        elif len(free_shape) == 3:
            v = v.rearrange("p (a b c) -> p a b c", a=free_shape[0], b=free_shape[1])
        return v


def build_program(NSLOT):
    NT = 16 * NSLOT
    NG = 4 * NSLOT
    NOWN = NSLOT * 512

    nc = bass.Bass("TRN2", target_bir_lowering=False)

    def din(name, shape, dt=F32):
        return nc.dram_tensor(name, list(shape), dt, kind="ExternalInput").ap()

    def dscr(name, shape, dt):
        return nc.dram_tensor(name, list(shape), dt, kind="Internal").ap()

    xv = din("xv", [NT * 128, D])
    w_in = din("w_in", [D, DIN])
    w_out = din("w_out", [D, D])
    norm_g = din("norm_g", [D])
    final_g = din("final_g", [D])
    ln_g = din("sgu_ln_g", [1024])
    ln_b = din("sgu_ln_b", [1024])
    sgu_w = din("sgu_w", [8, 128, 128])
    sgu_b = din("sgu_b", [8, 128])
    lam_in = din("lam_in", [4, 64])
    subln_g = din("subln_g", [128])
    rel_bias = din("rel_bias", [32, 8])
    c_ident = din("c_ident", [128, 128], BF16)
    c_identf = din("c_identf", [128, 128])
    c_rev = din("c_rev", [128, 128])
    c_onehot = din("c_onehot", [32, NR])
    c_masks = din("c_masks", [128, 4 * 512])
    c_sgumask = din("c_sgumask", [128, 128])
    c_kvalid = din("c_kvalid", [128, NT], BF16)
    out_own = nc.dram_tensor("out_own", [NOWN, D], F32, kind="ExternalOutput").ap()

    KTd = dscr("KTd", [NH, 128, NT * 128], BF16)
    Vd = dscr("Vd", [NH, 128, NT, 128], BF16)
    hTd = dscr("hTd", [NSLOT, 128, 16, 512], BF16)
    QTd = dscr("QTd", [NH, 2, 128, NOWN], BF16)
    GTd = dscr("GTd", [NH, 128, NOWN], F32)
    YTd = dscr("YTd", [16, 128, NOWN], BF16)
    Gd = dscr("Gd", [NH, NR], F32)
    BMd = dscr("BMd", [NH, 128, 5 * 512], F32)

    def bcast(ap1d, n, offset=0):
        return bass.AP(ap1d.tensor, offset, [[0, 128], [1, n]])

    CAP = 210944
    with ExitStack() as stack:
        big = stack.enter_context(nc.sbuf_tensor("big", [128, CAP // 2], BF16))
        ps = stack.enter_context(nc.psum_tensor("ps", [128, 8, 512], F32))
        ps2 = ps.rearrange("p b n -> p (b n)")
        B = Builder(nc, stack)
        mem = Mem(big, CAP)

        def psb16(bank, nbanks=1):
            return ps2[:, bank * 512:(bank + nbanks) * 512].bitcast(BF16)

        ident = mem.alloc([128], BF16)
        epsb = mem.alloc([1], F32)
        chb = mem.alloc([8], F32)
        lamb = mem.alloc([1], F32)
        sgl = mem.alloc([1], F32)
        kval = mem.alloc([NT], BF16)
        small = mem.alloc([64], F32)
        pmark = mem.ptr

        rb = mem.alloc([8], F32)
        oh = mem.alloc([NR], F32)
        rev = mem.alloc([128], F32)
        msk = mem.alloc([4, 512], F32)
        lv = mem.alloc([4, 64], F32)
        lj = mem.alloc([64], F32)
        gsb = mem.alloc([NR], F32)
        Xh = [mem.alloc([512], F32) for _ in range(2)]
        bmst = [mem.alloc([5, 512], F32) for _ in range(2)]

        B.dma("sp", "pl", ident, c_ident, writes=["ident"])
        B.dma("sp", "pl", chb, bcast(rel_bias, 8, 15 * 8), writes=["chb"])
        B.dma("sp", "pl", sgl, bass.AP(subln_g.tensor, 0, [[1, 128], [1, 1]]), writes=["sgl"])
        B.dma("sp", "pl", kval, c_kvalid, writes=["kval"])
        B.dma("sp", "pl", rb[0:32, :], rel_bias, writes=["rb"])
        B.dma("sp", "pl", oh[0:32, :], c_onehot, writes=["oh"])
        B.dma("sp", "pl", rev, c_rev, writes=["rev"])
        B.dma("sp", "pl", msk.rearrange("p a b -> p (a b)"), c_masks, writes=["msk"])
        for k in range(4):
            B.dma("sp", "pl", lv[:, k, :], bcast(lam_in, 64, k * 64), writes=[("lv", k)])
        B.barrier()

        B.op("dve", R.memset(epsb, EPS), writes=["epsb"])
        B.op("dve", R.tensor_scalar(out=sgl, in0=sgl, scalar1=0.4, scalar2=None, op0=ALU.mult),
             reads=["sgl"], writes=["sgl"])
        B.op("dve", R.scalar_tensor_tensor(out=lj, in0=lv[:, 0, :], scalar=1.0, in1=lv[:, 1, :],
                                                     op0=ALU.mult, op1=ALU.mult, accum_out=small[:, 0:1]),
             writes=["lj", "s0"])
        B.op("dve", R.scalar_tensor_tensor(out=lj, in0=lv[:, 2, :], scalar=1.0, in1=lv[:, 3, :],
                                                     op0=ALU.mult, op1=ALU.mult, accum_out=small[:, 1:2]),
             reads=[], writes=["lj", "s1"])
        B.op("act", R.activation(out=small[:, 2:4], in_=small[:, 0:2], func=AF.Exp),
             reads=["s0", "s1"], writes=["s23"])
        B.op("dve", R.tensor_tensor(out=small[:, 4:5], in0=small[:, 2:3], in1=small[:, 3:4],
                                              op=ALU.subtract), reads=["s23"], writes=["s4"])
        B.op("dve", R.tensor_scalar(out=lamb, in0=small[:, 4:5], scalar1=0.2, scalar2=None,
                                              op0=ALU.add), reads=["s4"], writes=["lamb"])
        for ci, (c0, n) in enumerate([(0, 512), (512, 512), (1024, NR - 1024)]):
            B.op("pe", R.matmul(ps[0:8, ci, 0:n], lhsT=rb[0:32, 0:8],
                                                              rhs=oh[0:32, c0:c0 + n], start=True, stop=True),
                 writes=[("ps", ci)])
            B.op("act", R.activation(out=gsb[0:8, c0:c0 + n], in_=ps[0:8, ci, 0:n],
                                                                  func=AF.Copy),
                 reads=[("ps", ci)], writes=[("gsb", ci)])
        B.dma("sp", "gd", Gd, gsb[0:8, :], reads=[("gsb", 0), ("gsb", 1), ("gsb", 2)], writes=["Gd"])
        cnt = 0
        for h in range(NH):
            bs = h % 2
            for r in range(-1, 4):
                xs = cnt % 2
                bk = 4 + cnt % 4
                cnt += 1
                src = bass.AP(Gd.tensor, h * NR + 128 * (3 - r), [[1, 128], [1, 512]])
                B.dma("sp", ("xh", xs), Xh[xs], src, reads=["Gd"], writes=[("xh", xs)])
                B.op("pe", R.matmul(ps[:, bk, :], lhsT=rev, rhs=Xh[xs], start=True, stop=True),
                     reads=[("xh", xs)], writes=[("ps", bk)])
                if r >= 0:
                    B.op("dve", R.tensor_tensor(out=bmst[bs][:, r + 1, :], in0=ps[:, bk, :],
                                                                            in1=msk[:, r, :], op=ALU.add),
                         reads=[("ps", bk)], writes=[("bmst", bs, r)])
                else:
                    B.op("dve", R.tensor_copy(out=bmst[bs][:, 0, :], in_=ps[:, bk, :]),
                         reads=[("ps", bk)], writes=[("bmst", bs, r)])
            B.dma("pool", ("bmst", bs), BMd[h], bmst[bs].rearrange("p a b -> p (a b)"),
                  reads=[("bmst", bs, r) for r in range(-1, 4)])
        B.barrier()

        mem.ptr = pmark
        Wkv = mem.alloc([16, 2048], BF16)
        gbc = mem.alloc([D], F32)
        B.dma("sp", "pl", gbc, bcast(norm_g, D), writes=["gbc"])
        xt = [mem.alloc([D], F32) for _ in range(3)]
        sqj = mem.alloc([D], BF16)
        hb = [mem.alloc([D], BF16) for _ in range(2)]
        hT = [mem.alloc([16, 512], BF16) for _ in range(2)]
        kst = [mem.alloc([8, 512], BF16) for _ in range(2)]
        vst = [mem.alloc([4, 1024], BF16) for _ in range(2)]

        for dc in range(16):
            B.dma("pool", "w", Wkv[:, dc, :], w_in[dc * 128:(dc + 1) * 128, 4096:6144], writes=["W"])

        def hT_keys(gs):
            return [("hT", gs, t, hf) for t in range(4) for hf in range(2)]

        def rms_stats(src, skey, col):
            B.op("act", R.activation(out=sqj, in_=src, func=AF.Square, accum_out=small[:, col:col + 1]),
                 reads=[skey], writes=["sqj", ("st", col)])
            B.op("act", R.activation(out=small[:, col + 1:col + 2], in_=small[:, col:col + 1], func=AF.Ln,
                                               scale=1.0 / D, bias=epsb),
                 reads=[("st", col)], writes=[("st", col + 1)])
            B.op("act", R.activation(out=small[:, col + 2:col + 3], in_=small[:, col + 1:col + 2],
                                               func=AF.Exp, scale=-0.5),
                 reads=[("st", col + 1)], writes=[("st", col + 2)])
            return ("st", col + 2), small[:, col + 2:col + 3]

        def front_a(T):
            xs, hs = T % 3, T % 2
            B.dma("sp", ("xt", xs), xt[xs], xv[T * 128:(T + 1) * 128, :], writes=[("xt", xs)])
            rkey, rstd = rms_stats(xt[xs], ("xt", xs), 8 + 4 * (T % 2))
            B.op("dve", R.scalar_tensor_tensor(out=hb[hs], in0=xt[xs], scalar=rstd, in1=gbc,
                                               op0=ALU.mult, op1=ALU.mult),
                 reads=[("xt", xs), rkey, "gbc"], writes=[("hb", hs)])

        def front_b(T):
            g, tt = divmod(T, 4)
            hs, ts, gs = T % 2, T % 2, g % 2
            tpv = psb16(2 * ts, 2)
            for dc in range(16):
                B.op("pe", R.transpose(tpv[:, dc * 128:(dc + 1) * 128],
                                       hb[hs][:, dc * 128:(dc + 1) * 128], ident),
                     reads=[("hb", hs)], writes=[("ps", 2 * ts + dc // 8)])
            B.op("act", R.activation(out=hT[gs][:, 0:8, tt * 128:(tt + 1) * 128],
                                     in_=tpv[:, 0:1024].rearrange("p (c n) -> p c n", c=8), func=AF.Copy),
                 reads=[("ps", 2 * ts)], writes=[("hT", gs, tt, 0)])
            B.op("dve", R.tensor_copy(out=hT[gs][:, 8:16, tt * 128:(tt + 1) * 128],
                                      in_=tpv[:, 1024:2048].rearrange("p (c n) -> p c n", c=8)),
                 reads=[("ps", 2 * ts + 1)], writes=[("hT", gs, tt, 1)])
            if T + 2 < NT:
                front_a(T + 2)

        chain_no = [0]

        def backend_chains(g):
            gs, ks, vs = g % 2, g % 2, g % 2
            chains = []

            def kchain(fc):
                bk = 4 + chain_no[0] % 4
                chain_no[0] += 1
                for dc in range(16):
                    B.op("pe", R.matmul(ps[:, bk, :], lhsT=Wkv[:, dc, fc * 128:(fc + 1) * 128],
                                                         rhs=hT[gs][:, dc, :], start=(dc == 0), stop=(dc == 15)),
                         reads=hT_keys(gs) + ["W"], writes=[("ps", bk)])
                if fc % 2 == 0:
                    B.op("act", R.activation(out=kst[ks][:, fc, :], in_=ps[:, bk, :], func=AF.Copy),
                         reads=[("ps", bk)], writes=[("kst", ks, fc)])
                else:
                    B.op("dve", R.tensor_copy(out=kst[ks][:, fc, :], in_=ps[:, bk, :]),
                         reads=[("ps", bk)], writes=[("kst", ks, fc)])
                if fc == 7:
                    B.dma("pool", ("kst", ks), KTd[:, :, g * 512:(g + 1) * 512].rearrange("h p t -> p h t"), kst[ks],
                          reads=[("kst", ks, f) for f in range(8)])

            def vchain(tt, hf):
                bk = 4 + chain_no[0] % 4
                chain_no[0] += 1
                for dc in range(16):
                    B.op("pe", R.matmul(ps[:, bk, :], lhsT=hT[gs][:, dc, tt * 128:(tt + 1) * 128],
                                                         rhs=Wkv[:, dc, 1024 + hf * 512:1024 + (hf + 1) * 512],
                                                         start=(dc == 0), stop=(dc == 15)),
                         reads=[("hT", gs, tt, 0), ("hT", gs, tt, 1), "W"], writes=[("ps", bk)])
                if hf == 0:
                    B.op("act", R.activation(out=vst[vs][:, tt, 0:512], in_=ps[:, bk, :], func=AF.Copy),
                         reads=[("ps", bk)], writes=[("vst", vs, tt, hf)])
                else:
                    B.op("dve", R.tensor_copy(out=vst[vs][:, tt, 512:1024], in_=ps[:, bk, :]),
                         reads=[("ps", bk)], writes=[("vst", vs, tt, hf)])
                if tt == 3 and hf == 1:
                    for t4 in range(4):
                        B.dma("pool", ("vst", vs),
                              Vd[:, :, 4 * g + t4, :].rearrange("h p d -> p h d"),
                              vst[vs][:, t4, :].rearrange("p (h d) -> p h d", h=8),
                              reads=[("vst", vs, t, f) for t in range(4) for f in range(2)])
                    if g % 4 == 3:
                        B.dma("pool", ("hTd", gs), hTd[g // 4], hT[gs], reads=hT_keys(gs))

            for fc in range(8):
                chains.append(lambda fc=fc: kchain(fc))
            for tt in range(4):
                for hf in range(2):
                    chains.append(lambda tt=tt, hf=hf: vchain(tt, hf))
            return chains

        front_a(0)
        front_a(1)
        for T in range(4):
            front_b(T)
        for g in range(NG):
            chains = backend_chains(g)
            for ci, ch in enumerate(chains):
                ch()
                if ci % 4 == 3 and g + 1 < NG:
                    front_b(4 * (g + 1) + ci // 4)
        B.barrier()

        mem.ptr = pmark
        Wqg = mem.alloc([16, 2048], BF16)
        hT = [mem.alloc([16, 512], BF16) for _ in range(2)]
        qst = [mem.alloc([2, 8, 512], BF16) for _ in range(2)]
        gst = [mem.alloc([512], F32) for _ in range(2)]
        th = [mem.alloc([512], F32) for _ in range(2)]
        for dc in range(16):
            B.dma("pool", "w", Wqg[:, dc, 0:1024], w_in[dc * 128:(dc + 1) * 128, 3072:4096], writes=["W"])
            B.dma("pool", "w", Wqg[:, dc, 1024:2048], w_in[dc * 128:(dc + 1) * 128, 6144:7168], writes=["W"])
        for s in range(2):
            B.op("dve", R.memset(qst[s].rearrange("p a b c -> p (a b c)"), 0.0),
                 writes=[("qst", s, m, f) for m in range(2) for f in range(8)])
        cno = 0
        tno = 0
        for i in range(NSLOT):
            hs = i % 2
            qs = i % 2
            B.dma("sp", ("hTl", hs), hT[hs], hTd[i], writes=[("hT", hs)])
            for fc in range(8):
                bk = cno % 4
                cno += 1
                for dc in range(16):
                    B.op("pe", R.matmul(ps[:, bk, :], lhsT=Wqg[:, dc, fc * 128:(fc + 1) * 128],
                                                                      rhs=hT[hs][:, dc, :], start=(dc == 0), stop=(dc == 15)),
                         reads=[("hT", hs), "W"], writes=[("ps", bk)])
                B.op("act", R.activation(out=qst[qs][0:64, 0, fc, :], in_=ps[0:64, bk, :], func=AF.Copy),
                     reads=[("ps", bk)], writes=[("qst", qs, 0, fc)])
                B.op("dve", R.tensor_copy(out=qst[qs][64:128, 1, fc, :], in_=ps[64:128, bk, :]),
                     reads=[("ps", bk)], writes=[("qst", qs, 1, fc)])
            for m2 in range(2):
                B.dma("pool", ("qst", qs), QTd[:, m2, :, i * 512:(i + 1) * 512].rearrange("h p t -> p h t"),
                      qst[qs][:, m2, :, :], reads=[("qst", qs, m, f) for m in range(2) for f in range(8)])
            for hh in range(8):
                gsl = hh % 2
                bk = 4 + cno % 4
                cno += 1
                tsl = tno % 2
                tno += 1
                for dc in range(16):
                    B.op("pe", R.matmul(ps[:, bk, :], lhsT=Wqg[:, dc, 1024 + hh * 128:1024 + (hh + 1) * 128],
                                        rhs=hT[hs][:, dc, :], start=(dc == 0), stop=(dc == 15)),
                         reads=[("hT", hs), "W"], writes=[("ps", bk)])
                B.op("act", R.activation(out=th[tsl], in_=ps[:, bk, :], func=AF.Tanh, scale=0.5),
                     reads=[("ps", bk)], writes=[("th", tsl)])
                B.op("dve", R.scalar_tensor_tensor(out=gst[gsl], in0=th[tsl], scalar=1.0, in1=ps[:, bk, :],
                                                   op0=ALU.add, op1=ALU.mult),
                     reads=[("th", tsl), ("ps", bk)], writes=[("gst", gsl)])
                B.dma("pool", ("gst", gsl), GTd[hh, :, i * 512:(i + 1) * 512], gst[gsl], reads=[("gst", gsl)])
        B.barrier()

        mem.ptr = pmark
        Wsg = mem.alloc([16, 3072], BF16)
        hT = [mem.alloc([16, 512], BF16) for _ in range(2)]
        wnat = mem.alloc([8, 128], F32)
        identf = mem.alloc([128], F32)
        smk = mem.alloc([128], F32)
        bnat = mem.alloc([128], F32)
        wT = mem.alloc([8, 128], BF16)
        bT = mem.alloc([8], F32)
        lngb = mem.alloc([1024], F32)
        lnbb = mem.alloc([1024], F32)
        gu = mem.alloc([1024], F32)
        gv = mem.alloc([1024], F32)
        t1 = mem.alloc([1024], F32)
        vn = mem.alloc([1024], BF16)
        tha = mem.alloc([1024], F32)
        sa2 = mem.alloc([1024], F32)
        aa = mem.alloc([1024], F32)
        aout = mem.alloc([1024], BF16)
        bst = mem.alloc([2, 6], F32)
        yst = [mem.alloc([8, 512], BF16) for _ in range(2)]
        for dc in range(16):
            B.dma("pool", "w", Wsg[:, dc, :], w_in[dc * 128:(dc + 1) * 128, 0:3072], writes=["W"])
        B.dma("sp", "pl", wnat, sgu_w.rearrange("g t s -> t g s"), writes=["wnat"])
        B.dma("sp", "pl", identf, c_identf, writes=["identf"])
        B.dma("sp", "pl", smk, c_sgumask, writes=["smk"])
        B.dma("sp", "pl", bnat[0:8, :], sgu_b, writes=["bnat"])
        B.dma("sp", "pl", lngb, bcast(ln_g, 1024), writes=["lngb"])
        B.dma("sp", "pl", lnbb, bcast(ln_b, 1024), writes=["lnbb"])
        B.barrier()
        for g8 in range(8):
            bk = g8 % 4
            B.op("pe", R.transpose(ps[:, bk, 0:128], wnat[:, g8, :], identf),
                 reads=["wnat", "identf"], writes=[("ps", bk)])
            B.op("dve", R.tensor_tensor(out=wT[:, g8, :], in0=ps[:, bk, 0:128], in1=smk, op=ALU.mult),
                 reads=[("ps", bk), "smk"], writes=[("wT", g8)])
        B.op("pe", R.transpose(ps[:, 4, 0:8], bnat[0:8, :], identf[0:8, 0:8]),
             reads=["bnat", "identf"], writes=[("ps", 4)])
        B.op("dve", R.tensor_copy(out=bT, in_=ps[:, 4, 0:8]), reads=[("ps", 4)], writes=["bT"])
        B.barrier()

        for i in range(NSLOT):
            hs = i % 2
            ys = i % 2
            B.dma("sp", ("hTl", hs), hT[hs], hTd[i], writes=[("hT", hs)])
            for tt in range(4):
                for cb in range(6):
                    for dc in range(16):
                        B.op("pe", R.matmul(
                            ps[:, cb, :], lhsT=hT[hs][:, dc, tt * 128:(tt + 1) * 128],
                            rhs=Wsg[:, dc, cb * 512:(cb + 1) * 512], start=(dc == 0), stop=(dc == 15)),
                            reads=[("hT", hs), "W"], writes=[("ps", cb)])
                B.op("act", R.activation(out=gu.rearrange("p (a b) -> p a b", a=2), in_=ps[:, 0:2, :],
                                                   func=AF.Gelu_apprx_tanh),
                     reads=[("ps", 0), ("ps", 1)], writes=["gu"])
                B.op("act", R.activation(out=gv.rearrange("p (a b) -> p a b", a=2), in_=ps[:, 2:4, :],
                                                   func=AF.Gelu_apprx_tanh),
                     reads=[("ps", 2), ("ps", 3)], writes=["gv"])
                B.op("act", R.activation(out=tha.rearrange("p (a b) -> p a b", a=2), in_=ps[:, 4:6, :],
                                                   func=AF.Tanh, scale=0.5),
                     reads=[("ps", 4), ("ps", 5)], writes=["tha"])
                B.op("dve", R.bn_stats(out=bst[:, 0, :], in_=gv[:, 0:512]), reads=["gv"], writes=["bst0"])
                B.op("dve", R.bn_stats(out=bst[:, 1, :], in_=gv[:, 512:1024]), reads=["gv"], writes=["bst1"])
                B.op("dve", R.bn_aggr(out=small[:, 16:18], in_=bst.rearrange("p a b -> p (a b)")),
                     reads=["bst0", "bst1"], writes=["mv"])
                B.op("act", R.activation(out=small[:, 18:19], in_=small[:, 17:18], func=AF.Ln, bias=epsb, scale=1.0),
                     reads=["mv"], writes=["lnv"])
                B.op("act", R.activation(out=small[:, 19:20], in_=small[:, 18:19], func=AF.Exp, scale=-0.5),
                     reads=["lnv"], writes=["lrs"])
                B.op("dve", R.tensor_scalar(out=t1, in0=gv, scalar1=small[:, 16:17], scalar2=small[:, 19:20],
                                                      op0=ALU.subtract, op1=ALU.mult),
                     reads=["gv", "mv", "lrs"], writes=["t1"])
                B.op("dve", R.tensor_tensor(out=t1, in0=t1, in1=lngb, op=ALU.mult), reads=["t1"], writes=["t1"])
                B.op("dve", R.tensor_tensor(out=vn, in0=t1, in1=lnbb, op=ALU.add), reads=["t1"], writes=["vn"])
                for g8 in range(8):
                    B.op("pe", R.matmul(ps[:, 6 + g8 // 4, (g8 % 4) * 128:(g8 % 4 + 1) * 128],
                                                         lhsT=wT[:, g8, :], rhs=vn[:, g8 * 128:(g8 + 1) * 128],
                                                         start=True, stop=True),
                         reads=["vn"], writes=[("ps", 6 + g8 // 4)])
                for hf in range(2):
                    B.op("dve", R.scalar_tensor_tensor(
                        out=sa2[:, hf * 512:(hf + 1) * 512], in0=tha[:, hf * 512:(hf + 1) * 512], scalar=1.0,
                        in1=ps[:, 4 + hf, :], op0=ALU.add, op1=ALU.mult),
                        reads=["tha", ("ps", 4 + hf)], writes=[("sa2", hf)])
                for g8 in range(8):
                    B.op("dve", R.scalar_tensor_tensor(
                        out=aa[:, g8 * 128:(g8 + 1) * 128], in0=ps[:, 6 + g8 // 4, (g8 % 4) * 128:(g8 % 4 + 1) * 128],
                        scalar=bT[:, g8:g8 + 1], in1=gu[:, g8 * 128:(g8 + 1) * 128], op0=ALU.add, op1=ALU.mult),
                        reads=[("ps", 6 + g8 // 4), "gu"], writes=[("aa", g8)])
                B.op("dve", R.scalar_tensor_tensor(out=aout, in0=aa, scalar=0.5, in1=sa2, op0=ALU.mult, op1=ALU.mult),
                     reads=[("aa", g8) for g8 in range(8)] + [("sa2", 0), ("sa2", 1)], writes=["aout"])
                tpb = psb16(0, 1)
                for ec in range(8):
                    B.op("pe", R.transpose(tpb[:, ec * 128:(ec + 1) * 128], aout[:, ec * 128:(ec + 1) * 128], ident),
                         reads=["aout"], writes=[("ps", 0)])
                B.op("act", R.activation(out=yst[ys][:, :, tt * 128:(tt + 1) * 128],
                                                          in_=tpb.rearrange("p (c n) -> p c n", c=8), func=AF.Copy),
                     reads=[("ps", 0)], writes=[("yst", ys, tt)])
            B.dma("pool", ("yst", ys), YTd[0:8, :, i * 512:(i + 1) * 512].rearrange("c p t -> p c t"), yst[ys],
                  reads=[("yst", ys, t) for t in range(4)])
        B.barrier()

        mem.ptr = pmark
        kT = [mem.alloc([NT * 128], BF16) for _ in range(2)]
        vA = [mem.alloc([NT, 128], BF16) for _ in range(2)]
        bm2 = [mem.alloc([5, 512], F32) for _ in range(2)]
        qT = [mem.alloc([2, 512], BF16) for _ in range(2)]
        gtT = [mem.alloc([512], F32) for _ in range(2)]
        pt = [mem.alloc([2, 512], BF16) for _ in range(3)]
        tmpf = mem.alloc([2, 512], F32)
        rsd = [mem.alloc([2, 512], F32) for _ in range(2)]
        sh = mem.alloc([512], F32)
        zb = mem.alloc([2, 128], BF16)
        sel = mem.alloc([2, 128], F32)
        onesf = mem.alloc([128], F32)
        kvf = mem.alloc([NT], F32)
        nlam = mem.alloc([1], F32)
        sbs = mem.alloc([2, 512], F32)
        ea = mem.alloc([512], F32)
        eb = mem.alloc([512], F32)
        eo = mem.alloc([512], F32)
        ep = mem.alloc([512], F32)
        ystc = [mem.alloc([512], BF16) for _ in range(2)]

        B.op("dve", R.memset(onesf, 1.0), writes=["onesf"])
        B.op("dve", R.memset(zb.rearrange("p a b -> p (a b)"), 0.0), writes=["zb"])
        B.op("dve", R.memset(zb[:, 0, 0:64], 1.0), reads=["zb"], writes=["zb"])
        B.op("dve", R.memset(zb[:, 1, 64:128], 1.0), reads=["zb"], writes=["zb"])
        B.op("dve", R.memset(sel.rearrange("p a b -> p (a b)"), 0.0), writes=["sel"])
        B.op("dve", R.memset(sel[0:1, 0, :], 1.0), reads=["sel"], writes=["sel"])
        B.op("dve", R.memset(sel[64:65, 1, :], 1.0), reads=["sel"], writes=["sel"])
        B.op("dve", R.tensor_copy(out=kvf, in_=kval), reads=["kval"], writes=["kvf"])
        B.op("dve", R.tensor_scalar(out=nlam, in0=lamb, scalar1=-1.0, scalar2=None, op0=ALU.mult),
             reads=["lamb"], writes=["nlam"])

        def flat(t):
            return t.rearrange("p a b -> p (a b)")

        def load_head(h):
            hs = h % 2
            B.dma("pool", ("kT", hs), kT[hs], KTd[h], writes=[("kT", hs)])
            B.dma("pool", ("vA", hs), vA[hs], Vd[h], writes=[("vA", hs)])
            B.dma("pool", ("bm", hs), flat(bm2[hs]), BMd[h], writes=[("bm", hs)])

        pcount = [0]
        slotc = [0]
        pending = []

        def epilogue_parts(h, i, sl, gs, ys):
            def copies():
                B.op("dve", R.tensor_copy(out=ea, in_=ps[:, 4, :]), reads=[("ps", 4)], writes=["ea"])
                B.op("dve", R.tensor_copy(out=eb, in_=ps[:, 5, :]), reads=[("ps", 5)], writes=["eb"])
                B.op("dve", R.tensor_copy(out=sh, in_=ps[:, 6, :]), reads=[("ps", 6)], writes=["sh"])

            def sums_mm(m):
                def f():
                    B.op("pe", R.matmul(ps[:, 7, :], lhsT=onesf, rhs=rsd[sl][:, m, :], start=True, stop=False),
                         reads=[("rsd", sl), "onesf"], writes=[("ps", 7)])
                    B.op("pe", R.matmul(ps[:, 7, :], lhsT=sel[:, m, :], rhs=sh, start=False, stop=True),
                         reads=["sh", "sel"], writes=[("ps", 7)])
                return f

            def sums_scale(m):
                def f():
                    B.op("dve", R.tensor_scalar(out=sbs[:, m, :], in0=ps[:, 7, :], scalar1=2.0 ** -14, scalar2=None,
                                                op0=ALU.mult),
                         reads=[("ps", 7)], writes=[("sbs", m)])
                return f

            def combine():
                B.op("dve", R.tensor_tensor(out=ea, in0=ea, in1=sbs[:, 1, :], op=ALU.mult), reads=["ea", ("sbs", 1)], writes=["ea"])
                B.op("dve", R.tensor_tensor(out=eb, in0=eb, in1=sbs[:, 0, :], op=ALU.mult), reads=["eb", ("sbs", 0)], writes=["eb"])
                B.op("dve", R.scalar_tensor_tensor(out=eo, in0=eb, scalar=nlam, in1=ea, op0=ALU.mult, op1=ALU.add),
                     reads=["ea", "eb", "nlam"], writes=["eo"])
                B.op("dve", R.tensor_tensor(out=eb, in0=eo, in1=eo, op=ALU.mult), reads=["eo"], writes=["eb"])
                B.op("dve", R.tensor_tensor(out=ep, in0=sbs[:, 0, :], in1=sbs[:, 1, :], op=ALU.mult),
                     reads=[("sbs", 0), ("sbs", 1)], writes=["ep"])
                B.op("dve", R.scalar_tensor_tensor(out=ep, in0=ep, scalar=EPS, in1=ep, op0=ALU.mult, op1=ALU.mult),
                     reads=["ep"], writes=["ep"])

            def ms_mm():
                B.op("pe", R.matmul(ps[:, 7, :], lhsT=onesf, rhs=eb, start=True, stop=True),
                     reads=["eb", "onesf"], writes=[("ps", 7)])

            def ms_add():
                B.op("dve", R.scalar_tensor_tensor(out=ea, in0=ps[:, 7, :], scalar=1.0 / 128, in1=ep, op0=ALU.mult, op1=ALU.add),
                     reads=[("ps", 7), "ep"], writes=["ea"])

            def rstd():
                B.op("act", R.activation(out=ea, in_=ea, func=AF.Ln), reads=["ea"], writes=["ea"])
                B.op("act", R.activation(out=ea, in_=ea, func=AF.Exp, scale=-0.5), reads=["ea"], writes=["ea"])

            def final():
                B.op("dve", R.scalar_tensor_tensor(out=eo, in0=eo, scalar=sgl, in1=ea, op0=ALU.mult, op1=ALU.mult),
                     reads=["eo", "ea", "sgl"], writes=["eo"])
                B.op("dve", R.tensor_tensor(out=ystc[ys], in0=eo, in1=gtT[gs], op=ALU.mult),
                     reads=["eo", ("gtT", gs)], writes=[("ystc", ys)])
                B.dma("pool", ("ystc", ys), YTd[8 + h, :, i * 512:(i + 1) * 512], ystc[ys], reads=[("ystc", ys)])
            return [(-1, copies), (0, sums_mm(0)), (1, sums_scale(0)), (2, sums_mm(1)), (3, sums_scale(1)), (3, combine),
                    (6, ms_mm), (7, ms_add), (9, rstd), (11, final)]

        zbv = mem.alloc([12, 2, 128], BF16)
        for v in range(12):
            B.op("dve", R.tensor_scalar(out=zbv[:, v, :, :], in0=zb, scalar1=kvf[:, v:v + 1], scalar2=None, op0=ALU.mult),
                 reads=["zb", "kvf"], writes=[("zbv", v)])
        tmpf2 = [tmpf, mem.alloc([2, 512], F32)]

        def run_pending(v):
            while pending and pending[0][0] <= v:
                pending.pop(0)[1]()

        load_head(0)
        for h in range(NH):
            hs = h % 2
            bm = bm2[hs]
            for i in range(NSLOT):
                sc = slotc[0]
                slotc[0] += 1
                qs = gs = sl = ys = sc % 2
                B.dma("sp", ("qT", qs), qT[qs], QTd[h, :, :, i * 512:(i + 1) * 512].rearrange("m p t -> p m t"),
                      writes=[("qT", qs)])
                B.dma("sp", ("gtT", gs), gtT[gs], GTd[h, :, i * 512:(i + 1) * 512], writes=[("gtT", gs)])
                if i == 0 and h + 1 < NH:
                    load_head(h + 1)
                n = 16 * (i + 1)

                def emit_S(v, pu):
                    b = pu % 2
                    for m in range(2):
                        B.op("pe", R.matmul(ps[:, 2 * b + m, :], lhsT=kT[hs][:, v * 128:(v + 1) * 128], rhs=qT[qs][:, m, :],
                                            start=True, stop=True),
                             reads=[("kT", hs), ("qT", qs)], writes=[("ps", 2 * b + m)])

                emit_S(0, pcount[0])
                emit_S(1, pcount[0] + 1)
                psum_started = False
                for v in range(n):
                    pu = pcount[0]
                    pcount[0] += 1
                    b = pu % 2
                    s3 = pu % 3
                    r = v - (n - 4)
                    if r < -1:
                        B.op("act", R.activation(out=pt[s3], in_=ps[:, 2 * b:2 * b + 2, :], func=AF.Exp, scale=0.125,
                                                 bias=chb[:, h:h + 1]),
                             reads=[("ps", 2 * b), ("ps", 2 * b + 1)], writes=[("pt", s3)])
                    else:
                        tf = tmpf2[(r + 1) % 2]
                        tk = (r + 1) % 2
                        for m in range(2):
                            B.op("dve", R.scalar_tensor_tensor(out=tf[:, m, :], in0=ps[:, 2 * b + m, :], scalar=0.125,
                                                               in1=bm[:, r + 1, :], op0=ALU.mult, op1=ALU.add),
                                 reads=[("ps", 2 * b + m), ("bm", hs)], writes=[("tmpf", tk, m)])
                        B.op("act", R.activation(out=pt[s3], in_=tf, func=AF.Exp),
                             reads=[("tmpf", tk, 0), ("tmpf", tk, 1)], writes=[("pt", s3)])
                    if v + 2 < n:
                        emit_S(v + 2, pu + 2)
                    for m in range(2):
                        B.op("pe", R.matmul(ps[:, 4 + m, :], lhsT=vA[hs][:, v, :], rhs=pt[s3][:, m, :],
                                            start=(v == 0), stop=(v == n - 1)),
                             reads=[("pt", s3), ("vA", hs)], writes=[("ps", 4 + m)])
                    if (v % 2 == 1 and v > 0) or (v >= n - 5):
                        for m in range(2):
                            zt = zbv[:, v, m, :] if v < 12 else zb[:, m, :]
                            B.op("pe", R.matmul(ps[:, 6, :], lhsT=zt, rhs=pt[s3][:, m, :],
                                                start=(not psum_started and m == 0), stop=(v == n - 1 and m == 1)),
                                 reads=[("pt", s3), "zb"] + ([("zbv", v)] if v < 12 else []), writes=[("ps", 6)])
                        psum_started = True
                    elif v == 0:
                        B.op("dve", R.tensor_scalar(out=flat(rsd[sl]), in0=flat(pt[s3]), scalar1=kvf[:, 0:1], scalar2=None,
                                                    op0=ALU.mult),
                             reads=[("pt", s3), "kvf"], writes=[("rsd", sl)])
                    elif v < 12:
                        B.op("dve", R.scalar_tensor_tensor(out=flat(rsd[sl]), in0=flat(pt[s3]), scalar=kvf[:, v:v + 1],
                                                           in1=flat(rsd[sl]), op0=ALU.mult, op1=ALU.add),
                             reads=[("pt", s3), "kvf", ("rsd", sl)], writes=[("rsd", sl)])
                    else:
                        B.op("dve", R.tensor_tensor(out=flat(rsd[sl]), in0=flat(rsd[sl]), in1=flat(pt[s3]), op=ALU.add),
                             reads=[("pt", s3), ("rsd", sl)], writes=[("rsd", sl)])
                    run_pending(v)
                run_pending(10 ** 9)
                for trig, fn in epilogue_parts(h, i, sl, gs, ys):
                    if trig < 0:
                        fn()
                    else:
                        pending.append((trig, fn))
        run_pending(10 ** 9)
        B.barrier()

        mem.ptr = pmark
        Wo = mem.alloc([16, 2048], BF16)
        fgb = mem.alloc([D], F32)
        yT = [mem.alloc([16, 512], BF16) for _ in range(2)]
        xo = [mem.alloc([D], F32) for _ in range(2)]
        rr = [mem.alloc([D], F32) for _ in range(2)]
        ot = [mem.alloc([D], F32) for _ in range(2)]
        sqj = mem.alloc([D], BF16)
        for dc in range(16):
            B.dma("pool", "w", Wo[:, dc, :], w_out[dc * 128:(dc + 1) * 128, :], writes=["W"])
        B.dma("sp", "pl", fgb, bcast(final_g, D), writes=["fgb"])
        tcount = 0
        for i in range(NSLOT):
            ysl = i % 2
            B.dma("sp", ("yT", ysl), yT[ysl], YTd[:, :, i * 512:(i + 1) * 512].rearrange("c p t -> p c t"),
                  writes=[("yT", ysl)])
            for tt in range(4):
                k = tcount % 2
                tcount += 1
                Tv = 16 * i + 12 + tt
                B.dma("sp", ("xo", k), xo[k], xv[Tv * 128:(Tv + 1) * 128, :], writes=[("xo", k)])
                for nb in range(4):
                    bk = 4 * k + nb
                    for ec in range(16):
                        B.op("pe", R.matmul(
                            ps[:, bk, :], lhsT=yT[ysl][:, ec, tt * 128:(tt + 1) * 128],
                            rhs=Wo[:, ec, nb * 512:(nb + 1) * 512], start=(ec == 0), stop=(ec == 15)),
                            reads=[("yT", ysl), "W"], writes=[("ps", bk)])
                    B.op("dve", R.tensor_tensor(out=rr[k][:, nb * 512:(nb + 1) * 512], in0=ps[:, bk, :],
                                                                      in1=xo[k][:, nb * 512:(nb + 1) * 512], op=ALU.add),
                         reads=[("ps", bk), ("xo", k)], writes=[("rr", k, nb)])
                col = 8 + 4 * k
                B.op("act", R.activation(out=sqj, in_=rr[k], func=AF.Square, accum_out=small[:, col:col + 1]),
                     reads=[("rr", k, nb) for nb in range(4)], writes=["sqj", ("st", col)])
                B.op("act", R.activation(out=small[:, col + 1:col + 2], in_=small[:, col:col + 1], func=AF.Ln,
                                                   scale=1.0 / D, bias=epsb),
                     reads=[("st", col)], writes=[("st", col + 1)])
                B.op("act", R.activation(out=small[:, col + 2:col + 3], in_=small[:, col + 1:col + 2], func=AF.Exp,
                                                   scale=-0.5),
                     reads=[("st", col + 1)], writes=[("st", col + 2)])
                B.op("dve", R.scalar_tensor_tensor(out=ot[k], in0=rr[k], scalar=small[:, col + 2:col + 3], in1=fgb,
                                                             op0=ALU.mult, op1=ALU.mult),
                     reads=[("rr", k, nb) for nb in range(4)] + [("st", col + 2), "fgb"], writes=[("ot", k)])
                B.dma("pool", ("ot", k), out_own[(4 * i + tt) * 128:(4 * i + tt + 1) * 128, :], ot[k], reads=[("ot", k)])
        B.barrier()

        def replay(e, stream):
            for ent in stream:
                if ent[0] == "wait":
                    e.wait_ge(ent[1], ent[2])
                else:
                    _, name, args, kwargs, sem, inc = ent
                    getattr(e, name)(*args, **kwargs).then_inc(sem, inc)

        with nc.Block() as block:
            @block.tensor
            def _(e):
                replay(e, B.streams["pe"])

            @block.scalar
            def _(e):
                replay(e, B.streams["act"])

            @block.vector
            def _(e):
                replay(e, B.streams["dve"])

            @block.gpsimd
            def _(e):
                replay(e, B.streams["pool"])

            @block.sync
            def _(e):
                replay(e, B.streams["sp"])
        print("instr counts", {k: len(v) for k, v in B.streams.items()}, "sems", len(B.dsem))
    return nc


def _t5_bucket_np(rel):
    try:
        import jax
        import jax.numpy as jnp
        with jax.default_device(jax.devices("cpu")[0]):
            rel_j = jnp.asarray(rel, dtype=jnp.int32)
            nb = 16
            max_exact = 8
            side = jnp.where(rel_j > 0, nb, 0)
            n = jnp.abs(rel_j)
            nf = jnp.maximum(n, 1).astype(jnp.float32)
            large = max_exact + (jnp.log(nf / max_exact) / math.log(128 / max_exact) * (nb - max_exact)).astype(jnp.int32)
            large = jnp.minimum(large, nb - 1)
            return np.asarray(side + jnp.where(n < max_exact, n, large)).astype(np.int64)
    except Exception:
        rel = np.asarray(rel, dtype=np.int64)
        side = np.where(rel > 0, 16, 0)
        n = np.abs(rel)
        nf = np.maximum(n, 1).astype(np.float32)
        large = 8 + (np.log(nf / np.float32(8)) / np.float32(math.log(16.0)) * np.float32(8)).astype(np.int32)
        large = np.minimum(large, 15)
        return side + np.where(n < 8, n, large)


_PROG_CACHE = {}


def kernel(x, norm_g, w_in, sgu_ln_g, sgu_ln_b, sgu_w, sgu_b, lambda_q1, lambda_k1,
           lambda_q2, lambda_k2, subln_g, rel_bias, w_out, final_g):
    x = np.asarray(x, dtype=np.float32)
    Bn, S, _ = x.shape
    assert Bn == 2 and S % 2048 == 0
    NSLOT = S // 2048
    NT = 16 * NSLOT
    bf = ml_dtypes.bfloat16

    u = np.arange(NR)
    bucket = _t5_bucket_np(511 - u)
    onehot = np.zeros((32, NR), np.float32)
    onehot[bucket, u] = 1.0
    kk = np.arange(128)[:, None]
    qq = np.arange(512)[None, :]
    masks = np.zeros((128, 4, 512), np.float32)
    for r in range(4):
        allowed = ((128 * r + kk) // 64) <= (qq // 64)
        masks[:, r, :] = np.where(allowed, 0.0, NEG)
    ss_ = np.arange(128)[:, None]
    tt_ = np.arange(128)[None, :]
    sgumask = ((ss_ // 64) <= (tt_ // 64)).astype(np.float32)
    ident = np.eye(128, dtype=np.float32)
    rev = np.ascontiguousarray(ident[::-1])
    lam_in = np.stack([np.asarray(a, np.float32).reshape(64) for a in (lambda_q1, lambda_k1, lambda_q2, lambda_k2)])

    common = {
        "w_in": np.ascontiguousarray(np.asarray(w_in, np.float32).reshape(D, DIN)),
        "w_out": np.ascontiguousarray(np.asarray(w_out, np.float32).reshape(D, D)),
        "norm_g": np.asarray(norm_g, np.float32).reshape(D),
        "final_g": np.asarray(final_g, np.float32).reshape(D),
        "sgu_ln_g": np.asarray(sgu_ln_g, np.float32).reshape(1024),
        "sgu_ln_b": np.asarray(sgu_ln_b, np.float32).reshape(1024),
        "sgu_w": np.ascontiguousarray(np.asarray(sgu_w, np.float32).reshape(8, 128, 128)),
        "sgu_b": np.ascontiguousarray(np.asarray(sgu_b, np.float32).reshape(8, 128)),
        "lam_in": lam_in,
        "subln_g": np.asarray(subln_g, np.float32).reshape(128),
        "rel_bias": np.ascontiguousarray(np.asarray(rel_bias, np.float32).reshape(32, 8)),
        "c_ident": ident.astype(bf),
        "c_identf": ident,
        "c_rev": rev,
        "c_onehot": onehot,
        "c_masks": np.ascontiguousarray(masks.reshape(128, 4 * 512)),
        "c_sgumask": sgumask,
    }
    in_maps = []
    for c in range(8):
        b, j = divmod(c, 4)
        npad = 12 - 4 * j
        xvirt = np.zeros((NT * 128, D), np.float32)
        xvirt[npad * 128:] = x[b, :(NT - npad) * 128]
        kvalid = np.zeros((128, NT), np.float32)
        kvalid[:, npad:] = 1.0
        m = dict(common)
        m["xv"] = xvirt
        m["c_kvalid"] = kvalid.astype(bf)
        in_maps.append(m)

    if NSLOT not in _PROG_CACHE:
        _PROG_CACHE[NSLOT] = build_program(NSLOT)
    nc = _PROG_CACHE[NSLOT]
    res = run_bass_kernel_spmd(nc, in_maps, core_ids=list(range(8)))
    out = np.empty((Bn, S, D), np.float32)
    for c in range(8):
        b, j = divmod(c, 4)
        o = np.asarray(res.results[c]["out_own"], np.float32)
        for i in range(NSLOT):
            q0 = (4 * i + j) * 512
            out[b, q0:q0 + 512] = o[i * 512:(i + 1) * 512]
    return out
```

```python
import math
from contextlib import ExitStack

import numpy as np
import ml_dtypes

import concourse.bass as bass
import concourse.mybir as mybir
from concourse.bass_utils import run_bass_kernel_spmd

F32 = mybir.dt.float32
BF16 = mybir.dt.bfloat16
AF = mybir.ActivationFunctionType
ALU = mybir.AluOpType

D = 2048
DIN = 7168
NH = 8
NR = 1151
EPS = 1e-6
NEG = -30000.0
ENG = ("pe", "act", "dve", "pool", "sp")
NDS = 90


class _Rec:
    def __getattr__(self, name):
        return lambda *a, **k: (name, a, k)


R = _Rec()


class Builder:
    def __init__(self, nc, stack):
        self.nc = nc
        self.streams = {e: [] for e in ENG}
        self.esem = {e: stack.enter_context(nc.semaphore("s_" + e)) for e in ENG}
        self.cnt = {e: 0 for e in ENG}
        self.pool = [stack.enter_context(nc.semaphore("d%d" % i)) for i in range(NDS)]
        self.dsem = {}
        self.waited = {e: {} for e in ENG}
        self.res = {}

    def _semof(self, key):
        if isinstance(key, str) and key in self.esem:
            return self.esem[key]
        return self.dsem[key][0]

    def _wait(self, eng, ev):
        key, val = ev
        if key == "pe" and eng == "pe":
            return
        w = self.waited[eng]
        if w.get(key, 0) >= val:
            return
        w[key] = val
        sem = self._semof(key)
        self.streams[eng].append(("wait", sem, val))

    def _deps(self, eng, reads, writes):
        for r in reads:
            st = self.res.get(r)
            if st and st[0] is not None:
                self._wait(eng, st[0])
        for w in writes:
            st = self.res.get(w)
            if st:
                if st[0] is not None:
                    self._wait(eng, st[0])
                for k, v in st[1].items():
                    self._wait(eng, (k, v))

    def _update(self, ev, reads, writes):
        for r in reads:
            st = self.res.setdefault(r, [None, {}])
            if st[1].get(ev[0], 0) < ev[1]:
                st[1][ev[0]] = ev[1]
        for w in writes:
            self.res[w] = [ev, {}]

    def op(self, eng, fn, reads=(), writes=()):
        self._deps(eng, reads, writes)
        self.cnt[eng] += 1
        ev = (eng, self.cnt[eng])
        sem = self.esem[eng]
        name, args, kwargs = fn
        self.streams[eng].append(("op", name, args, kwargs, sem, 1))
        self._update(ev, reads, writes)
        return ev

    def dma(self, q, key, out, in_, reads=(), writes=()):
        self._deps(q, reads, writes)
        if key not in self.dsem:
            self.dsem[key] = [self.pool.pop(), 0]
        d = self.dsem[key]
        d[1] += 16
        ev = (key, d[1])
        sem = d[0]
        self.streams[q].append(("op", "dma_start", (), dict(out=out, in_=in_), sem, 16))
        self._update(ev, reads, writes)
        return ev

    def barrier(self):
        evs = [(e, self.cnt[e]) for e in ENG if self.cnt[e] > 0]
        evs += [(k, d[1]) for k, d in self.dsem.items() if d[1] > 0]
        for eng in ENG:
            for ev in evs:
                self._wait(eng, ev)
        self.res = {}


class Mem:
    def __init__(self, big, cap):
        self.big = big
        self.cap = cap
        self.ptr = 0

    def alloc(self, free_shape, dtype):
        esz = 4 if dtype == F32 else 2
        n = 1
        for s in free_shape:
            n *= s
        nb = n * esz
        off = (self.ptr + 63) // 64 * 64
        assert off + nb <= self.cap, ("SBUF overflow", off, nb, self.cap)
        self.ptr = off + nb
        v = self.big[:, off // 2:(off + nb) // 2]
        if dtype == F32:
            v = v.bitcast(F32)
        if len(free_shape) == 2:
            v = v.rearrange("p (a b) -> p a b", a=free_shape[0])
        elif len(free_shape) == 3:
            v = v.rearrange("p (a b c) -> p a b c", a=free_shape[0], b=free_shape[1])
        return v


def build_program(NSLOT):
    NT = 16 * NSLOT
    NG = 4 * NSLOT
    NOWN = NSLOT * 512

    nc = bass.Bass("TRN2", target_bir_lowering=False)

    def din(name, shape, dt=F32):
        return nc.dram_tensor(name, list(shape), dt, kind="ExternalInput").ap()

    def dscr(name, shape, dt):
        return nc.dram_tensor(name, list(shape), dt, kind="Internal").ap()

    xv = din("xv", [NT * 128, D])
    w_in = din("w_in", [D, DIN])
    w_out = din("w_out", [D, D])
    norm_g = din("norm_g", [D])
    final_g = din("final_g", [D])
    ln_g = din("sgu_ln_g", [1024])
    ln_b = din("sgu_ln_b", [1024])
    sgu_w = din("sgu_w", [8, 128, 128])
    sgu_b = din("sgu_b", [8, 128])
    lam_in = din("lam_in", [4, 64])
    subln_g = din("subln_g", [128])
    rel_bias = din("rel_bias", [32, 8])
    c_ident = din("c_ident", [128, 128], BF16)
    c_identf = din("c_identf", [128, 128])
    c_rev = din("c_rev", [128, 128])
    c_onehot = din("c_onehot", [32, NR])
    c_masks = din("c_masks", [128, 4 * 512])
    c_sgumask = din("c_sgumask", [128, 128])
    c_kvalid = din("c_kvalid", [128, NT], BF16)
    out_own = nc.dram_tensor("out_own", [NOWN, D], F32, kind="ExternalOutput").ap()

    KTd = dscr("KTd", [NH, 128, NT * 128], BF16)
    Vd = dscr("Vd", [NH, 128, NT, 128], BF16)
    hTd = dscr("hTd", [NSLOT, 128, 16, 512], BF16)
    QTd = dscr("QTd", [NH, 2, 128, NOWN], BF16)
    GTd = dscr("GTd", [NH, 128, NOWN], F32)
    YTd = dscr("YTd", [16, 128, NOWN], BF16)
    Gd = dscr("Gd", [NH, NR], F32)
    BMd = dscr("BMd", [NH, 128, 5 * 512], F32)

    def bcast(ap1d, n, offset=0):
        return bass.AP(ap1d.tensor, offset, [[0, 128], [1, n]])

    CAP = 210944
    with ExitStack() as stack:
        big = stack.enter_context(nc.sbuf_tensor("big", [128, CAP // 2], BF16))
        ps = stack.enter_context(nc.psum_tensor("ps", [128, 8, 512], F32))
        ps2 = ps.rearrange("p b n -> p (b n)")
        B = Builder(nc, stack)
        mem = Mem(big, CAP)

        def load_wblock(k, Wt, c0, c1, src, s0, key):
            B.dma("pool", ("wb", k), Wt[:, :, c0:c1], src[:, s0:s0 + (c1 - c0)].rearrange("(c p) n -> p c n", p=128),
                  writes=[key])

        def psb16(bank, nbanks=1):
            return ps2[:, bank * 512:(bank + nbanks) * 512].bitcast(BF16)

        ident = mem.alloc([128], BF16)
        epsb = mem.alloc([1], F32)
        chb = mem.alloc([8], F32)
        lamb = mem.alloc([1], F32)
        sgl = mem.alloc([1], F32)
        kval = mem.alloc([NT], BF16)
        small = mem.alloc([64], F32)
        pmark = mem.ptr

        rb = mem.alloc([8], F32)
        oh = mem.alloc([NR], F32)
        rev = mem.alloc([128], F32)
        msk = mem.alloc([4, 512], F32)
        lv = mem.alloc([4, 64], F32)
        lj = mem.alloc([64], F32)
        gsb = mem.alloc([NR], F32)
        Xh = [mem.alloc([512], F32) for _ in range(2)]
        bmst = [mem.alloc([5, 512], F32) for _ in range(2)]

        B.dma("sp", "pl", ident, c_ident, writes=["ident"])
        B.dma("sp", "pl", chb, bcast(rel_bias, 8, 15 * 8), writes=["chb"])
        B.dma("sp", "pl", sgl, bass.AP(subln_g.tensor, 0, [[1, 128], [1, 1]]), writes=["sgl"])
        B.dma("sp", "pl", kval, c_kvalid, writes=["kval"])
        B.dma("sp", "pl", rb[0:32, :], rel_bias, writes=["rb"])
        B.dma("sp", "pl", oh[0:32, :], c_onehot, writes=["oh"])
        B.dma("sp", "pl", rev, c_rev, writes=["rev"])
        B.dma("sp", "pl", msk.rearrange("p a b -> p (a b)"), c_masks, writes=["msk"])
        for k in range(4):
            B.dma("sp", "pl", lv[:, k, :], bcast(lam_in, 64, k * 64), writes=[("lv", k)])
        B.barrier()

        B.op("dve", R.memset(epsb, EPS), writes=["epsb"])
        B.op("dve", R.tensor_scalar(out=sgl, in0=sgl, scalar1=0.4, scalar2=None, op0=ALU.mult),
             reads=["sgl"], writes=["sgl"])
        B.op("dve", R.scalar_tensor_tensor(out=lj, in0=lv[:, 0, :], scalar=1.0, in1=lv[:, 1, :],
                                                     op0=ALU.mult, op1=ALU.mult, accum_out=small[:, 0:1]),
             writes=["lj", "s0"])
        B.op("dve", R.scalar_tensor_tensor(out=lj, in0=lv[:, 2, :], scalar=1.0, in1=lv[:, 3, :],
                                                     op0=ALU.mult, op1=ALU.mult, accum_out=small[:, 1:2]),
             reads=[], writes=["lj", "s1"])
        B.op("act", R.activation(out=small[:, 2:4], in_=small[:, 0:2], func=AF.Exp),
             reads=["s0", "s1"], writes=["s23"])
        B.op("dve", R.tensor_tensor(out=small[:, 4:5], in0=small[:, 2:3], in1=small[:, 3:4],
                                              op=ALU.subtract), reads=["s23"], writes=["s4"])
        B.op("dve", R.tensor_scalar(out=lamb, in0=small[:, 4:5], scalar1=0.2, scalar2=None,
                                              op0=ALU.add), reads=["s4"], writes=["lamb"])
        for ci, (c0, n) in enumerate([(0, 512), (512, 512), (1024, NR - 1024)]):
            B.op("pe", R.matmul(ps[0:8, ci, 0:n], lhsT=rb[0:32, 0:8],
                                                              rhs=oh[0:32, c0:c0 + n], start=True, stop=True),
                 writes=[("ps", ci)])
            B.op("act", R.activation(out=gsb[0:8, c0:c0 + n], in_=ps[0:8, ci, 0:n],
                                                                  func=AF.Copy),
                 reads=[("ps", ci)], writes=[("gsb", ci)])
        B.dma("sp", "gd", Gd, gsb[0:8, :], reads=[("gsb", 0), ("gsb", 1), ("gsb", 2)], writes=["Gd"])
        cnt = 0
        for h in range(NH):
            bs = h % 2
            for r in range(-1, 4):
                xs = cnt % 2
                bk = 4 + cnt % 4
                cnt += 1
                src = bass.AP(Gd.tensor, h * NR + 128 * (3 - r), [[1, 128], [1, 512]])
                B.dma("sp", ("xh", xs), Xh[xs], src, reads=["Gd"], writes=[("xh", xs)])
                B.op("pe", R.matmul(ps[:, bk, :], lhsT=rev, rhs=Xh[xs], start=True, stop=True),
                     reads=[("xh", xs)], writes=[("ps", bk)])
                if r >= 0:
                    B.op("dve", R.tensor_tensor(out=bmst[bs][:, r + 1, :], in0=ps[:, bk, :],
                                                                            in1=msk[:, r, :], op=ALU.add),
                         reads=[("ps", bk)], writes=[("bmst", bs, r)])
                else:
                    B.op("dve", R.tensor_copy(out=bmst[bs][:, 0, :], in_=ps[:, bk, :]),
                         reads=[("ps", bk)], writes=[("bmst", bs, r)])
            B.dma("pool", ("bmst", bs), BMd[h], bmst[bs].rearrange("p a b -> p (a b)"),
                  reads=[("bmst", bs, r) for r in range(-1, 4)])
        B.barrier()

        mem.ptr = pmark
        Wkv = mem.alloc([16, 2048], BF16)
        gbc = mem.alloc([D], F32)
        B.dma("sp", "pl", gbc, bcast(norm_g, D), writes=["gbc"])
        xt = [mem.alloc([D], F32) for _ in range(3)]
        sqj = mem.alloc([D], BF16)
        hb = [mem.alloc([D], BF16) for _ in range(2)]
        hT = [mem.alloc([16, 512], BF16) for _ in range(2)]
        kst = [mem.alloc([8, 512], BF16) for _ in range(2)]
        vst = [mem.alloc([4, 1024], BF16) for _ in range(2)]

        for fc in range(8):
            load_wblock(fc, Wkv, fc * 128, (fc + 1) * 128, w_in, 4096 + fc * 128, ("W", "k", fc))
        for hf in range(2):
            load_wblock(8 + hf, Wkv, 1024 + hf * 512, 1024 + (hf + 1) * 512, w_in, 5120 + hf * 512, ("W", "v", hf))

        def hT_keys(gs):
            return [("hT", gs, t, hf) for t in range(4) for hf in range(2)]

        def rms_stats(src, skey, col):
            B.op("act", R.activation(out=sqj, in_=src, func=AF.Square, accum_out=small[:, col:col + 1]),
                 reads=[skey], writes=["sqj", ("st", col)])
            B.op("act", R.activation(out=small[:, col + 1:col + 2], in_=small[:, col:col + 1], func=AF.Ln,
                                               scale=1.0 / D, bias=epsb),
                 reads=[("st", col)], writes=[("st", col + 1)])
            B.op("act", R.activation(out=small[:, col + 2:col + 3], in_=small[:, col + 1:col + 2],
                                               func=AF.Exp, scale=-0.5),
                 reads=[("st", col + 1)], writes=[("st", col + 2)])
            return ("st", col + 2), small[:, col + 2:col + 3]

        def front_a(T):
            xs, hs = T % 3, T % 2
            B.dma("sp", ("xt", xs), xt[xs], xv[T * 128:(T + 1) * 128, :], writes=[("xt", xs)])
            rkey, rstd = rms_stats(xt[xs], ("xt", xs), 8 + 4 * (T % 2))
            B.op("dve", R.scalar_tensor_tensor(out=hb[hs], in0=xt[xs], scalar=rstd, in1=gbc,
                                               op0=ALU.mult, op1=ALU.mult),
                 reads=[("xt", xs), rkey, "gbc"], writes=[("hb", hs)])

        def front_b(T):
            g, tt = divmod(T, 4)
            hs, ts, gs = T % 2, T % 2, g % 2
            tpv = psb16(2 * ts, 2)
            for dc in range(16):
                B.op("pe", R.transpose(tpv[:, dc * 128:(dc + 1) * 128],
                                       hb[hs][:, dc * 128:(dc + 1) * 128], ident),
                     reads=[("hb", hs)], writes=[("ps", 2 * ts + dc // 8)])
            B.op("act", R.activation(out=hT[gs][:, 0:8, tt * 128:(tt + 1) * 128],
                                     in_=tpv[:, 0:1024].rearrange("p (c n) -> p c n", c=8), func=AF.Copy),
                 reads=[("ps", 2 * ts)], writes=[("hT", gs, tt, 0)])
            B.op("dve", R.tensor_copy(out=hT[gs][:, 8:16, tt * 128:(tt + 1) * 128],
                                      in_=tpv[:, 1024:2048].rearrange("p (c n) -> p c n", c=8)),
                 reads=[("ps", 2 * ts + 1)], writes=[("hT", gs, tt, 1)])
            if T + 2 < NT:
                front_a(T + 2)

        chain_no = [0]

        def backend_chains(g):
            gs, ks, vs = g % 2, g % 2, g % 2
            chains = []

            def kchain(fc):
                bk = 4 + chain_no[0] % 4
                chain_no[0] += 1
                for dc in range(16):
                    B.op("pe", R.matmul(ps[:, bk, :], lhsT=Wkv[:, dc, fc * 128:(fc + 1) * 128],
                                                         rhs=hT[gs][:, dc, :], start=(dc == 0), stop=(dc == 15)),
                         reads=hT_keys(gs) + [("W", "k", fc)], writes=[("ps", bk)])
                if fc % 2 == 0:
                    B.op("act", R.activation(out=kst[ks][:, fc, :], in_=ps[:, bk, :], func=AF.Copy),
                         reads=[("ps", bk)], writes=[("kst", ks, fc)])
                else:
                    B.op("dve", R.tensor_copy(out=kst[ks][:, fc, :], in_=ps[:, bk, :]),
                         reads=[("ps", bk)], writes=[("kst", ks, fc)])
                if fc == 7:
                    B.dma("pool", ("kst", ks), KTd[:, :, g * 512:(g + 1) * 512].rearrange("h p t -> p h t"), kst[ks],
                          reads=[("kst", ks, f) for f in range(8)])

            def vchain(tt, hf):
                bk = 4 + chain_no[0] % 4
                chain_no[0] += 1
                for dc in range(16):
                    B.op("pe", R.matmul(ps[:, bk, :], lhsT=hT[gs][:, dc, tt * 128:(tt + 1) * 128],
                                                         rhs=Wkv[:, dc, 1024 + hf * 512:1024 + (hf + 1) * 512],
                                                         start=(dc == 0), stop=(dc == 15)),
                         reads=[("hT", gs, tt, 0), ("hT", gs, tt, 1), ("W", "v", hf)], writes=[("ps", bk)])
                if hf == 0:
                    B.op("act", R.activation(out=vst[vs][:, tt, 0:512], in_=ps[:, bk, :], func=AF.Copy),
                         reads=[("ps", bk)], writes=[("vst", vs, tt, hf)])
                else:
                    B.op("dve", R.tensor_copy(out=vst[vs][:, tt, 512:1024], in_=ps[:, bk, :]),
                         reads=[("ps", bk)], writes=[("vst", vs, tt, hf)])
                if tt == 3 and hf == 1:
                    for t4 in range(4):
                        B.dma("pool", ("vst", vs),
                              Vd[:, :, 4 * g + t4, :].rearrange("h p d -> p h d"),
                              vst[vs][:, t4, :].rearrange("p (h d) -> p h d", h=8),
                              reads=[("vst", vs, t, f) for t in range(4) for f in range(2)])
                    if g % 4 == 3:
                        B.dma("pool", ("hTd", gs), hTd[g // 4], hT[gs], reads=hT_keys(gs))

            for fc in range(8):
                chains.append(lambda fc=fc: kchain(fc))
            for tt in range(4):
                for hf in range(2):
                    chains.append(lambda tt=tt, hf=hf: vchain(tt, hf))
            return chains

        front_a(0)
        front_a(1)
        for T in range(4):
            front_b(T)
        for g in range(NG):
            chains = backend_chains(g)
            for ci, ch in enumerate(chains):
                ch()
                if ci % 4 == 3 and g + 1 < NG:
                    front_b(4 * (g + 1) + ci // 4)
        B.barrier()

        mem.ptr = pmark
        Wqg = mem.alloc([16, 2048], BF16)
        hT = [mem.alloc([16, 512], BF16) for _ in range(2)]
        qst = [mem.alloc([2, 8, 512], BF16) for _ in range(2)]
        gst = [mem.alloc([512], F32) for _ in range(2)]
        th = [mem.alloc([512], F32) for _ in range(2)]
        for fc in range(8):
            load_wblock(fc, Wqg, fc * 128, (fc + 1) * 128, w_in, 3072 + fc * 128, ("W", "q", fc))
        for hh in range(8):
            load_wblock(8 + hh, Wqg, 1024 + hh * 128, 1024 + (hh + 1) * 128, w_in, 6144 + hh * 128, ("W", "g", hh))
        for s in range(2):
            B.op("dve", R.memset(qst[s].rearrange("p a b c -> p (a b c)"), 0.0),
                 writes=[("qst", s, m, f) for m in range(2) for f in range(8)])
        cno = 0
        tno = 0
        for i in range(NSLOT):
            hs = i % 2
            qs = i % 2
            B.dma("sp", ("hTl", hs), hT[hs], hTd[i], writes=[("hT", hs)])
            for fc in range(8):
                bk = cno % 4
                cno += 1
                for dc in range(16):
                    B.op("pe", R.matmul(ps[:, bk, :], lhsT=Wqg[:, dc, fc * 128:(fc + 1) * 128],
                                                                      rhs=hT[hs][:, dc, :], start=(dc == 0), stop=(dc == 15)),
                         reads=[("hT", hs), ("W", "q", fc)], writes=[("ps", bk)])
                B.op("act", R.activation(out=qst[qs][0:64, 0, fc, :], in_=ps[0:64, bk, :], func=AF.Copy),
                     reads=[("ps", bk)], writes=[("qst", qs, 0, fc)])
                B.op("dve", R.tensor_copy(out=qst[qs][64:128, 1, fc, :], in_=ps[64:128, bk, :]),
                     reads=[("ps", bk)], writes=[("qst", qs, 1, fc)])
            for m2 in range(2):
                B.dma("pool", ("qst", qs), QTd[:, m2, :, i * 512:(i + 1) * 512].rearrange("h p t -> p h t"),
                      qst[qs][:, m2, :, :], reads=[("qst", qs, m, f) for m in range(2) for f in range(8)])
            for hh in range(8):
                gsl = hh % 2
                bk = 4 + cno % 4
                cno += 1
                tsl = tno % 2
                tno += 1
                for dc in range(16):
                    B.op("pe", R.matmul(ps[:, bk, :], lhsT=Wqg[:, dc, 1024 + hh * 128:1024 + (hh + 1) * 128],
                                        rhs=hT[hs][:, dc, :], start=(dc == 0), stop=(dc == 15)),
                         reads=[("hT", hs), ("W", "g", hh)], writes=[("ps", bk)])
                B.op("act", R.activation(out=th[tsl], in_=ps[:, bk, :], func=AF.Tanh, scale=0.5),
                     reads=[("ps", bk)], writes=[("th", tsl)])
                B.op("dve", R.scalar_tensor_tensor(out=gst[gsl], in0=th[tsl], scalar=1.0, in1=ps[:, bk, :],
                                                   op0=ALU.add, op1=ALU.mult),
                     reads=[("th", tsl), ("ps", bk)], writes=[("gst", gsl)])
                B.dma("pool", ("gst", gsl), GTd[hh, :, i * 512:(i + 1) * 512], gst[gsl], reads=[("gst", gsl)])
        B.barrier()

        mem.ptr = pmark
        Wsg = mem.alloc([16, 3072], BF16)
        hT = [mem.alloc([16, 512], BF16) for _ in range(2)]
        wnat = mem.alloc([8, 128], F32)
        identf = mem.alloc([128], F32)
        smk = mem.alloc([128], F32)
        bnat = mem.alloc([128], F32)
        wT = mem.alloc([8, 128], BF16)
        bT = mem.alloc([8], F32)
        lngb = mem.alloc([1024], F32)
        lnbb = mem.alloc([1024], F32)
        gu = mem.alloc([1024], F32)
        gv = mem.alloc([1024], F32)
        t1 = mem.alloc([1024], F32)
        vn = mem.alloc([1024], BF16)
        tha = mem.alloc([1024], F32)
        sa2 = mem.alloc([1024], F32)
        aa = mem.alloc([1024], F32)
        aout = mem.alloc([1024], BF16)
        bst = mem.alloc([2, 6], F32)
        yst = [mem.alloc([8, 512], BF16) for _ in range(2)]
        for k6, cb in enumerate([2, 3, 0, 1, 4, 5]):
            load_wblock(k6, Wsg, cb * 512, (cb + 1) * 512, w_in, cb * 512, ("W", "s", cb))
        B.dma("sp", "pl", wnat, sgu_w.rearrange("g t s -> t g s"), writes=["wnat"])
        B.dma("sp", "pl", identf, c_identf, writes=["identf"])
        B.dma("sp", "pl", smk, c_sgumask, writes=["smk"])
        B.dma("sp", "pl", bnat[0:8, :], sgu_b, writes=["bnat"])
        B.dma("sp", "pl", lngb, bcast(ln_g, 1024), writes=["lngb"])
        B.dma("sp", "pl", lnbb, bcast(ln_b, 1024), writes=["lnbb"])
        B.barrier()
        for g8 in range(8):
            bk = g8 % 4
            B.op("pe", R.transpose(ps[:, bk, 0:128], wnat[:, g8, :], identf),
                 reads=["wnat", "identf"], writes=[("ps", bk)])
            B.op("dve", R.tensor_tensor(out=wT[:, g8, :], in0=ps[:, bk, 0:128], in1=smk, op=ALU.mult),
                 reads=[("ps", bk), "smk"], writes=[("wT", g8)])
        B.op("pe", R.transpose(ps[:, 4, 0:8], bnat[0:8, :], identf[0:8, 0:8]),
             reads=["bnat", "identf"], writes=[("ps", 4)])
        B.op("dve", R.tensor_copy(out=bT, in_=ps[:, 4, 0:8]), reads=[("ps", 4)], writes=["bT"])
        B.barrier()

        NTILE = 4 * NSLOT

        def b2_load(i):
            B.dma("sp", ("hTl", i % 2), hT[i % 2], hTd[i], writes=[("hT", i % 2)])

        def b2_chains(gt, cbs):
            i, tt = divmod(gt, 4)
            hs = i % 2
            for cb in cbs:
                for dc in range(16):
                    B.op("pe", R.matmul(ps[:, cb, :], lhsT=hT[hs][:, dc, tt * 128:(tt + 1) * 128],
                                        rhs=Wsg[:, dc, cb * 512:(cb + 1) * 512], start=(dc == 0), stop=(dc == 15)),
                         reads=[("hT", hs), ("W", "s", cb)], writes=[("ps", cb)])

        def b2_post_v():
            B.op("act", R.activation(out=gv.rearrange("p (a b) -> p a b", a=2), in_=ps[:, 2:4, :], func=AF.Gelu_apprx_tanh),
                 reads=[("ps", 2), ("ps", 3)], writes=["gv"])
            B.op("dve", R.bn_stats(out=bst[:, 0, :], in_=gv[:, 0:512]), reads=["gv"], writes=["bst0"])
            B.op("dve", R.bn_stats(out=bst[:, 1, :], in_=gv[:, 512:1024]), reads=["gv"], writes=["bst1"])
            B.op("dve", R.bn_aggr(out=small[:, 16:18], in_=bst.rearrange("p a b -> p (a b)")),
                 reads=["bst0", "bst1"], writes=["mv"])
            B.op("act", R.activation(out=small[:, 18:19], in_=small[:, 17:18], func=AF.Ln, bias=epsb, scale=1.0),
                 reads=["mv"], writes=["lnv"])
            B.op("act", R.activation(out=small[:, 19:20], in_=small[:, 18:19], func=AF.Exp, scale=-0.5),
                 reads=["lnv"], writes=["lrs"])
            B.op("dve", R.tensor_scalar(out=t1, in0=gv, scalar1=small[:, 16:17], scalar2=small[:, 19:20],
                                        op0=ALU.subtract, op1=ALU.mult),
                 reads=["gv", "mv", "lrs"], writes=["t1"])
            B.op("dve", R.tensor_tensor(out=t1, in0=t1, in1=lngb, op=ALU.mult), reads=["t1"], writes=["t1"])
            B.op("dve", R.tensor_tensor(out=vn, in0=t1, in1=lnbb, op=ALU.add), reads=["t1"], writes=["vn"])

        def b2_post_uga():
            B.op("act", R.activation(out=gu.rearrange("p (a b) -> p a b", a=2), in_=ps[:, 0:2, :], func=AF.Gelu_apprx_tanh),
                 reads=[("ps", 0), ("ps", 1)], writes=["gu"])
            B.op("act", R.activation(out=tha.rearrange("p (a b) -> p a b", a=2), in_=ps[:, 4:6, :], func=AF.Tanh, scale=0.5),
                 reads=[("ps", 4), ("ps", 5)], writes=["tha"])

        def b2_mix():
            for g8 in range(8):
                B.op("pe", R.matmul(ps[:, 6 + g8 // 4, (g8 % 4) * 128:(g8 % 4 + 1) * 128],
                                    lhsT=wT[:, g8, :], rhs=vn[:, g8 * 128:(g8 + 1) * 128], start=True, stop=True),
                     reads=["vn"], writes=[("ps", 6 + g8 // 4)])

        def b2_finish():
            for hf in range(2):
                B.op("dve", R.scalar_tensor_tensor(
                    out=sa2[:, hf * 512:(hf + 1) * 512], in0=tha[:, hf * 512:(hf + 1) * 512], scalar=1.0,
                    in1=ps[:, 4 + hf, :], op0=ALU.add, op1=ALU.mult),
                    reads=["tha", ("ps", 4 + hf)], writes=[("sa2", hf)])
            for g8 in range(8):
                B.op("dve", R.scalar_tensor_tensor(
                    out=aa[:, g8 * 128:(g8 + 1) * 128], in0=ps[:, 6 + g8 // 4, (g8 % 4) * 128:(g8 % 4 + 1) * 128],
                    scalar=bT[:, g8:g8 + 1], in1=gu[:, g8 * 128:(g8 + 1) * 128], op0=ALU.add, op1=ALU.mult),
                    reads=[("ps", 6 + g8 // 4), "gu"], writes=[("aa", g8)])
            B.op("dve", R.scalar_tensor_tensor(out=aout, in0=aa, scalar=0.5, in1=sa2, op0=ALU.mult, op1=ALU.mult),
                 reads=[("aa", g8) for g8 in range(8)] + [("sa2", 0), ("sa2", 1)], writes=["aout"])

        def b2_transposes(gt):
            i, tt = divmod(gt, 4)
            ys = i % 2
            tpb = psb16(0, 1)
            for ec in range(8):
                B.op("pe", R.transpose(tpb[:, ec * 128:(ec + 1) * 128], aout[:, ec * 128:(ec + 1) * 128], ident),
                     reads=["aout"], writes=[("ps", 0)])
            B.op("act", R.activation(out=yst[ys][:, :, tt * 128:(tt + 1) * 128],
                                     in_=tpb.rearrange("p (c n) -> p c n", c=8), func=AF.Copy),
                 reads=[("ps", 0)], writes=[("yst", ys, tt)])
            if tt == 3:
                B.dma("pool", ("yst", ys), YTd[0:8, :, i * 512:(i + 1) * 512].rearrange("c p t -> p c t"), yst[ys],
                      reads=[("yst", ys, t) for t in range(4)])

        b2_load(0)
        if NSLOT > 1:
            b2_load(1)
        b2_chains(0, [2, 3])
        for gt in range(NTILE):
            if gt % 4 == 0 and gt // 4 >= 1 and gt // 4 + 1 < NSLOT:
                b2_load(gt // 4 + 1)
            b2_post_v()
            b2_chains(gt, [0, 1, 4, 5])
            b2_post_uga()
            b2_mix()
            b2_finish()
            if gt + 1 < NTILE:
                b2_chains(gt + 1, [2, 3])
            b2_transposes(gt)
        B.barrier()

        mem.ptr = pmark
        kT = [mem.alloc([NT * 128], BF16) for _ in range(2)]
        vA = [mem.alloc([NT, 128], BF16) for _ in range(2)]
        bm2 = [mem.alloc([5, 512], F32) for _ in range(2)]
        qT = [mem.alloc([2, 512], BF16) for _ in range(2)]
        gtT = [mem.alloc([512], F32) for _ in range(2)]
        pt = [mem.alloc([2, 512], BF16) for _ in range(3)]
        tmpf = mem.alloc([2, 512], F32)
        rsd = [mem.alloc([2, 512], F32) for _ in range(2)]
        sh = mem.alloc([512], F32)
        zb = mem.alloc([2, 128], BF16)
        sel = mem.alloc([2, 128], F32)
        onesf = mem.alloc([128], F32)
        kvf = mem.alloc([NT], F32)
        nlam = mem.alloc([1], F32)
        sbs = mem.alloc([2, 512], F32)
        ea = mem.alloc([512], F32)
        eb = mem.alloc([512], F32)
        eo = mem.alloc([512], F32)
        ep = mem.alloc([512], F32)
        ystc = [mem.alloc([512], BF16) for _ in range(2)]

        B.op("dve", R.memset(onesf, 1.0), writes=["onesf"])
        B.op("dve", R.memset(zb.rearrange("p a b -> p (a b)"), 0.0), writes=["zb"])
        B.op("dve", R.memset(zb[:, 0, 0:64], 1.0), reads=["zb"], writes=["zb"])
        B.op("dve", R.memset(zb[:, 1, 64:128], 1.0), reads=["zb"], writes=["zb"])
        B.op("dve", R.memset(sel.rearrange("p a b -> p (a b)"), 0.0), writes=["sel"])
        B.op("dve", R.memset(sel[0:1, 0, :], 1.0), reads=["sel"], writes=["sel"])
        B.op("dve", R.memset(sel[64:65, 1, :], 1.0), reads=["sel"], writes=["sel"])
        B.op("dve", R.tensor_copy(out=kvf, in_=kval), reads=["kval"], writes=["kvf"])
        B.op("dve", R.tensor_scalar(out=nlam, in0=lamb, scalar1=-1.0, scalar2=None, op0=ALU.mult),
             reads=["lamb"], writes=["nlam"])

        def flat(t):
            return t.rearrange("p a b -> p (a b)")

        def load_head(h):
            hs = h % 2
            B.dma("pool", ("kT", hs), kT[hs], KTd[h], writes=[("kT", hs)])
            B.dma("pool", ("vA", hs), vA[hs], Vd[h], writes=[("vA", hs)])
            B.dma("pool", ("bm", hs), flat(bm2[hs]), BMd[h], writes=[("bm", hs)])

        pcount = [0]
        slotc = [0]
        pending = []

        def epilogue_parts(h, i, sl, gs, ys):
            def copies():
                B.op("dve", R.tensor_copy(out=ea, in_=ps[:, 4, :]), reads=[("ps", 4)], writes=["ea"])
                B.op("dve", R.tensor_copy(out=eb, in_=ps[:, 5, :]), reads=[("ps", 5)], writes=["eb"])
                B.op("dve", R.tensor_copy(out=sh, in_=ps[:, 6, :]), reads=[("ps", 6)], writes=["sh"])

            def sums_mm(m):
                def f():
                    B.op("pe", R.matmul(ps[:, 7, :], lhsT=onesf, rhs=rsd[sl][:, m, :], start=True, stop=False),
                         reads=[("rsd", sl), "onesf"], writes=[("ps", 7)])
                    B.op("pe", R.matmul(ps[:, 7, :], lhsT=sel[:, m, :], rhs=sh, start=False, stop=True),
                         reads=["sh", "sel"], writes=[("ps", 7)])
                return f

            def sums_scale(m):
                def f():
                    B.op("dve", R.tensor_scalar(out=sbs[:, m, :], in0=ps[:, 7, :], scalar1=2.0 ** -14, scalar2=None,
                                                op0=ALU.mult),
                         reads=[("ps", 7)], writes=[("sbs", m)])
                return f

            def combine():
                B.op("dve", R.tensor_tensor(out=ea, in0=ea, in1=sbs[:, 1, :], op=ALU.mult), reads=["ea", ("sbs", 1)], writes=["ea"])
                B.op("dve", R.tensor_tensor(out=eb, in0=eb, in1=sbs[:, 0, :], op=ALU.mult), reads=["eb", ("sbs", 0)], writes=["eb"])
                B.op("dve", R.scalar_tensor_tensor(out=eo, in0=eb, scalar=nlam, in1=ea, op0=ALU.mult, op1=ALU.add),
                     reads=["ea", "eb", "nlam"], writes=["eo"])
                B.op("dve", R.tensor_tensor(out=eb, in0=eo, in1=eo, op=ALU.mult), reads=["eo"], writes=["eb"])
                B.op("dve", R.tensor_tensor(out=ep, in0=sbs[:, 0, :], in1=sbs[:, 1, :], op=ALU.mult),
                     reads=[("sbs", 0), ("sbs", 1)], writes=["ep"])
                B.op("dve", R.scalar_tensor_tensor(out=ep, in0=ep, scalar=EPS, in1=ep, op0=ALU.mult, op1=ALU.mult),
                     reads=["ep"], writes=["ep"])

            def ms_mm():
                B.op("pe", R.matmul(ps[:, 7, :], lhsT=onesf, rhs=eb, start=True, stop=True),
                     reads=["eb", "onesf"], writes=[("ps", 7)])

            def ms_add():
                B.op("dve", R.scalar_tensor_tensor(out=ea, in0=ps[:, 7, :], scalar=1.0 / 128, in1=ep, op0=ALU.mult, op1=ALU.add),
                     reads=[("ps", 7), "ep"], writes=["ea"])

            def rstd():
                B.op("act", R.activation(out=ea, in_=ea, func=AF.Ln), reads=["ea"], writes=["ea"])
                B.op("act", R.activation(out=ea, in_=ea, func=AF.Exp, scale=-0.5), reads=["ea"], writes=["ea"])

            def final():
                B.op("dve", R.scalar_tensor_tensor(out=eo, in0=eo, scalar=sgl, in1=ea, op0=ALU.mult, op1=ALU.mult),
                     reads=["eo", "ea", "sgl"], writes=["eo"])
                B.op("dve", R.tensor_tensor(out=ystc[ys], in0=eo, in1=gtT[gs], op=ALU.mult),
                     reads=["eo", ("gtT", gs)], writes=[("ystc", ys)])
                B.dma("pool", ("ystc", ys), YTd[8 + h, :, i * 512:(i + 1) * 512], ystc[ys], reads=[("ystc", ys)])
            return [(-1, copies), (0, sums_mm(0)), (1, sums_scale(0)), (2, sums_mm(1)), (3, sums_scale(1)), (3, combine),
                    (6, ms_mm), (7, ms_add), (9, rstd), (11, final)]

        zbv = mem.alloc([12, 2, 128], BF16)
        for v in range(12):
            B.op("dve", R.tensor_scalar(out=zbv[:, v, :, :], in0=zb, scalar1=kvf[:, v:v + 1], scalar2=None, op0=ALU.mult),
                 reads=["zb", "kvf"], writes=[("zbv", v)])
        tmpf2 = [tmpf, mem.alloc([2, 512], F32)]

        def run_pending(v):
            while pending and pending[0][0] <= v:
                pending.pop(0)[1]()

        load_head(0)
        for h in range(NH):
            hs = h % 2
            bm = bm2[hs]
            for i in range(NSLOT):
                sc = slotc[0]
                slotc[0] += 1
                qs = gs = sl = ys = sc % 2
                B.dma("sp", ("qT", qs), qT[qs], QTd[h, :, :, i * 512:(i + 1) * 512].rearrange("m p t -> p m t"),
                      writes=[("qT", qs)])
                B.dma("sp", ("gtT", gs), gtT[gs], GTd[h, :, i * 512:(i + 1) * 512], writes=[("gtT", gs)])
                if i == 0 and h + 1 < NH:
                    load_head(h + 1)
                n = 16 * (i + 1)

                def emit_S(v, pu):
                    b = pu % 2
                    for m in range(2):
                        B.op("pe", R.matmul(ps[:, 2 * b + m, :], lhsT=kT[hs][:, v * 128:(v + 1) * 128], rhs=qT[qs][:, m, :],
                                            start=True, stop=True),
                             reads=[("kT", hs), ("qT", qs)], writes=[("ps", 2 * b + m)])

                emit_S(0, pcount[0])
                emit_S(1, pcount[0] + 1)
                psum_started = False
                for v in range(n):
                    pu = pcount[0]
                    pcount[0] += 1
                    b = pu % 2
                    s3 = pu % 3
                    r = v - (n - 4)
                    if r < -1:
                        B.op("act", R.activation(out=pt[s3], in_=ps[:, 2 * b:2 * b + 2, :], func=AF.Exp, scale=0.125,
                                                 bias=chb[:, h:h + 1]),
                             reads=[("ps", 2 * b), ("ps", 2 * b + 1)], writes=[("pt", s3)])
                    else:
                        tf = tmpf2[(r + 1) % 2]
                        tk = (r + 1) % 2
                        for m in range(2):
                            B.op("dve", R.scalar_tensor_tensor(out=tf[:, m, :], in0=ps[:, 2 * b + m, :], scalar=0.125,
                                                               in1=bm[:, r + 1, :], op0=ALU.mult, op1=ALU.add),
                                 reads=[("ps", 2 * b + m), ("bm", hs)], writes=[("tmpf", tk, m)])
                        B.op("act", R.activation(out=pt[s3], in_=tf, func=AF.Exp),
                             reads=[("tmpf", tk, 0), ("tmpf", tk, 1)], writes=[("pt", s3)])
                    if v + 2 < n:
                        emit_S(v + 2, pu + 2)
                    for m in range(2):
                        B.op("pe", R.matmul(ps[:, 4 + m, :], lhsT=vA[hs][:, v, :], rhs=pt[s3][:, m, :],
                                            start=(v == 0), stop=(v == n - 1)),
                             reads=[("pt", s3), ("vA", hs)], writes=[("ps", 4 + m)])
                    if (v % 3 == 1) or (v >= n - 5):
                        for m in range(2):
                            zt = zbv[:, v, m, :] if v < 12 else zb[:, m, :]
                            B.op("pe", R.matmul(ps[:, 6, :], lhsT=zt, rhs=pt[s3][:, m, :],
                                                start=(not psum_started and m == 0), stop=(v == n - 1 and m == 1)),
                                 reads=[("pt", s3), "zb"] + ([("zbv", v)] if v < 12 else []), writes=[("ps", 6)])
                        psum_started = True
                    elif v == 0:
                        B.op("dve", R.tensor_scalar(out=flat(rsd[sl]), in0=flat(pt[s3]), scalar1=kvf[:, 0:1], scalar2=None,
                                                    op0=ALU.mult),
                             reads=[("pt", s3), "kvf"], writes=[("rsd", sl)])
                    elif v < 12:
                        B.op("dve", R.scalar_tensor_tensor(out=flat(rsd[sl]), in0=flat(pt[s3]), scalar=kvf[:, v:v + 1],
                                                           in1=flat(rsd[sl]), op0=ALU.mult, op1=ALU.add),
                             reads=[("pt", s3), "kvf", ("rsd", sl)], writes=[("rsd", sl)])
                    else:
                        B.op("dve", R.tensor_tensor(out=flat(rsd[sl]), in0=flat(rsd[sl]), in1=flat(pt[s3]), op=ALU.add),
                             reads=[("pt", s3), ("rsd", sl)], writes=[("rsd", sl)])
                    run_pending(v)
                run_pending(10 ** 9)
                for trig, fn in epilogue_parts(h, i, sl, gs, ys):
                    if trig < 0:
                        fn()
                    else:
                        pending.append((trig, fn))
        run_pending(10 ** 9)
        B.barrier()

        mem.ptr = pmark
        Wo = mem.alloc([16, 2048], BF16)
        fgb = mem.alloc([D], F32)
        yT = [mem.alloc([16, 512], BF16) for _ in range(2)]
        xo = [mem.alloc([D], F32) for _ in range(2)]
        rr = [mem.alloc([D], F32) for _ in range(2)]
        ot = [mem.alloc([D], F32) for _ in range(2)]
        sqj = mem.alloc([D], BF16)
        for nb in range(4):
            load_wblock(nb, Wo, nb * 512, (nb + 1) * 512, w_out, nb * 512, ("W", "o", nb))
        B.dma("sp", "pl", fgb, bcast(final_g, D), writes=["fgb"])
        tcount = 0
        for i in range(NSLOT):
            ysl = i % 2
            B.dma("sp", ("yT", ysl), yT[ysl], YTd[:, :, i * 512:(i + 1) * 512].rearrange("c p t -> p c t"),
                  writes=[("yT", ysl)])
            for tt in range(4):
                k = tcount % 2
                tcount += 1
                Tv = 16 * i + 12 + tt
                B.dma("sp", ("xo", k), xo[k], xv[Tv * 128:(Tv + 1) * 128, :], writes=[("xo", k)])
                for nb in range(4):
                    bk = 4 * k + nb
                    for ec in range(16):
                        B.op("pe", R.matmul(
                            ps[:, bk, :], lhsT=yT[ysl][:, ec, tt * 128:(tt + 1) * 128],
                            rhs=Wo[:, ec, nb * 512:(nb + 1) * 512], start=(ec == 0), stop=(ec == 15)),
                            reads=[("yT", ysl), ("W", "o", nb)], writes=[("ps", bk)])
                    B.op("dve", R.tensor_tensor(out=rr[k][:, nb * 512:(nb + 1) * 512], in0=ps[:, bk, :],
                                                                      in1=xo[k][:, nb * 512:(nb + 1) * 512], op=ALU.add),
                         reads=[("ps", bk), ("xo", k)], writes=[("rr", k, nb)])
                col = 8 + 4 * k
                B.op("act", R.activation(out=sqj, in_=rr[k], func=AF.Square, accum_out=small[:, col:col + 1]),
                     reads=[("rr", k, nb) for nb in range(4)], writes=["sqj", ("st", col)])
                B.op("act", R.activation(out=small[:, col + 1:col + 2], in_=small[:, col:col + 1], func=AF.Ln,
                                                   scale=1.0 / D, bias=epsb),
                     reads=[("st", col)], writes=[("st", col + 1)])
                B.op("act", R.activation(out=small[:, col + 2:col + 3], in_=small[:, col + 1:col + 2], func=AF.Exp,
                                                   scale=-0.5),
                     reads=[("st", col + 1)], writes=[("st", col + 2)])
                B.op("dve", R.scalar_tensor_tensor(out=ot[k], in0=rr[k], scalar=small[:, col + 2:col + 3], in1=fgb,
                                                             op0=ALU.mult, op1=ALU.mult),
                     reads=[("rr", k, nb) for nb in range(4)] + [("st", col + 2), "fgb"], writes=[("ot", k)])
                B.dma("pool", ("ot", k), out_own[(4 * i + tt) * 128:(4 * i + tt + 1) * 128, :], ot[k], reads=[("ot", k)])
        B.barrier()

        def replay(e, stream):
            for ent in stream:
                if ent[0] == "wait":
                    e.wait_ge(ent[1], ent[2])
                else:
                    _, name, args, kwargs, sem, inc = ent
                    getattr(e, name)(*args, **kwargs).then_inc(sem, inc)

        with nc.Block() as block:
            @block.tensor
            def _(e):
                replay(e, B.streams["pe"])

            @block.scalar
            def _(e):
                replay(e, B.streams["act"])

            @block.vector
            def _(e):
                replay(e, B.streams["dve"])

            @block.gpsimd
            def _(e):
                replay(e, B.streams["pool"])

            @block.sync
            def _(e):
                replay(e, B.streams["sp"])
        print("instr counts", {k: len(v) for k, v in B.streams.items()}, "sems", len(B.dsem))
    return nc


def _t5_bucket_np(rel):
    try:
        import jax
        import jax.numpy as jnp
        with jax.default_device(jax.devices("cpu")[0]):
            rel_j = jnp.asarray(rel, dtype=jnp.int32)
            nb = 16
            max_exact = 8
            side = jnp.where(rel_j > 0, nb, 0)
            n = jnp.abs(rel_j)
            nf = jnp.maximum(n, 1).astype(jnp.float32)
            large = max_exact + (jnp.log(nf / max_exact) / math.log(128 / max_exact) * (nb - max_exact)).astype(jnp.int32)
            large = jnp.minimum(large, nb - 1)
            return np.asarray(side + jnp.where(n < max_exact, n, large)).astype(np.int64)
    except Exception:
        rel = np.asarray(rel, dtype=np.int64)
        side = np.where(rel > 0, 16, 0)
        n = np.abs(rel)
        nf = np.maximum(n, 1).astype(np.float32)
        large = 8 + (np.log(nf / np.float32(8)) / np.float32(math.log(16.0)) * np.float32(8)).astype(np.int32)
        large = np.minimum(large, 15)
        return side + np.where(n < 8, n, large)


_PROG_CACHE = {}


def kernel(x, norm_g, w_in, sgu_ln_g, sgu_ln_b, sgu_w, sgu_b, lambda_q1, lambda_k1,
           lambda_q2, lambda_k2, subln_g, rel_bias, w_out, final_g):
    x = np.asarray(x, dtype=np.float32)
    Bn, S, _ = x.shape
    assert Bn == 2 and S % 2048 == 0
    NSLOT = S // 2048
    NT = 16 * NSLOT
    bf = ml_dtypes.bfloat16

    u = np.arange(NR)
    bucket = _t5_bucket_np(511 - u)
    onehot = np.zeros((32, NR), np.float32)
    onehot[bucket, u] = 1.0
    kk = np.arange(128)[:, None]
    qq = np.arange(512)[None, :]
    masks = np.zeros((128, 4, 512), np.float32)
    for r in range(4):
        allowed = ((128 * r + kk) // 64) <= (qq // 64)
        masks[:, r, :] = np.where(allowed, 0.0, NEG)
    ss_ = np.arange(128)[:, None]
    tt_ = np.arange(128)[None, :]
    sgumask = ((ss_ // 64) <= (tt_ // 64)).astype(np.float32)
    ident = np.eye(128, dtype=np.float32)
    rev = np.ascontiguousarray(ident[::-1])
    lam_in = np.stack([np.asarray(a, np.float32).reshape(64) for a in (lambda_q1, lambda_k1, lambda_q2, lambda_k2)])

    common = {
        "w_in": np.ascontiguousarray(np.asarray(w_in, np.float32).reshape(D, DIN)),
        "w_out": np.ascontiguousarray(np.asarray(w_out, np.float32).reshape(D, D)),
        "norm_g": np.asarray(norm_g, np.float32).reshape(D),
        "final_g": np.asarray(final_g, np.float32).reshape(D),
        "sgu_ln_g": np.asarray(sgu_ln_g, np.float32).reshape(1024),
        "sgu_ln_b": np.asarray(sgu_ln_b, np.float32).reshape(1024),
        "sgu_w": np.ascontiguousarray(np.asarray(sgu_w, np.float32).reshape(8, 128, 128)),
        "sgu_b": np.ascontiguousarray(np.asarray(sgu_b, np.float32).reshape(8, 128)),
        "lam_in": lam_in,
        "subln_g": np.asarray(subln_g, np.float32).reshape(128),
        "rel_bias": np.ascontiguousarray(np.asarray(rel_bias, np.float32).reshape(32, 8)),
        "c_ident": ident.astype(bf),
        "c_identf": ident,
        "c_rev": rev,
        "c_onehot": onehot,
        "c_masks": np.ascontiguousarray(masks.reshape(128, 4 * 512)),
        "c_sgumask": sgumask,
    }
    in_maps = []
    for c in range(8):
        b, j = divmod(c, 4)
        npad = 12 - 4 * j
        xvirt = np.zeros((NT * 128, D), np.float32)
        xvirt[npad * 128:] = x[b, :(NT - npad) * 128]
        kvalid = np.zeros((128, NT), np.float32)
        kvalid[:, npad:] = 1.0
        m = dict(common)
        m["xv"] = xvirt
        m["c_kvalid"] = kvalid.astype(bf)
        in_maps.append(m)

    if NSLOT not in _PROG_CACHE:
        _PROG_CACHE[NSLOT] = build_program(NSLOT)
    nc = _PROG_CACHE[NSLOT]
    res = run_bass_kernel_spmd(nc, in_maps, core_ids=list(range(8)))
    out = np.empty((Bn, S, D), np.float32)
    for c in range(8):
        b, j = divmod(c, 4)
        o = np.asarray(res.results[c]["out_own"], np.float32)
        for i in range(NSLOT):
            q0 = (4 * i + j) * 512
            out[b, q0:q0 + 512] = o[i * 512:(i + 1) * 512]
    return out
```
